# Optimizing a Trainium2 kernel written in Bass

```python
import math
import jax, jax.numpy as jnp
from jax import lax
import numpy as np

D_MODEL = 1024
BATCH = 8
SEQ = 4096
DEPTH = 1

N_HEADS = 8
HEAD_DIM = 64
D_ATTN = N_HEADS * HEAD_DIM
BLOCK = 256
TOPK = 3
Q_CHUNK = 32
NUM_BUCKETS = 32
MAX_DISTANCE = 128
D_RNN = D_MODEL
RNN_BLOCKS = 16
RNN_BLOCK_DIM = D_RNN // RNN_BLOCKS
CONV_WIDTH = 4
LRU_C = 8.0
D_FF = 4 * D_MODEL
EPS = 1e-6
IN_SPLITS = (D_ATTN, D_ATTN, D_ATTN, D_RNN, D_RNN, D_MODEL, D_MODEL)
D_IN = sum(IN_SPLITS)

kernel_name = "hybrid_moba_rglru_gated_block"


def rmsnorm(x, w):
    xf = x.astype(jnp.float32)
    y = xf * lax.rsqrt(jnp.mean(jnp.square(xf), axis=-1, keepdims=True) + EPS)
    return (y * w.astype(jnp.float32)).astype(x.dtype)


def t5_bucket(rel):
    max_exact = NUM_BUCKETS // 2
    n = jnp.maximum(rel, 0)
    nf = jnp.maximum(n, 1).astype(jnp.float32)
    large = max_exact + (jnp.log(nf / max_exact) / math.log(MAX_DISTANCE / max_exact)
                         * (NUM_BUCKETS - max_exact)).astype(jnp.int32)
    large = jnp.minimum(large, NUM_BUCKETS - 1)
    return jnp.where(n < max_exact, n, large)


def moba_attention(q, k, v, rel_bias):
    B, H, S, Dh = q.shape
    nb = -(-S // BLOCK)
    s_pad = nb * BLOCK
    pad = ((0, 0), (0, 0), (0, s_pad - S), (0, 0))
    q = jnp.pad(q, pad)
    k = jnp.pad(k, pad)
    v = jnp.pad(v, pad)
    kb = k.reshape(B, H, nb, BLOCK, Dh)
    vb = v.reshape(B, H, nb, BLOCK, Dh)
    k_mean = jnp.mean(kb.astype(jnp.float32), axis=3)
    topk = min(TOPK, nb)
    scale = HEAD_DIM ** -0.5
    bias_tab = rel_bias.T
    b_idx = jnp.arange(B)[:, None, None, None]
    h_idx = jnp.arange(H)[None, :, None, None]
    blk_ids = jnp.arange(nb)
    offs = jnp.arange(BLOCK)
    slot_ids = jnp.arange(topk)

    def chunk(c):
        start = c * Q_CHUNK
        qc = lax.dynamic_slice_in_dim(q, start, Q_CHUNK, axis=2)
        q_pos = start + jnp.arange(Q_CHUNK)
        own = start // BLOCK
        gate = jnp.einsum('bhqd,bhnd->bhqn', qc.astype(jnp.float32), k_mean)
        gate = jnp.where(blk_ids < own, gate, -jnp.inf)
        _, sel = lax.top_k(gate, topk)
        sel_valid = slot_ids < own
        k_sel = kb[b_idx, h_idx, sel]
        v_sel = vb[b_idx, h_idx, sel]
        s_sel = jnp.einsum('bhqd,bhqjkd->bhqjk', qc, k_sel).astype(jnp.float32) * scale
        k_pos_sel = sel[..., None] * BLOCK + offs
        bucket_sel = t5_bucket(q_pos[:, None, None] - k_pos_sel)
        s_sel = s_sel + bias_tab[h_idx[..., None], bucket_sel]
        s_sel = jnp.where(sel_valid[:, None], s_sel, -jnp.inf)
        s_sel = s_sel.reshape(B, H, Q_CHUNK, topk * BLOCK)
        k_own = lax.dynamic_slice_in_dim(k, own * BLOCK, BLOCK, axis=2)
        v_own = lax.dynamic_slice_in_dim(v, own * BLOCK, BLOCK, axis=2)
        s_own = jnp.einsum('bhqd,bhkd->bhqk', qc, k_own).astype(jnp.float32) * scale
        rel_own = q_pos[:, None] - (own * BLOCK + offs)[None, :]
        s_own = s_own + bias_tab[:, t5_bucket(rel_own)]
        s_own = jnp.where(rel_own >= 0, s_own, -jnp.inf)
        p = jax.nn.softmax(jnp.concatenate([s_sel, s_own], axis=-1), axis=-1)
        p_sel = p[..., :topk * BLOCK].reshape(B, H, Q_CHUNK, topk, BLOCK).astype(v.dtype)
        p_own = p[..., topk * BLOCK:].astype(v.dtype)
        return (jnp.einsum('bhqjk,bhqjkd->bhqd', p_sel, v_sel)
                + jnp.einsum('bhqk,bhkd->bhqd', p_own, v_own))

    out = lax.map(chunk, jnp.arange(s_pad // Q_CHUNK))
    out = out.transpose(1, 2, 0, 3, 4).reshape(B, H, s_pad, Dh)
    return out[:, :, :S]


def causal_depthwise_conv(x, w, b):
    S = x.shape[1]
    xp = jnp.pad(x, ((0, 0), (CONV_WIDTH - 1, 0), (0, 0)))
    y = b + xp[:, 0:S] * w[0]
    for j in range(1, CONV_WIDTH):
        y = y + xp[:, j:j + S] * w[j]
    return y


def block_diag_linear(x, w, b):
    B, S, _ = x.shape
    xb = x.reshape(B, S, RNN_BLOCKS, RNN_BLOCK_DIM)
    y = jnp.einsum('bsnd,nde->bsne', xb, w).reshape(B, S, D_RNN)
    return y + b


def rg_lru(x, w_a, b_a, w_i, b_i, lam):
    r = jax.nn.sigmoid(block_diag_linear(x, w_a, b_a).astype(jnp.float32))
    i = jax.nn.sigmoid(block_diag_linear(x, w_i, b_i).astype(jnp.float32))
    log_a = -LRU_C * r * jax.nn.softplus(-lam.astype(jnp.float32))
    a = jnp.exp(log_a)
    u = jnp.sqrt(-jnp.expm1(2.0 * log_a)) * (i * x.astype(jnp.float32))

    def combine(left, right):
        a1, b1 = left
        a2, b2 = right
        return a1 * a2, a2 * b1 + b2

    _, h = lax.associative_scan(combine, (a, u), axis=1)
    return h.astype(x.dtype)


def setup_inputs(seed: int = 0) -> dict:
    key = jax.random.key(seed)
    ks = jax.random.split(key, 20)
    L, D = DEPTH, D_MODEL
    nrm = lambda k, shape, s: jax.random.normal(k, shape, jnp.float32) * s
    u = jax.random.uniform(ks[13], (L, D_RNN), jnp.float32, minval=0.9, maxval=0.999)
    sa = u ** (1.0 / LRU_C)
    lam = jnp.log(sa) - jnp.log1p(-sa)
    return {
        "x": nrm(ks[0], (BATCH, SEQ, D), 1.0),
        "norm1_w": 1.0 + nrm(ks[1], (L, D), 0.02),
        "w_in": nrm(ks[2], (L, D, D_IN), D ** -0.5),
        "b_gate": nrm(ks[3], (L, 2, D), 0.1),
        "q_norm_w": 1.0 + nrm(ks[4], (L, HEAD_DIM), 0.02),
        "k_norm_w": 1.0 + nrm(ks[5], (L, HEAD_DIM), 0.02),
        "rel_bias": nrm(ks[6], (NUM_BUCKETS, N_HEADS), 0.2),
        "conv_w": nrm(ks[7], (L, CONV_WIDTH, D_RNN), CONV_WIDTH ** -0.5),
        "conv_b": nrm(ks[8], (L, D_RNN), 0.02),
        "w_rg_a": nrm(ks[9], (L, RNN_BLOCKS, RNN_BLOCK_DIM, RNN_BLOCK_DIM), RNN_BLOCK_DIM ** -0.5),
        "b_rg_a": nrm(ks[10], (L, D_RNN), 0.1),
        "w_rg_i": nrm(ks[11], (L, RNN_BLOCKS, RNN_BLOCK_DIM, RNN_BLOCK_DIM), RNN_BLOCK_DIM ** -0.5),
        "b_rg_i": nrm(ks[12], (L, D_RNN), 0.1),
        "lru_lambda": lam,
        "w_proj_attn": nrm(ks[14], (L, D_ATTN, D), D_ATTN ** -0.5),
        "w_proj_rnn": nrm(ks[15], (L, D_RNN, D), D_RNN ** -0.5),
        "w_out": nrm(ks[16], (L, D, D), D ** -0.5),
        "norm2_w": 1.0 + nrm(ks[17], (L, D), 0.02),
        "w_ff1": nrm(ks[18], (L, D, D_FF), D ** -0.5),
        "w_ff2": nrm(ks[19], (L, D_FF, D), D_FF ** -0.5),
    }


def reference(x, norm1_w, w_in, b_gate, q_norm_w, k_norm_w, rel_bias, conv_w, conv_b,
              w_rg_a, b_rg_a, w_rg_i, b_rg_i, lru_lambda, w_proj_attn, w_proj_rnn,
              w_out, norm2_w, w_ff1, w_ff2):
    B, S, _ = x.shape
    split_points = [int(p) for p in np.cumsum(IN_SPLITS)[:-1]]
    for l in range(DEPTH):
        h = rmsnorm(x, norm1_w[l])
        z = h @ w_in[l]
        q, k, v, xr, yr, ga, gr = jnp.split(z, split_points, axis=-1)
        to_heads = lambda t: t.reshape(B, S, N_HEADS, HEAD_DIM).transpose(0, 2, 1, 3)
        qh = rmsnorm(to_heads(q), q_norm_w[l])
        kh = rmsnorm(to_heads(k), k_norm_w[l])
        vh = to_heads(v)
        o_attn = moba_attention(qh, kh, vh, rel_bias)
        o_attn = o_attn.transpose(0, 2, 1, 3).reshape(B, S, D_ATTN)
        xr = causal_depthwise_conv(xr, conv_w[l], conv_b[l])
        hr = rg_lru(xr, w_rg_a[l], b_rg_a[l], w_rg_i[l], b_rg_i[l], lru_lambda[l])
        o_rnn = hr * jax.nn.gelu(yr)
        g_attn = jax.nn.sigmoid(ga + b_gate[l, 0])
        g_rnn = jax.nn.sigmoid(gr + b_gate[l, 1])
        merged = g_attn * (o_attn @ w_proj_attn[l]) + g_rnn * (o_rnn @ w_proj_rnn[l])
        x = x + merged @ w_out[l]
        h2 = rmsnorm(x, norm2_w[l])
        x = x + jnp.square(jax.nn.relu(h2 @ w_ff1[l])) @ w_ff2[l]
    return x
```

```python
import os
import math
import contextlib
import numpy as np
import ml_dtypes
import concourse.bass as bass
import concourse.mybir as mybir
from concourse.bass_utils import run_bass_kernel_spmd

F32 = mybir.dt.float32
BF16 = mybir.dt.bfloat16
AF = mybir.ActivationFunctionType
ALU = mybir.AluOpType
AX = mybir.AxisListType

S, D, NH, DH = 4096, 1024, 8, 64
DIN, DFF = 5632, 4096
NT, CH, NCH, TPC = 32, 512, 8, 4
NB = 16
EPS = 1e-6
NEG = -30000.0
NCORES = 8


class Prog:
    ENGS = ('sync', 'scalar', 'vector', 'gpsimd', 'tensor')
    LAT = 200.0

    def __init__(self):
        self.ops = []
        self.barriers = []
        self.nowaw = set()
        self.fam = []
        self.dbg = None

    def add(self, eng, fn, reads=(), writes=(), dsem=None, cost=100.0, nbytes=0, nowaw=False, fam=None):
        self.ops.append((eng, fn, tuple(reads), tuple(writes), dsem, float(cost), nbytes))
        self.fam.append(fam)
        if nowaw:
            self.nowaw.add(len(self.ops) - 1)
        return len(self.ops) - 1

    def barrier(self):
        self.barriers.append(len(self.ops))

    def _deps(self):
        ops = self.ops
        n = len(ops)
        last_w = {}
        readers = {}
        deps = [None] * n
        for i, (eng, fn, reads, writes, dsem, cost, nb) in enumerate(ops):
            d = set()
            for k in reads:
                j = last_w.get(k)
                if j is not None:
                    d.add(j)
                if len(k) == 2 and k[0] == 'B':
                    for r in readers.get(k, ()):
                        if ops[r][0] != eng:
                            d.add(r)
            if i not in self.nowaw:
                for k in writes:
                    j = last_w.get(k)
                    if j is not None:
                        d.add(j)
                    for r in readers.get(k, ()):
                        d.add(r)
            for k in reads:
                readers.setdefault(k, []).append(i)
            for k in writes:
                last_w[k] = i
                if i not in self.nowaw:
                    readers[k] = []
            d.discard(i)
            deps[i] = d
        return deps, last_w

    def _schedule(self, deps, window):
        ops = self.ops
        n = len(ops)
        bounds = [0] + list(self.barriers) + [n]
        order = {e: [] for e in self.ENGS}
        seg_orders = []
        finish = [0.0] * n
        free = {e: 0.0 for e in self.ENGS}
        dma_pipe = 0.0
        done = [False] * n
        LAT = self.LAT
        act_set = ('exp', 'tanh')
        for si in range(len(bounds) - 1):
            lo, hi = bounds[si], bounds[si + 1]
            pend = {e: [] for e in self.ENGS}
            for i in range(lo, hi):
                pend[ops[i][0]].append(i)
            if lo > 0:
                tf = max(finish[:lo])
                for e in self.ENGS:
                    free[e] = max(free[e], tf)
                dma_pipe = max(dma_pipe, tf)
            seg = {e: [] for e in self.ENGS}
            remaining = hi - lo
            while remaining:
                best = None
                for e in self.ENGS:
                    lst = pend[e]
                    if not lst:
                        continue
                    w = 1 if e == 'sync' else window
                    fe = free[e]
                    for pos in range(min(w, len(lst))):
                        i = lst[pos]
                        isdma = ops[i][4] is not None
                        if isdma and pos > 0:
                            break
                        st = fe
                        ready = True
                        for j in deps[i]:
                            if not done[j]:
                                ready = False
                                break
                            if ops[j][0] == e and ops[j][4] is None:
                                t = finish[j] + (0.0 if e == 'tensor' else 130.0)
                            else:
                                t = finish[j] + LAT
                            if t > st:
                                st = t
                        if ready:
                            if best is None or st < best[0] - 1e-6:
                                best = (st, e, pos, i)
                            if st <= fe + 1e-6:
                                break
                        if isdma:
                            break
                st, e, pos, i = best
                eng, fn, reads, writes, dsem, cost, nb = ops[i]
                if self.dbg is not None and st > free[e] + 1.0:
                    jb = max(deps[i], key=lambda j: finish[j] + (0.0 if ops[j][0] == e and ops[j][4] is None else LAT))
                    import re as _re
                    kk = (si, e, ops[jb][0], _re.sub(r'[0-9]+', '#', (ops[jb][3] or ('?',))[0]), _re.sub(r'[0-9]+', '#', (writes or ('?',))[0]))
                    self.dbg[kk] = self.dbg.get(kk, 0.0) + (st - free[e])
                if dsem is not None:
                    free[e] = st + cost
                    xfer = nb / 270.0
                    t0 = max(st + cost, dma_pipe)
                    dma_pipe = t0 + xfer
                    finish[i] = t0 + xfer + 2000.0
                else:
                    fam = self.fam[i]
                    if fam is not None and fam not in act_set:
                        cost += 1283.0
                        act_set = {'exp': act_set if 'exp' in act_set else ('exp', 'tanh'), 'tanh': ('exp', 'tanh'),
                                   'ln': ('exp', 'ln'), 'sqrt': ('sqrt',)}[fam]
                    finish[i] = st + cost
                    free[e] = finish[i]
                done[i] = True
                pend[e].pop(pos)
                seg[e].append(i)
                remaining -= 1
            seg_orders.append(seg)
            self.seg_busy = getattr(self, 'seg_busy', []) + [{e: round(sum(ops[i][5] for i in seg[e]) / 1e3) for e in self.ENGS}]
            self.seg_end = getattr(self, 'seg_end', []) + [max(finish[lo:hi])]
            for e in self.ENGS:
                order[e].extend(seg[e])
        self.sim_ns = max(finish) if n else 0.0
        return order, seg_orders

    def emit(self, nc, final_wait_keys=(), window=256):
        ops = self.ops
        n = len(ops)
        deps, last_w = self._deps()
        order, seg_orders = self._schedule(deps, window)
        final_deps = set(last_w[k] for k in final_wait_keys if k in last_w)
        lasts = {}
        dl = {}
        for si, seg in enumerate(seg_orders):
            if si > 0:
                for e in self.ENGS:
                    if seg[e]:
                        i = seg[e][0]
                        for j in list(lasts.values()) + list(dl.values()):
                            deps[i].add(j)
            for e in self.ENGS:
                for i in seg[e]:
                    if ops[i][4] is None:
                        lasts[e] = i
            for e in self.ENGS:
                for i in seg[e]:
                    if ops[i][4] is not None:
                        dl[ops[i][4]] = i

        def skip(j, i):
            return (ops[j][0] == 'tensor' and ops[i][0] == 'tensor'
                    and ops[j][4] is None and ops[i][4] is None)

        has_dep = [False] * n
        for i in range(n):
            for j in deps[i]:
                if not skip(j, i):
                    has_dep[j] = True
        for j in final_deps:
            has_dep[j] = True
        sig = [None] * n
        dq = {}
        for e in self.ENGS:
            c = 0
            for i in order[e]:
                dsem = ops[i][4]
                if dsem is not None:
                    assert dq.setdefault(dsem, e) == e, "a DMA semaphore must stay on one queue"
                    sig[i] = ('d:' + dsem, None)
                elif has_dep[i]:
                    c += 1
                    sig[i] = ('e:' + e, c)
        dcount = {}
        for e in self.ENGS:
            for i in order[e]:
                dsem = ops[i][4]
                if dsem is not None:
                    dcount[dsem] = dcount.get(dsem, 0) + 16
                    sig[i] = ('d:' + dsem, dcount[dsem])
        sem_names = sorted(set(s[0] for s in sig if s is not None))
        self.n_sems = len(sem_names)
        stack = contextlib.ExitStack()
        sems = {}
        for sname in sem_names:
            sems[sname] = stack.enter_context(nc.semaphore(sname.replace(':', '_')))
        final_waits = {}
        for j in final_deps:
            s = sig[j]
            final_waits[s[0]] = max(final_waits.get(s[0], 0), s[1])
        self.n_waits = 0

        def run_engine(ename, handle):
            waited = {}
            for i in order[ename]:
                eng, fn, reads, writes, dsem, cost, nb = ops[i]
                need = {}
                for j in deps[i]:
                    if skip(j, i):
                        continue
                    s = sig[j]
                    if s[1] > need.get(s[0], 0):
                        need[s[0]] = s[1]
                for sname, val in need.items():
                    if waited.get(sname, 0) >= val:
                        continue
                    handle.wait_ge(sems[sname], val)
                    self.n_waits += 1
                    waited[sname] = val
                ins = fn(handle)
                if sig[i] is not None:
                    ins.then_inc(sems[sig[i][0]], 16 if dsem is not None else 1)
            if ename == 'sync':
                for sname, val in final_waits.items():
                    handle.wait_ge(sems[sname], val)

        with stack:
            with nc.Block() as block:
                @block.sync
                def _(e):
                    run_engine('sync', e)

                @block.scalar
                def _(e):
                    run_engine('scalar', e)

                @block.vector
                def _(e):
                    run_engine('vector', e)

                @block.gpsimd
                def _(e):
                    run_engine('gpsimd', e)

                @block.tensor
                def _(e):
                    run_engine('tensor', e)


class Arena:
    def __init__(self, t, nbytes):
        self.t = t
        self.cap = nbytes
        self.off = 0

    def mark(self):
        return self.off

    def reset(self, m):
        self.off = m

    def alloc(self, free_shape, dtype):
        esz = 4 if dtype == F32 else 2
        nel = 1
        for s_ in free_shape:
            nel *= s_
        nbytes = nel * esz
        off = (self.off + 63) // 64 * 64
        assert off + nbytes <= self.cap, ("arena overflow", off, nbytes, self.cap)
        self.off = off + nbytes
        v = self.t[:, off // 2:(off + nbytes) // 2]
        if dtype == F32:
            v = v.bitcast(F32)
        if len(free_shape) == 2:
            v = v.rearrange("p (a b) -> p a b", a=free_shape[0], b=free_shape[1])
        elif len(free_shape) == 3:
            v = v.rearrange("p (a b c) -> p a b c", a=free_shape[0], b=free_shape[1], c=free_shape[2])
        return v


def _t5_bucket(n):
    n = np.maximum(n, 0)
    nf = np.maximum(n, 1).astype(np.float32)
    large = 16 + (np.log(nf / np.float32(16)) / np.float32(math.log(8.0)) * np.float32(16)).astype(np.int32)
    large = np.minimum(large, 31)
    return np.where(n < 16, n, large)


def _host_consts():
    c = {}
    c["c_identb"] = np.eye(128, dtype=np.float32).astype(ml_dtypes.bfloat16)
    c["c_J"] = np.eye(128, dtype=np.float32)[::-1].copy()
    G = np.zeros((33, 384), np.float32)
    for i in range(383):
        d = i - 127
        if d < 0:
            G[32, i] = 1.0
        else:
            b = int(_t5_bucket(np.array([d]))[0])
            G[b, i] += 1.0
            G[31, i] -= 1.0
    c["c_G"] = G
    E = np.zeros((16, NH, S), np.float32)
    for nb in range(16):
        E[nb, :, nb * 256:(nb + 1) * 256] = 1.0
    c["c_E"] = E.reshape(16, NH * S).astype(ml_dtypes.bfloat16)
    neg1 = np.zeros((16, 16), np.float32)
    ownfix = np.full((16, 16), NEG, np.float32)
    static = np.full((16, 16), NEG, np.float32)
    for b in range(16):
        neg1[b, b:] = -1e9
        ownfix[b, b] = 0.0
        static[b, :b + 1] = 0.0
    c["c_mtab"] = np.broadcast_to(
        np.concatenate([neg1.reshape(-1), ownfix.reshape(-1), static.reshape(-1)])[None, :], (128, 768)).copy()
    return c


def build_nc(debug=False, phases=(1, 2, 3)):
    nc = bass.Bass("TRN2", target_bir_lowering=False)
    dt_in = lambda name, shape, dt=F32: nc.dram_tensor(name, list(shape), dt, kind="ExternalInput").ap()
    x_d = dt_in("x", [S, D])
    w_in_d = dt_in("w_in", [D, DIN])
    wpa_d = dt_in("w_proj_attn", [512, D])
    wpr_d = dt_in("w_proj_rnn", [D, D])
    wout_d = dt_in("w_out", [D, D])
    w1_d = dt_in("w_ff1", [D, DFF])
    w2_d = dt_in("w_ff2", [DFF, D])
    nw_d = dt_in("nw_bc", [128, 2 * D])
    nwc_d = dt_in("nw_col", [128, 16])
    qk_d = dt_in("qk_col", [64, 2])
    relb_d = dt_in("rel_bias", [32, NH])
    pvec_d = dt_in("pvec", [128, 8 * 11])
    wrga_d = dt_in("w_rg_a", [16, 64, 64])
    wrgi_d = dt_in("w_rg_i", [16, 64, 64])
    cid_d = dt_in("c_identb", [128, 128], BF16)
    cJ_d = dt_in("c_J", [128, 128])
    cG_d = dt_in("c_G", [33, 384])
    cE_d = dt_in("c_E", [16, NH * S], BF16)
    cm_d = dt_in("c_mtab", [128, 768])
    out_d = nc.dram_tensor("out", [S, D], F32, kind="ExternalOutput").ap()
    skind = "ExternalOutput" if debug else "Internal"
    scr_w = nc.dram_tensor("scr_w", [NH, 384], F32, kind="Internal").ap()
    scr_o = nc.dram_tensor("scr_o", [NCH, 128, 4 * CH], BF16, kind=skind).ap()
    scr_h = nc.dram_tensor("scr_h", [NCH, 128, 8 * CH], BF16, kind="Internal").ap()

    P = Prog()
    ARENA_BYTES = 212480
    st = contextlib.ExitStack()
    with st:
        arena_t = st.enter_context(nc.sbuf_tensor("arena", [128, ARENA_BYTES // 2], BF16))
        AR = Arena(arena_t, ARENA_BYTES)
        banks = [st.enter_context(nc.psum_tensor("bank%d" % i, [128, 512], F32)) for i in range(8)]
        bankf = [b[:, :] for b in banks]
        bankb = [b[:, :].bitcast(BF16) for b in banks]

        def _fsize(ap):
            n_ = 1
            for d_ in ap.shape[1:]:
                n_ *= d_
            return n_

        def op(eng, meth, reads, writes, cost=None, **kw):
            if cost is None:
                if eng == 'tensor':
                    nmov = _fsize(kw['rhs']) if 'rhs' in kw else 128
                    cost = max(nmov, 64) / 2.4 + 3.0
                    if 'rhs' in kw and kw['rhs'].dtype == F32:
                        cost *= 4
                else:
                    ref_ap = kw.get('out', kw.get('ap', kw.get('in_')))
                    nel = _fsize(ref_ap)
                    if eng == 'scalar':
                        cost = (nel + 175) / 1.4
                        if 'out' in kw and kw['out'].dtype == F32 and nel >= 256:
                            cost *= 1.35
                        elif nel >= 1024:
                            cost *= 1.15
                    elif eng == 'vector':
                        f_ = 1.0
                        if meth in ('tensor_tensor', 'scalar_tensor_tensor'):
                            f_ = 2.0 if nel >= 1024 else 1.2
                        if meth == 'tensor_tensor_scan':
                            f_ = 2.3
                        if meth == 'reciprocal':
                            f_ = 8.0
                        cost = (nel * f_ + 60) / 0.96
                    else:
                        f_ = 3.2 if meth == 'tensor_tensor' else 1.4
                        cost = (nel * f_ + 150) / 1.4
                        if kw.get('op', None) == ALU.pow:
                            cost = nel * 160.0 + 1500.0
            fam = None
            if eng == 'scalar' and 'func' in kw:
                fam = {AF.Exp: 'exp', AF.Tanh: 'tanh', AF.Ln: 'ln', AF.Sqrt: 'sqrt'}.get(kw['func'])
            return P.add(eng, lambda e, m=meth, kw=kw: getattr(e, m)(**kw), reads, writes, cost=cost, fam=fam)

        def dma(q, out, in_, dsem, reads, writes, nowaw=False, **kw):
            esz = 4 if in_.dtype == F32 else 2
            nb = esz * in_.shape[0] * _fsize(in_)
            return P.add(q, lambda e, kw=kw: e.dma_start(out=out, in_=in_, **kw), reads, writes, dsem=dsem,
                         cost=(1200.0 if q == 'gpsimd' else 60.0), nbytes=nb, nowaw=nowaw)

        identb = AR.alloc([128], BF16)
        nw = AR.alloc([2 * D], F32)
        pvec = AR.alloc([8, 11], F32)
        nwc = AR.alloc([16], F32)
        nsp = AR.alloc([8, 2], F32)
        hbias = AR.alloc([8, 4], F32)
        tmp8 = AR.alloc([8], F32)
        rstd1 = AR.alloc([NT], F32)
        mP = AR.mark()
        mtab = AR.alloc([3, 16, 16], F32)
        qkcol = AR.alloc([2], F32)
        qkw = AR.alloc([1], F32)
        BT = AR.alloc([NH, 256], BF16)
        dma('sync', identb, cid_d, 'c0a', [], ['identb'])
        dma('sync', nw, nw_d, 'c0b', [], ['nw'])
        dma('sync', nwc, nwc_d, 'c0f', [], ['nwc'])
        dma('sync', pvec, pvec_d.rearrange("p (t j) -> p t j", t=8, j=11), 'c0c', [], ['pvec'])
        dma('sync', mtab, cm_d.rearrange("p (a b c) -> p a b c", a=3, b=16, c=16), 'c0d', [], ['mtab'])
        dma('sync', qkcol[0:64, :], qk_d, 'c0e', [], ['qkcol'])
        op('vector', 'scalar_tensor_tensor', ['qkcol'], ['qkw'], out=qkw[0:64, :], in0=qkcol[0:64, 0:1], scalar=DH ** -0.5,
           in1=qkcol[0:64, 1:2], op0=ALU.mult, op1=ALU.mult)
        op('scalar', 'activation', ['pvec'], ['tmp8'], out=tmp8, in_=pvec[:, :, 7], func=AF.Exp, scale=-1.0)
        op('scalar', 'activation', ['tmp8'], ['tmp8'], out=tmp8, in_=tmp8, func=AF.Ln, bias=1.0)
        op('vector', 'tensor_scalar', ['tmp8'], ['nsp'], out=nsp[:, :, 0], in0=tmp8, scalar1=-4.0, scalar2=None, op0=ALU.mult)
        op('vector', 'tensor_scalar', ['tmp8'], ['nsp'], out=nsp[:, :, 1], in0=tmp8, scalar1=-8.0, scalar2=None, op0=ALU.mult)
        for jj, src in enumerate((5, 6, 8, 9)):
            op('vector', 'tensor_scalar', ['pvec'], ['hbias'], out=hbias[:, :, jj], in0=pvec[:, :, src], scalar1=0.5, scalar2=None, op0=ALU.mult)
        m0 = AR.mark()
        tabx = AR.alloc([NH], F32)
        Gs = AR.alloc([384], F32)
        Js = AR.alloc([128], F32)
        wrow = AR.alloc([384], F32)
        hk = AR.alloc([2, 256], F32)
        op('vector', 'memset', [], ['tabx'], ap=tabx[32:33, :], constant=NEG)
        dma('sync', tabx[0:32, :], relb_d, 'c1a', ['tabx'], ['tabx'])
        dma('sync', Gs[0:33, :], cG_d, 'c1b', [], ['Gs'])
        dma('sync', Js, cJ_d, 'c1c', [], ['Js'])
        op('tensor', 'matmul', ['tabx', 'Gs'], ['B0'], out=bankf[0][0:8, 0:384], lhsT=tabx[0:33, :], rhs=Gs[0:33, :], start=True, stop=True)
        op('vector', 'tensor_copy', ['B0'], ['wrow'], out=wrow[0:8, :], in_=bankf[0][0:8, 0:384])
        dma('sync', scr_w, wrow[0:8, :], 'c2', ['wrow'], ['scr_w'])
        for g in range(4):
            hank_src = bass.AP(scr_w.tensor, 2 * g * 384, [[1, 128], [384, 2], [1, 256]])
            dma('sync', hk, hank_src, 'c3', ['scr_w'], ['hk'])
            op('tensor', 'matmul', ['hk', 'Js'], ['B%d' % (1 + g)], out=bankf[1 + g], lhsT=Js, rhs=hk,
               start=True, stop=True)
            op('vector', 'tensor_copy', ['B%d' % (1 + g)], ['BT'], out=BT[:, 2 * g:2 * g + 2, :], in_=bankf[1 + g].rearrange("p (a b) -> p a b", a=2, b=256))
        assert AR.off - m0 <= 8192
        AR.reset(m0)

        tok_tiles = x_d.rearrange("(n p) d -> n p d", p=128)
        out_tiles = out_d.rearrange("(n p) d -> n p d", p=128)

        def frontend(tg, xk, xt, hb, stat, nwoff, hT, tl, rs=None, rs_key=None, compute=True, hkey=None, nw_folded=False):
            if rs is None:
                rs, rs_key = stat[:, 2:3], tg + 'stat'
            if compute:
                op('scalar', 'activation', [xk], [tg + 'hb', tg + 'stat'], out=hb, in_=xt, func=AF.Square, accum_out=stat[:, 0:1])
                op('scalar', 'activation', [tg + 'stat'], [tg + 'stat'], out=stat[:, 1:2], in_=stat[:, 0:1], func=AF.Ln, scale=1.0 / D, bias=EPS)
                op('scalar', 'activation', [tg + 'stat'], [rs_key], out=rs, in_=stat[:, 1:2], func=AF.Exp, scale=-0.5)
            if nw_folded:
                op('scalar', 'activation', [xk, rs_key], [tg + 'hb'], out=hb, in_=xt, func=AF.Identity, scale=rs)
            else:
                op('vector', 'scalar_tensor_tensor', [xk, rs_key, 'nw'], [tg + 'hb'], out=hb, in0=xt, scalar=rs,
                   in1=nw[:, nwoff:nwoff + D], op0=ALU.mult, op1=ALU.mult)
            for dt_ in range(8):
                op('tensor', 'transpose', [tg + 'hb', 'identb'], ['B0'], out=bankb[0][:, dt_ * 128:(dt_ + 1) * 128],
                   in_=hb[:, dt_ * 128:(dt_ + 1) * 128], identity=identb)
            op('scalar', 'copy', ['B0'], [(hkey or tg) + 'hT%d' % tl], out=hT[:, :, tl * 128:(tl + 1) * 128],
               in_=bankb[0].rearrange("p (a b) -> p a b", a=8, b=128))

        mA = AR.mark()
        if 1 in phases:
            oT = [AR.alloc([4, CH], BF16) for _ in range(2)]
            wqkv = AR.alloc([8, 1536], BF16)
            kT = AR.alloc([NH, S], BF16)
            Vg = AR.alloc([NT, NH, 65], BF16)
            xs = [AR.alloc([D], F32) for _ in range(2)]
            hb = AR.alloc([D], BF16)
            stat = AR.alloc([4], F32)
            hTA = [AR.alloc([8, CH], BF16) for _ in range(2)]
            sq = AR.alloc([2 * 512], F32)
            ssqk = AR.alloc([32], F32)
            qa = AR.alloc([NH, 64], BF16)
            ktm = AR.alloc([512], BF16)
            qT = [AR.alloc([NH, CH], BF16) for _ in range(2)]
            kmsum = AR.alloc([NH], F32)
            kmT = AR.alloc([NH, 16], BF16)
            gm = AR.alloc([NH, 16], F32)
            top8 = AR.alloc([NH, 8], F32)
            m01 = AR.alloc([NH, 16], F32)
            maskb = AR.alloc([NH, 16], BF16)
            NPT = 10
            PT = [AR.alloc([512], BF16) for _ in range(NPT)]
            otm = [AR.alloc([512], BF16) for _ in range(2)]
            rcp = AR.alloc([4], F32)

            w_in_v = w_in_d.rearrange("(t p) c -> p t c", p=128)
            for j in range(3):
                dma('gpsimd', wqkv[:, :, j * 512:(j + 1) * 512], w_in_v[:, :, j * 512:(j + 1) * 512], 'wqkv%d' % j, [], ['wqkv%d' % j], nowaw=True)
            dma('sync', kT[64:80, :, :], cE_d.rearrange("n (h s) -> n h s", h=NH, s=S), 'c4', [], ['kTE'])
            op('gpsimd', 'memset', [], ['Vg'], ap=Vg.rearrange("p a b c -> p (a b c)"), constant=1.0)
            op('vector', 'memset', [], ['kmT'], ap=kmT.rearrange("p a b -> p (a b)"), constant=0.0)

            def load_x(t):
                dma('sync', xs[t % 2], tok_tiles[t], 'xA%d' % (t % 2), [], ['Ax%d' % (t % 2)])

            SB = [1, 2, 3, 4, 5]
            sctr_box = [0]
            load_x(0)
            def FE(c, tls=range(TPC)):
                qTc = qT[c % 2]
                qp = 'q%d_' % (c % 2)
                hT = hTA[c % 2]
                hp = 'A%d' % (c % 2)
                for tl in tls:
                    t = c * TPC + tl
                    if t + 1 < NT:
                        load_x(t + 1)
                    frontend('A', 'Ax%d' % (t % 2), xs[t % 2], hb, stat, 0, hT, tl, rs=rstd1[:, t:t + 1], rs_key='rstd1_%d' % t, hkey=hp)
                    for j, bk in enumerate(('B1', 'B2', 'B3')):
                        for dt_ in range(8):
                            op('tensor', 'matmul', [hp + 'hT%d' % tl, 'wqkv%d' % j], [bk], out=bankf[1 + j], lhsT=hT[:, dt_, tl * 128:(tl + 1) * 128],
                               rhs=wqkv[:, dt_, j * 512:(j + 1) * 512], start=(dt_ == 0), stop=(dt_ == 7))
                    op('scalar', 'activation', ['B1'], ['sq0'], out=sq[:, 0:512], in_=bankf[1], func=AF.Square)
                    op('scalar', 'activation', ['B2'], ['sq1'], out=sq[:, 512:1024], in_=bankf[2], func=AF.Square)
                    op('vector', 'tensor_reduce', ['sq0', 'sq1'], ['ssqk'], out=ssqk[:, 0:16], in_=sq.rearrange("p (a b) -> p a b", a=16, b=64),
                       axis=AX.X, op=ALU.add)
                    op('scalar', 'activation', ['ssqk'], ['ssqk'], out=ssqk[:, 16:32], in_=ssqk[:, 0:16], func=AF.Ln, scale=1.0 / DH, bias=EPS)
                    op('scalar', 'activation', ['ssqk'], ['ssqk'], out=ssqk[:, 0:16], in_=ssqk[:, 16:32], func=AF.Exp, scale=-0.5)
                    op('vector', 'tensor_tensor', ['B1', 'ssqk'], ['qa'], out=qa, in0=bankf[1].rearrange("p (a b) -> p a b", a=8, b=64),
                       in1=ssqk[:, 0:8].unsqueeze(2).to_broadcast([128, 8, 64]), op=ALU.mult)
                    op('vector', 'tensor_tensor', ['B2', 'ssqk'], ['ktm'], out=ktm.rearrange("p (a b) -> p a b", a=8, b=64),
                       in0=bankf[2].rearrange("p (a b) -> p a b", a=8, b=64),
                       in1=ssqk[:, 8:16].unsqueeze(2).to_broadcast([128, 8, 64]), op=ALU.mult)
                    op('scalar', 'copy', ['B3', 'Vg'], ['V%d' % t], out=Vg[:, t, :, 0:64], in_=bankf[3].rearrange("p (a b) -> p a b", a=8, b=64))
                    for h in range(NH):
                        op('tensor', 'transpose', ['ktm', 'identb'], ['B4'], out=bankb[4][0:64, h * 128:(h + 1) * 128],
                           in_=ktm[:, h * 64:(h + 1) * 64], identity=identb)
                    op('scalar', 'activation', ['B4', 'qkw'], ['kT%d' % t], out=kT[0:64, :, t * 128:(t + 1) * 128],
                       in_=bankb[4][0:64, :].rearrange("p (a b) -> p a b", a=8, b=128), func=AF.Identity, scale=qkw[0:64, 0:1])
                    for h in range(NH):
                        op('tensor', 'transpose', ['qa', 'identb'], ['B5'], out=bankb[5][0:64, h * 128:(h + 1) * 128],
                           in_=qa[:, h, :], identity=identb)
                    op('vector', 'tensor_copy', ['B5'], [qp + 'qT%d' % tl], out=qTc[0:64, :, tl * 128:(tl + 1) * 128],
                       in_=bankb[5][0:64, :].rearrange("p (a b) -> p a b", a=8, b=128))
                    if tl == TPC - 1:
                        dma('sync', scr_h[c], hT.rearrange("p a b -> p (a b)"), 'sh%d' % (c % 2), [hp + 'hT%d' % i_ for i_ in range(TPC)], ['scr_h%d' % c])
                    b = t // 2
                    if t % 2 == 1 and b < NB - 1:
                        op('vector', 'tensor_reduce', ['kT%d' % (t - 1), 'kT%d' % t], ['kmsum'], out=kmsum[0:64, :],
                           in_=kT[0:64, :, b * 256:(b + 1) * 256], axis=AX.X, op=ALU.add)
                        op('vector', 'tensor_scalar', ['kmsum'], ['kmT'], out=kmT[0:64, :, b], in0=kmsum[0:64, :], scalar1=1.0 / 256,
                           scalar2=None, op0=ALU.mult)

            def GM(c):
                qTc = qT[c % 2]
                qp = 'q%d_' % (c % 2)
                for tl in range(TPC):
                    t = c * TPC + tl
                    b = t // 2
                    if b >= 3:
                        for h in range(NH):
                            op('tensor', 'matmul', [qp + 'qT%d' % tl, 'kmT'], ['B0'], out=bankf[0][:, h * 16:(h + 1) * 16],
                               lhsT=qTc[0:64, h, tl * 128:(tl + 1) * 128], rhs=kmT[0:64, h, :], start=True, stop=True)
                        op('vector', 'tensor_tensor', ['B0', 'mtab'], ['gm'], out=gm, in0=bankf[0][:, 0:128].rearrange("p (a b) -> p a b", a=8, b=16),
                           in1=mtab[:, 0, b, :].unsqueeze(1).to_broadcast([128, 8, 16]), op=ALU.add)
                        for h in range(NH):
                            op('vector', 'max', ['gm'], ['top8'], out=top8[:, h, :], in_=gm[:, h, :])
                        op('vector', 'tensor_tensor', ['gm', 'top8'], ['m01'], out=m01, in0=gm,
                           in1=top8[:, :, 2:3].to_broadcast([128, 8, 16]), op=ALU.is_lt)
                        op('vector', 'tensor_tensor', ['m01', 'mtab'], ['maskb'], out=maskb, in0=m01,
                           in1=mtab[:, 1, b, :].unsqueeze(1).to_broadcast([128, 8, 16]), op=ALU.mult)
                    else:
                        op('vector', 'tensor_copy', ['mtab'], ['maskb'], out=maskb,
                           in_=mtab[:, 2, b, :].unsqueeze(1).to_broadcast([128, 8, 16]))
                    for h in range(NH):
                        op('tensor', 'transpose', ['maskb', 'identb'], ['B0'], out=bankb[0][0:16, h * 128:(h + 1) * 128],
                           in_=maskb[:, h, :], identity=identb)
                    op('vector', 'tensor_copy', ['B0'], [qp + 'qTm%d' % tl], out=qTc[64:80, :, tl * 128:(tl + 1) * 128],
                       in_=bankb[0][0:16, :].rearrange("p (a b) -> p a b", a=8, b=128))

            def ATT(c, bl, heads=range(NH), fin=True):
                qTc = qT[c % 2]
                qp = 'q%d_' % (c % 2)
                sctr = sctr_box[0]
                if True:
                    b = 2 * c + bl
                    q0 = bl * 256
                    tq = (2 * bl, 2 * bl + 1)
                    qkeys = [qp + 'qT%d' % tq[0], qp + 'qT%d' % tq[1], qp + 'qTm%d' % tq[0], qp + 'qTm%d' % tq[1]]
                    for h in heads:
                        par = h % 2
                        ob = 6 + par
                        obk = 'B%d' % ob
                        npair = b + 1
                        for j in range(npair):
                            sb_ = SB[sctr % len(SB)]
                            pt = PT[sctr % NPT]
                            ptk = 'PT%d' % (sctr % NPT)
                            sk = 'B%d' % sb_
                            sctr += 1
                            sctr_box[0] = sctr
                            k0, k1 = 2 * j, 2 * j + 1
                            last = (j == b)
                            prev = (j == b - 1)
                            op('tensor', 'matmul', ['kT%d' % k0, 'kTE'] + qkeys, [sk], out=bankf[sb_][:, 0:256],
                               lhsT=kT[0:80, h, k0 * 128:(k0 + 1) * 128], rhs=qTc[0:80, h, q0:q0 + 256], start=True, stop=not last)
                            if last:
                                op('tensor', 'matmul', ['BT', 'identb'], [sk], out=bankf[sb_][:, 0:256], lhsT=identb, rhs=BT[:, h, 0:256],
                                   start=False, stop=True)
                                op('tensor', 'matmul', ['kT%d' % k1, 'kTE'] + qkeys, [sk], out=bankf[sb_][:, 256:384],
                                   lhsT=kT[0:80, h, k1 * 128:(k1 + 1) * 128], rhs=qTc[0:80, h, q0 + 128:q0 + 256], start=True, stop=False)
                                op('tensor', 'matmul', ['BT', 'identb'], [sk], out=bankf[sb_][:, 256:384], lhsT=identb, rhs=BT[:, h, 0:128],
                                   start=False, stop=True)
                            else:
                                op('tensor', 'matmul', ['kT%d' % k1, 'kTE'] + qkeys, [sk], out=bankf[sb_][:, 256:512],
                                   lhsT=kT[0:80, h, k1 * 128:(k1 + 1) * 128], rhs=qTc[0:80, h, q0:q0 + 256], start=True, stop=not prev)
                                if prev:
                                    op('tensor', 'matmul', ['BT', 'identb'], [sk], out=bankf[sb_][:, 256:384], lhsT=identb, rhs=BT[:, h, 128:256],
                                       start=False, stop=True)
                            if last:
                                op('scalar', 'activation', [sk], [ptk], out=pt[:, 0:384], in_=bankf[sb_][:, 0:384], func=AF.Exp)
                            else:
                                op('scalar', 'activation', [sk], [ptk], out=pt, in_=bankf[sb_], func=AF.Exp)
                            first = (j == 0)
                            op('tensor', 'matmul', [ptk, 'V%d' % k0], [obk], out=bankf[ob][:, 0:65], lhsT=pt[:, 0:128],
                               rhs=Vg[:, k0, h, :], start=first, stop=last, skip_group_check=True)
                            op('tensor', 'matmul', [ptk, 'V%d' % k0], [obk], out=bankf[ob][:, 128:193], lhsT=pt[:, 128:256],
                               rhs=Vg[:, k0, h, :], start=False, stop=False, skip_group_check=True)
                            if not last:
                                op('tensor', 'matmul', [ptk, 'V%d' % k1], [obk], out=bankf[ob][:, 0:65], lhsT=pt[:, 256:384],
                                   rhs=Vg[:, k1, h, :], start=False, stop=False, skip_group_check=True)
                            op('tensor', 'matmul', [ptk, 'V%d' % k1], [obk], out=bankf[ob][:, 128:193], lhsT=(pt[:, 256:384] if last else pt[:, 384:512]),
                               rhs=Vg[:, k1, h, :], start=False, stop=last, skip_group_check=True)
                        for qi in range(2):
                            oc = qi * 128
                            op('vector', 'reciprocal', [obk], ['rcp%d' % qi], out=rcp[:, qi:qi + 1], in_=bankf[ob][:, oc + 64:oc + 65])
                            op('vector', 'tensor_scalar', [obk, 'rcp%d' % qi], ['otm%d_%d' % (qi, h)], out=otm[qi][:, h * 64:(h + 1) * 64],
                               in0=bankf[ob][:, oc:oc + 64], scalar1=rcp[:, qi:qi + 1], scalar2=None, op0=ALU.mult)
                    for qi in (range(2) if fin else ()):
                        okeys = ['otm%d_%d' % (qi, h) for h in range(NH)]
                        for g in range(4):
                            op('tensor', 'transpose', okeys + ['identb'], ['B0'], out=bankb[0][:, g * 128:(g + 1) * 128],
                               in_=otm[qi][:, g * 128:(g + 1) * 128], identity=identb)
                        col = q0 + qi * 128
                        op('vector', 'tensor_copy', ['B0'], ['oT%d' % (c % 2)], out=oT[c % 2][:, :, col:col + 128],
                           in_=bankb[0][:, 0:512].rearrange("p (a b) -> p a b", a=4, b=128))

            FE(0)
            GM(0)
            FINE = False
            for c in range(NCH):
                if FINE and c + 1 < NCH:
                    ATT(c, 0, range(0, 4), False)
                    FE(c + 1, [0])
                    ATT(c, 0, range(4, 8), True)
                    FE(c + 1, [1])
                    ATT(c, 1, range(0, 4), False)
                    FE(c + 1, [2])
                    ATT(c, 1, range(4, 8), True)
                    FE(c + 1, [3])
                    GM(c + 1)
                else:
                    ORD = 'A3'
                    nx = c + 1 < NCH
                    if ORD == 'A0':
                        ATT(c, 0)
                        if nx:
                            FE(c + 1)
                            GM(c + 1)
                        ATT(c, 1)
                    elif ORD == 'A1':
                        ATT(c, 0)
                        if nx:
                            FE(c + 1)
                        ATT(c, 1)
                        if nx:
                            GM(c + 1)
                    elif ORD == 'A2':
                        if nx:
                            FE(c + 1)
                            GM(c + 1)
                        ATT(c, 0)
                        ATT(c, 1)
                    elif ORD == 'A3':
                        ATT(c, 0, range(0, 4), False)
                        if nx:
                            FE(c + 1)
                            GM(c + 1)
                        ATT(c, 0, range(4, 8), True)
                        ATT(c, 1)
                    elif ORD.startswith('K'):
                        kk = int(ORD[1:])
                        ATT(c, 0, range(0, kk), False)
                        if nx:
                            FE(c + 1)
                            GM(c + 1)
                        ATT(c, 0, range(kk, 8), True)
                        ATT(c, 1)
                    elif ORD.startswith('P'):
                        pa, pb, pg = int(ORD[1:3]), int(ORD[3:5]), int(ORD[5:7])
                        cuts = sorted(set([0, pa, pb, pg, 8, 16]))
                        for i_ in range(len(cuts) - 1):
                            lo_, hi_ = cuts[i_], cuts[i_ + 1]
                            if lo_ == pa and nx:
                                FE(c + 1, [0, 1])
                            if lo_ == pb and nx:
                                FE(c + 1, [2, 3])
                            if lo_ == pg and nx:
                                GM(c + 1)
                            if hi_ > lo_:
                                bl_ = 0 if lo_ < 8 else 1
                                ATT(c, bl_, range(lo_ - 8 * bl_, hi_ - 8 * bl_), (hi_ % 8 == 0))
                    elif ORD == 'A4':
                        ATT(c, 0)
                        if nx:
                            FE(c + 1, [0, 1])
                        ATT(c, 1, range(0, 4), False)
                        if nx:
                            FE(c + 1, [2, 3])
                            GM(c + 1)
                        ATT(c, 1, range(4, 8), True)
                dma('sync', scr_o[c], oT[c % 2].rearrange("p a b -> p (a b)"), 'so%d' % (c % 2), ['oT%d' % (c % 2)], ['scr_o%d' % c])
        AR.reset(mA)

        if 2 in phases:
            P.barrier()
            AR.reset(mP)
            wr = AR.alloc([8, 4096], BF16)
            wpa = AR.alloc([4, D], BF16)
            wpr = AR.alloc([8, D], BF16)
            wo = AR.alloc([8, D], BF16)
            wbda = AR.alloc([8, 128], BF16)
            wbdi = AR.alloc([8, 128], BF16)
            xres = [AR.alloc([D], F32) for _ in range(3)]
            hTs = [AR.alloc([8, CH], BF16) for _ in range(2)]
            halo = AR.alloc([8, 3], F32)
            hstate = AR.alloc([8], F32)
            NR = 2
            xr = [AR.alloc([CH + 3], F32) for _ in range(NR)]
            cv = [AR.alloc([CH], F32) for _ in range(NR)]
            cvb = [AR.alloc([CH], BF16) for _ in range(NR)]
            rr = [AR.alloc([CH], F32) for _ in range(NR)]
            ii = [AR.alloc([CH], F32) for _ in range(NR)]
            aa = [AR.alloc([CH], F32) for _ in range(NR)]
            gq = [AR.alloc([CH], F32) for _ in range(NR)]
            ornTs = [AR.alloc([8, CH], BF16) for _ in range(2)]
            oaTs = [AR.alloc([4, CH], BF16) for _ in range(2)]
            t1 = [AR.alloc([CH], F32) for _ in range(1)]
            t2 = [AR.alloc([CH], F32) for _ in range(1)]
            mT = AR.alloc([8, CH], BF16)

            op('vector', 'memset', [], ['wbda'], ap=wbda.rearrange("p a b -> p (a b)"), constant=0.0)
            op('vector', 'memset', [], ['wbdi'], ap=wbdi.rearrange("p a b -> p (a b)"), constant=0.0)
            op('vector', 'memset', [], ['halo'], ap=halo.rearrange("p a b -> p (a b)"), constant=0.0)
            op('vector', 'memset', [], ['hstate'], ap=hstate, constant=0.0)
            w_in_v = w_in_d.rearrange("(t p) c -> p t c", p=128)
            for g in range(8):
                if g == 4:
                    for wsrc, wdst, nm in ((wrga_d, wbda, 'wbda'), (wrgi_d, wbdi, 'wbdi')):
                        v = wsrc.rearrange("(t two) d e -> two d t e", two=2)
                        dma('gpsimd', wdst[0:64, :, 0:64], v[0], nm, [nm], [nm])
                        dma('gpsimd', wdst[64:128, :, 64:128], v[1], nm, [nm], [nm + 'x'])
                    dma('gpsimd', wpa, wpa_d.rearrange("(t p) c -> p t c", p=128), 'wpa', [], ['wpa'], nowaw=True)
                    dma('gpsimd', wpr, wpr_d.rearrange("(t p) c -> p t c", p=128), 'wpr', [], ['wpr'], nowaw=True)
                dma('gpsimd', wr[:, :, g * 512:(g + 1) * 512], w_in_v[:, :, 1536 + g * 512:1536 + (g + 1) * 512], 'wr%d' % g, [], ['wr%d' % g], nowaw=True)
            dma('gpsimd', wo, wout_d.rearrange("(t p) c -> p t c", p=128), 'wo', [], ['wo'], nowaw=True)

            def FEB(c):
                dma('sync', hTs[c % 2].rearrange("p a b -> p (a b)"), scr_h[c], 'hB%d' % (c % 2), ['scr_h%d' % c], ['BhT%d' % (c % 2)])

            def rnnA(c, ct, hT, hTk):
                r_ = ct % NR
                sfx = '%d' % r_
                bx, by = (1, 2) if ct % 2 == 0 else (3, 4)
                for dt_ in range(8):
                    op('tensor', 'matmul', hTk + ['wr%d' % (ct // 4)], ['B%d' % bx], out=bankf[bx], lhsT=wr[:, dt_, ct * 128:(ct + 1) * 128], rhs=hT[:, dt_, :],
                       start=(dt_ == 0), stop=(dt_ == 7))
                for dt_ in range(8):
                    op('tensor', 'matmul', hTk + ['wr%d' % (2 + ct // 4)], ['B%d' % by], out=bankf[by], lhsT=wr[:, dt_, 1024 + ct * 128:1024 + (ct + 1) * 128],
                       rhs=hT[:, dt_, :], start=(dt_ == 0), stop=(dt_ == 7))
                op('vector', 'tensor_copy', ['halo'], ['xrh' + sfx], out=xr[r_][:, 0:3], in_=halo[:, ct, :])
                op('scalar', 'copy', ['B%d' % bx], ['xr' + sfx], out=xr[r_][:, 3:CH + 3], in_=bankf[bx])
                op('scalar', 'activation', ['xr' + sfx, 'pvec'], ['cv' + sfx], out=cv[r_], in_=xr[r_][:, 3:CH + 3], func=AF.Identity, scale=pvec[:, ct, 3:4],
                   bias=pvec[:, ct, 4:5])
                for j in range(3):
                    op('vector', 'scalar_tensor_tensor', ['xr' + sfx, 'xrh' + sfx, 'pvec', 'cv' + sfx], ['cv' + sfx], out=cv[r_], in0=xr[r_][:, j:j + CH],
                       scalar=pvec[:, ct, j:j + 1], in1=cv[r_], op0=ALU.mult, op1=ALU.add)
                op('vector', 'tensor_copy', ['xr' + sfx], ['halo'], out=halo[:, ct, :], in_=xr[r_][:, CH:CH + 3])
                op('scalar', 'copy', ['cv' + sfx], ['cvb' + sfx], out=cvb[r_], in_=cv[r_])
                op('tensor', 'matmul', ['cvb' + sfx, 'wbda', 'wbdax'], ['B5'], out=bankf[5], lhsT=wbda[:, ct, :], rhs=cvb[r_], start=True, stop=True)
                op('tensor', 'matmul', ['cvb' + sfx, 'wbdi', 'wbdix'], ['B6'], out=bankf[6], lhsT=wbdi[:, ct, :], rhs=cvb[r_], start=True, stop=True)
                op('scalar', 'activation', ['B5', 'hbias'], ['rr' + sfx], out=rr[r_], in_=bankf[5], func=AF.Tanh, scale=0.5, bias=hbias[:, ct, 0:1])
                op('scalar', 'activation', ['B6', 'hbias'], ['ii' + sfx], out=ii[r_], in_=bankf[6], func=AF.Tanh, scale=0.5, bias=hbias[:, ct, 1:2])
                op('scalar', 'activation', ['rr' + sfx, 'nsp'], ['aa' + sfx], out=aa[r_], in_=rr[r_], func=AF.Exp, scale=nsp[:, ct, 0:1], bias=nsp[:, ct, 0:1])
                op('scalar', 'activation', ['rr' + sfx, 'nsp'], ['rr' + sfx], out=rr[r_], in_=rr[r_], func=AF.Exp, scale=nsp[:, ct, 1:2], bias=nsp[:, ct, 1:2])
                op('scalar', 'activation', ['B%d' % by], ['gq' + sfx], out=gq[r_], in_=bankf[by], func=AF.Square, scale=math.sqrt(0.044715))
                op('vector', 'scalar_tensor_tensor', ['gq' + sfx, 'B%d' % by], ['gq' + sfx], out=gq[r_], in0=gq[r_], scalar=1.0, in1=bankf[by],
                   op0=ALU.add, op1=ALU.mult)
                op('scalar', 'activation', ['gq' + sfx], ['gq' + sfx], out=gq[r_], in_=gq[r_], func=AF.Tanh, scale=math.sqrt(2.0 / math.pi))
                op('vector', 'scalar_tensor_tensor', ['gq' + sfx, 'B%d' % by], ['gq' + sfx], out=gq[r_], in0=gq[r_], scalar=1.0, in1=bankf[by],
                   op0=ALU.add, op1=ALU.mult)

            def rnnB(c, ct):
                r_ = ct % NR
                sfx = '%d' % r_
                op('gpsimd', 'tensor_scalar', ['rr' + sfx], ['rr' + sfx], out=rr[r_], in0=rr[r_], scalar1=1.0, scalar2=0.0, op0=ALU.min, op1=ALU.max)
                op('scalar', 'activation', ['rr' + sfx], ['rr' + sfx], out=rr[r_], in_=rr[r_], func=AF.Sqrt, scale=-0.0625, bias=0.0625)

            def rnnC(c, ct):
                r_ = ct % NR
                sfx = '%d' % r_
                hh_ = xr[r_][:, 0:CH]
                op('vector', 'scalar_tensor_tensor', ['ii' + sfx, 'rr' + sfx], ['ii' + sfx], out=ii[r_], in0=ii[r_], scalar=1.0, in1=rr[r_], op0=ALU.add, op1=ALU.mult)
                op('vector', 'tensor_tensor', ['ii' + sfx, 'cv' + sfx], ['cv' + sfx], out=cv[r_], in0=ii[r_], in1=cv[r_], op=ALU.mult)
                op('vector', 'tensor_tensor_scan', ['aa' + sfx, 'cv' + sfx, 'hstate'], ['xr' + sfx, 'xrh' + sfx], out=hh_, data0=aa[r_], data1=cv[r_],
                   initial=hstate[:, ct:ct + 1], op0=ALU.mult, op1=ALU.add)
                op('vector', 'tensor_copy', ['xr' + sfx], ['hstate'], out=hstate[:, ct:ct + 1], in_=xr[r_][:, CH - 1:CH])
                op('gpsimd', 'tensor_tensor', ['xr' + sfx, 'xrh' + sfx, 'gq' + sfx], ['ornT%d_%d' % (c % 2, ct)], out=ornTs[c % 2][:, ct, :], in0=hh_,
                   in1=gq[r_], op=ALU.mult)

            def gating(c, m, hT, hTk):
                b1, b2, b3, b4 = (1, 2, 3, 4) if m % 2 == 0 else (5, 6, 7, 0)
                for dt_ in range(8):
                    op('tensor', 'matmul', hTk + ['wr%d' % (4 + m // 4)], ['B%d' % b1], out=bankf[b1], lhsT=wr[:, dt_, 2048 + m * 128:2048 + (m + 1) * 128],
                       rhs=hT[:, dt_, :], start=(dt_ == 0), stop=(dt_ == 7))
                for dt_ in range(8):
                    op('tensor', 'matmul', hTk + ['wr%d' % (6 + m // 4)], ['B%d' % b2], out=bankf[b2], lhsT=wr[:, dt_, 3072 + m * 128:3072 + (m + 1) * 128],
                       rhs=hT[:, dt_, :], start=(dt_ == 0), stop=(dt_ == 7))
                for kt in range(4):
                    op('tensor', 'matmul', ['oaT%d' % (c % 2), 'wpa'], ['B%d' % b3], out=bankf[b3], lhsT=wpa[:, kt, m * 128:(m + 1) * 128], rhs=oaTs[c % 2][:, kt, :],
                       start=(kt == 0), stop=(kt == 3))
                for kt in range(8):
                    op('tensor', 'matmul', ['ornT%d_%d' % (c % 2, i_) for i_ in range(8)] + ['wpr'], ['B%d' % b4], out=bankf[b4], lhsT=wpr[:, kt, m * 128:(m + 1) * 128],
                       rhs=ornTs[c % 2][:, kt, :], start=(kt == 0), stop=(kt == 7))
                op('scalar', 'activation', ['B%d' % b1, 'hbias'], ['t1'], out=t1[0], in_=bankf[b1], func=AF.Tanh, scale=0.5, bias=hbias[:, m, 2:3])
                op('scalar', 'activation', ['B%d' % b2, 'hbias'], ['t2'], out=t2[0], in_=bankf[b2], func=AF.Tanh, scale=0.5, bias=hbias[:, m, 3:4])
                op('vector', 'scalar_tensor_tensor', ['t1', 'B%d' % b3], ['t1'], out=t1[0], in0=t1[0], scalar=1.0, in1=bankf[b3], op0=ALU.add, op1=ALU.mult)
                op('vector', 'scalar_tensor_tensor', ['t2', 'B%d' % b4], ['t2'], out=t2[0], in0=t2[0], scalar=1.0, in1=bankf[b4], op0=ALU.add, op1=ALU.mult)
                op('vector' if m % 2 else 'gpsimd', 'tensor_tensor', ['t1', 't2'], ['mT%d' % m], out=mT[:, m, :], in0=t1[0], in1=t2[0], op=ALU.add)

            mTk = ['mT%d' % i for i in range(8)]
            def rnn_pair(c, p2):
                hT = hTs[c % 2]
                hTk = ['BhT%d' % (c % 2)]
                for ct in (2 * p2, 2 * p2 + 1):
                    rnnA(c, ct, hT, hTk)
                for ct in (2 * p2, 2 * p2 + 1):
                    rnnB(c, ct)
                for ct in (2 * p2, 2 * p2 + 1):
                    rnnC(c, ct)

            def load_res(t):
                dma('sync', xres[t % 3], tok_tiles[t], 'xr%d' % (t % 3), ['out_s%d' % (t % 3)], ['Bxres%d' % (t % 3)])

            def load_oa(c):
                dma('sync', oaTs[c % 2].rearrange("p a b -> p (a b)"), scr_o[c], 'oa%d' % (c % 2), ['scr_o%d' % c], ['oaT%d' % (c % 2)])

            load_oa(0)
            FEB(0)
            if NCH > 1:
                FEB(1)
            for t_ in range(3):
                load_res(t_)
            for p2 in range(4):
                rnn_pair(0, p2)
            for c in range(NCH):
                hT = hTs[c % 2]
                hTk = ['BhT%d' % (c % 2)]
                if c + 1 < NCH:
                    load_oa(c + 1)
                BO = '0'
                for m in range(8):
                    if BO == '1' and m % 2 == 0 and c + 1 < NCH:
                        rnn_pair(c + 1, m // 2)
                    gating(c, m, hT, hTk)
                    if BO == '0' and m % 2 == 1 and c + 1 < NCH:
                        rnn_pair(c + 1, m // 2)
                    if BO == '2' and m < 4 and c + 1 < NCH:
                        rnn_pair(c + 1, m)
                if c + 2 < NCH:
                    FEB(c + 2)
                for tl in range(TPC):
                    t = c * TPC + tl
                    sl = t % 3
                    for hf in range(2):
                        bk = (1, 2, 3, 4)[(2 * tl + hf) % 4]
                        for kt in range(8):
                            op('tensor', 'matmul', mTk + ['wo'], ['B%d' % bk], out=bankf[bk], lhsT=mT[:, kt, tl * 128:(tl + 1) * 128],
                               rhs=wo[:, kt, hf * 512:(hf + 1) * 512], start=(kt == 0), stop=(kt == 7))
                        op('vector', 'scalar_tensor_tensor', ['B%d' % bk, 'Bxres%d' % sl], ['Bxres%d' % sl], out=xres[sl][:, hf * 512:(hf + 1) * 512],
                           in0=bankf[bk], scalar=0.5, in1=xres[sl][:, hf * 512:(hf + 1) * 512], op0=ALU.mult, op1=ALU.add)
                    dma('sync', out_tiles[t], xres[sl], 'os%d' % sl, ['Bxres%d' % sl], ['out%d' % t, 'out_s%d' % sl])
                    if t + 3 < NT:
                        load_res(t + 3)

        if 3 in phases:
            AR.reset(mP)
            W1 = AR.alloc([8, DFF], BF16)
            w1_v = w1_d.rearrange("(t p) c -> p t c", p=128)
            for g in range(8):
                dma('gpsimd', W1[:, :, g * 512:(g + 1) * 512], w1_v[:, :, g * 512:(g + 1) * 512], 'w1_%d' % g, [],
                    ['w1_%d' % g] + (['wr%d' % g] if 2 in phases else []))
            P.barrier()
            AR.reset(mP)
            W1 = AR.alloc([8, DFF], BF16)
            W2 = AR.alloc([32, D], BF16)
            xs = [AR.alloc([D], F32) for _ in range(2)]
            xres = [AR.alloc([D], F32) for _ in range(2)]
            hb = AR.alloc([D], BF16)
            stat = AR.alloc([4], F32)
            hTs = [AR.alloc([8, CH], BF16) for _ in range(2)]
            uT = AR.alloc([32, CH], BF16)
            rl = [AR.alloc([CH], F32) for _ in range(2)]
            w2_v = w2_d.rearrange("(t p) c -> p t c", p=128)
            for g in range(8):
                dma('gpsimd', W2[:, 4 * g:4 * g + 4, :], w2_v[:, 4 * g:4 * g + 4, :], 'w2_%d' % g, [], ['w2_%d' % g], nowaw=True)

            def load_xC(t):
                dma('sync', xs[t % 2], out_tiles[t], 'xC%d' % (t % 2), ['out%d' % t], ['Cx%d' % (t % 2)])

            def FEC(c):
                for tl in range(TPC):
                    t = c * TPC + tl
                    load_xC(t)
                    frontend('C', 'Cx%d' % (t % 2), xs[t % 2], hb, stat, D, hTs[c % 2], tl, hkey='C%d' % (c % 2))

            rctr = 0
            fctr = 0
            octr = 0
            FPOS = 'f31'
            FEC(0)
            for c in range(NCH):
                hT = hTs[c % 2]
                hTk = ['C%dhT%d' % (c % 2, i) for i in range(TPC)]
                for f in range(32):
                    bk = 1 + fctr % 4
                    rb = fctr % 2
                    fctr += 1
                    for dt_ in range(8):
                        op('tensor', 'matmul', hTk + ['w1_%d' % (f // 4)], ['B%d' % bk], out=bankf[bk], lhsT=W1[:, dt_, f * 128:(f + 1) * 128], rhs=hT[:, dt_, :],
                           start=(dt_ == 0), stop=(dt_ == 7))
                    op('scalar', 'activation', ['B%d' % bk], ['rl%d' % rb], out=rl[rb], in_=bankf[bk], func=AF.Relu)
                    op('gpsimd' if f % 3 == 2 else 'vector', 'tensor_tensor', ['rl%d' % rb], ['uT%d' % f], out=uT[:, f, :], in0=rl[rb], in1=rl[rb], op=ALU.mult)
                    if FPOS == 'f%d' % f and c + 1 < NCH:
                        FEC(c + 1)
                uTk = ['uT%d' % i for i in range(32)]
                for tl in range(TPC):
                    t = c * TPC + tl
                    sl = rctr % 2
                    rctr += 1
                    dma('sync', xres[sl], out_tiles[t], 'xq%d' % sl, ['out%d' % t, 'fin_s%d' % sl], ['Cxres%d' % sl])
                    for hf in range(2):
                        bk = 5 + octr % 3
                        octr += 1
                        for f in range(32):
                            op('tensor', 'matmul', uTk + ['w2_%d' % (f // 4)], ['B%d' % bk], out=bankf[bk], lhsT=uT[:, f, tl * 128:(tl + 1) * 128],
                               rhs=W2[:, f, hf * 512:(hf + 1) * 512], start=(f == 0), stop=(f == 31))
                        op('vector', 'tensor_tensor', ['B%d' % bk, 'Cxres%d' % sl], ['Cxres%d' % sl], out=xres[sl][:, hf * 512:(hf + 1) * 512], in0=bankf[bk],
                           in1=xres[sl][:, hf * 512:(hf + 1) * 512], op=ALU.add)
                    dma('sync', out_tiles[t], xres[sl], 'fs%d' % sl, ['Cxres%d' % sl], ['out%d' % t, 'fin_s%d' % sl])
                    if FPOS == 't%d' % tl and c + 1 < NCH:
                        FEC(c + 1)

        fk = ['scr_o%d' % c for c in range(NCH)] if 1 in phases else []
        P.emit(nc, final_wait_keys=fk + ['out%d' % t for t in range(NT)])
        if P.dbg is not None:
            for kk, v in sorted(P.dbg.items(), key=lambda t: -t[1])[:40]:
                print('GAP', kk, round(v / 1e3, 1))
        build_nc.stats = (len(P.ops), P.n_sems, P.n_waits, P.sim_ns, getattr(P, 'seg_end', None), getattr(P, 'seg_busy', None))
    return nc


def _prep_inputs(inputs):
    f = lambda a: np.ascontiguousarray(np.asarray(a, dtype=np.float32))
    shared = {}
    shared["w_in"] = f(inputs["w_in"][0])
    shared["w_proj_attn"] = f(inputs["w_proj_attn"][0])
    shared["w_proj_rnn"] = f(inputs["w_proj_rnn"][0])
    shared["w_out"] = f(inputs["w_out"][0])
    shared["w_ff1"] = f(inputs["w_ff1"][0])
    shared["w_ff2"] = f(inputs["w_ff2"][0])
    nwcat = np.concatenate([f(inputs["norm1_w"][0]), f(inputs["norm2_w"][0])])
    shared["nw_bc"] = np.ascontiguousarray(np.broadcast_to(nwcat[None, :], (128, 2 * D)))
    shared["nw_col"] = np.ascontiguousarray(nwcat.reshape(2, 8, 128).transpose(2, 0, 1).reshape(128, 16))
    shared["qk_col"] = np.ascontiguousarray(np.stack([f(inputs["q_norm_w"][0]), f(inputs["k_norm_w"][0])], axis=1))
    shared["rel_bias"] = f(inputs["rel_bias"])
    vecs = [f(inputs["conv_w"][0][j]) for j in range(4)] + [f(inputs["conv_b"][0]), f(inputs["b_rg_a"][0]),
            f(inputs["b_rg_i"][0]), f(inputs["lru_lambda"][0]), f(inputs["b_gate"][0][0]), f(inputs["b_gate"][0][1])]
    vecs.append(np.zeros(D, np.float32))
    pv = np.stack(vecs, axis=1)
    pv = pv.reshape(8, 128, 11).transpose(1, 0, 2)
    shared["pvec"] = np.ascontiguousarray(pv.reshape(128, 88))
    shared["w_rg_a"] = f(inputs["w_rg_a"][0])
    shared["w_rg_i"] = f(inputs["w_rg_i"][0])
    shared.update(_host_consts())
    return shared


def kernel(**inputs):
    x = np.asarray(inputs["x"], dtype=np.float32)
    shared = _prep_inputs(inputs)
    nc = build_nc()
    in_maps = []
    for i in range(NCORES):
        m = dict(shared)
        m["x"] = np.ascontiguousarray(x[i])
        in_maps.append(m)
    res = run_bass_kernel_spmd(nc, in_maps, core_ids=list(range(NCORES)))
    return np.stack([np.asarray(r["out"], dtype=np.float32) for r in res.results], axis=0)
```

```python
import os
import math
import contextlib
import numpy as np
import ml_dtypes
import concourse.bass as bass
import concourse.mybir as mybir
from concourse.bass_utils import run_bass_kernel_spmd

F32 = mybir.dt.float32
BF16 = mybir.dt.bfloat16
AF = mybir.ActivationFunctionType
ALU = mybir.AluOpType
AX = mybir.AxisListType

S, D, NH, DH = 4096, 1024, 8, 64
DIN, DFF = 5632, 4096
NT, CH, NCH, TPC = 32, 512, 8, 4
NB = 16
EPS = 1e-6
NEG = -30000.0
NCORES = 8


class Prog:
    ENGS = ('sync', 'scalar', 'vector', 'gpsimd', 'tensor')
    LAT = 200.0

    def __init__(self):
        self.ops = []
        self.barriers = []
        self.nowaw = set()
        self.fam = []
        self.dbg = None

    def add(self, eng, fn, reads=(), writes=(), dsem=None, cost=100.0, nbytes=0, nowaw=False, fam=None):
        self.ops.append((eng, fn, tuple(reads), tuple(writes), dsem, float(cost), nbytes))
        self.fam.append(fam)
        if nowaw:
            self.nowaw.add(len(self.ops) - 1)
        return len(self.ops) - 1

    def barrier(self):
        self.barriers.append(len(self.ops))

    def _deps(self):
        ops = self.ops
        n = len(ops)
        last_w = {}
        readers = {}
        deps = [None] * n
        for i, (eng, fn, reads, writes, dsem, cost, nb) in enumerate(ops):
            d = set()
            for k in reads:
                j = last_w.get(k)
                if j is not None:
                    d.add(j)
                if len(k) == 2 and k[0] == 'B':
                    for r in readers.get(k, ()):
                        if ops[r][0] != eng:
                            d.add(r)
            if i not in self.nowaw:
                for k in writes:
                    j = last_w.get(k)
                    if j is not None:
                        d.add(j)
                    for r in readers.get(k, ()):
                        d.add(r)
            for k in reads:
                readers.setdefault(k, []).append(i)
            for k in writes:
                last_w[k] = i
                if i not in self.nowaw:
                    readers[k] = []
            d.discard(i)
            deps[i] = d
        return deps, last_w

    def _schedule(self, deps, window):
        ops = self.ops
        n = len(ops)
        bounds = [0] + list(self.barriers) + [n]
        order = {e: [] for e in self.ENGS}
        seg_orders = []
        finish = [0.0] * n
        free = {e: 0.0 for e in self.ENGS}
        dma_pipe = 0.0
        done = [False] * n
        LAT = self.LAT
        act_set = ('exp', 'tanh')
        for si in range(len(bounds) - 1):
            lo, hi = bounds[si], bounds[si + 1]
            pend = {e: [] for e in self.ENGS}
            for i in range(lo, hi):
                pend[ops[i][0]].append(i)
            if lo > 0:
                tf = max(finish[:lo])
                for e in self.ENGS:
                    free[e] = max(free[e], tf)
                dma_pipe = max(dma_pipe, tf)
            seg = {e: [] for e in self.ENGS}
            remaining = hi - lo
            while remaining:
                best = None
                for e in self.ENGS:
                    lst = pend[e]
                    if not lst:
                        continue
                    w = 1 if e == 'sync' else window
                    fe = free[e]
                    for pos in range(min(w, len(lst))):
                        i = lst[pos]
                        isdma = ops[i][4] is not None
                        if isdma and pos > 0:
                            break
                        st = fe
                        ready = True
                        for j in deps[i]:
                            if not done[j]:
                                ready = False
                                break
                            if ops[j][0] == e and ops[j][4] is None:
                                t = finish[j] + (0.0 if e == 'tensor' else 130.0)
                            else:
                                t = finish[j] + LAT
                            if t > st:
                                st = t
                        if ready:
                            if best is None or st < best[0] - 1e-6:
                                best = (st, e, pos, i)
                            if st <= fe + 1e-6:
                                break
                        if isdma:
                            break
                st, e, pos, i = best
                eng, fn, reads, writes, dsem, cost, nb = ops[i]
                if self.dbg is not None and st > free[e] + 1.0:
                    jb = max(deps[i], key=lambda j: finish[j] + (0.0 if ops[j][0] == e and ops[j][4] is None else LAT))
                    import re as _re
                    kk = (si, e, ops[jb][0], _re.sub(r'[0-9]+', '#', (ops[jb][3] or ('?',))[0]), _re.sub(r'[0-9]+', '#', (writes or ('?',))[0]))
                    self.dbg[kk] = self.dbg.get(kk, 0.0) + (st - free[e])
                if dsem is not None:
                    free[e] = st + cost
                    xfer = nb / 270.0
                    t0 = max(st + cost, dma_pipe)
                    dma_pipe = t0 + xfer
                    finish[i] = t0 + xfer + 2000.0
                else:
                    fam = self.fam[i]
                    if fam is not None and fam not in act_set:
                        cost += 1283.0
                        act_set = {'exp': act_set if 'exp' in act_set else ('exp', 'tanh'), 'tanh': ('exp', 'tanh'),
                                   'ln': ('exp', 'ln'), 'sqrt': ('sqrt',)}[fam]
                    finish[i] = st + cost
                    free[e] = finish[i]
                done[i] = True
                pend[e].pop(pos)
                seg[e].append(i)
                remaining -= 1
            seg_orders.append(seg)
            self.seg_busy = getattr(self, 'seg_busy', []) + [{e: round(sum(ops[i][5] for i in seg[e]) / 1e3) for e in self.ENGS}]
            self.seg_end = getattr(self, 'seg_end', []) + [max(finish[lo:hi])]
            for e in self.ENGS:
                order[e].extend(seg[e])
        self.sim_ns = max(finish) if n else 0.0
        return order, seg_orders

    def emit(self, nc, final_wait_keys=(), window=256):
        ops = self.ops
        n = len(ops)
        deps, last_w = self._deps()
        order, seg_orders = self._schedule(deps, window)
        final_deps = set(last_w[k] for k in final_wait_keys if k in last_w)
        lasts = {}
        dl = {}
        for si, seg in enumerate(seg_orders):
            if si > 0:
                for e in self.ENGS:
                    if seg[e]:
                        i = seg[e][0]
                        for j in list(lasts.values()) + list(dl.values()):
                            deps[i].add(j)
            for e in self.ENGS:
                for i in seg[e]:
                    if ops[i][4] is None:
                        lasts[e] = i
            for e in self.ENGS:
                for i in seg[e]:
                    if ops[i][4] is not None:
                        dl[ops[i][4]] = i

        def skip(j, i):
            return (ops[j][0] == 'tensor' and ops[i][0] == 'tensor'
                    and ops[j][4] is None and ops[i][4] is None)

        has_dep = [False] * n
        for i in range(n):
            for j in deps[i]:
                if not skip(j, i):
                    has_dep[j] = True
        for j in final_deps:
            has_dep[j] = True
        sig = [None] * n
        dq = {}
        for e in self.ENGS:
            c = 0
            for i in order[e]:
                dsem = ops[i][4]
                if dsem is not None:
                    assert dq.setdefault(dsem, e) == e, "a DMA semaphore must stay on one queue"
                    sig[i] = ('d:' + dsem, None)
                elif has_dep[i]:
                    c += 1
                    sig[i] = ('e:' + e, c)
        dcount = {}
        for e in self.ENGS:
            for i in order[e]:
                dsem = ops[i][4]
                if dsem is not None:
                    dcount[dsem] = dcount.get(dsem, 0) + 16
                    sig[i] = ('d:' + dsem, dcount[dsem])
        sem_names = sorted(set(s[0] for s in sig if s is not None))
        self.n_sems = len(sem_names)
        stack = contextlib.ExitStack()
        sems = {}
        for sname in sem_names:
            sems[sname] = stack.enter_context(nc.semaphore(sname.replace(':', '_')))
        final_waits = {}
        for j in final_deps:
            s = sig[j]
            final_waits[s[0]] = max(final_waits.get(s[0], 0), s[1])
        self.n_waits = 0

        def run_engine(ename, handle):
            waited = {}
            for i in order[ename]:
                eng, fn, reads, writes, dsem, cost, nb = ops[i]
                need = {}
                for j in deps[i]:
                    if skip(j, i):
                        continue
                    s = sig[j]
                    if s[1] > need.get(s[0], 0):
                        need[s[0]] = s[1]
                for sname, val in need.items():
                    if waited.get(sname, 0) >= val:
                        continue
                    handle.wait_ge(sems[sname], val)
                    self.n_waits += 1
                    waited[sname] = val
                ins = fn(handle)
                if sig[i] is not None:
                    ins.then_inc(sems[sig[i][0]], 16 if dsem is not None else 1)
            if ename == 'sync':
                for sname, val in final_waits.items():
                    handle.wait_ge(sems[sname], val)

        with stack:
            with nc.Block() as block:
                @block.sync
                def _(e):
                    run_engine('sync', e)

                @block.scalar
                def _(e):
                    run_engine('scalar', e)

                @block.vector
                def _(e):
                    run_engine('vector', e)

                @block.gpsimd
                def _(e):
                    run_engine('gpsimd', e)

                @block.tensor
                def _(e):
                    run_engine('tensor', e)


class Arena:
    def __init__(self, t, nbytes):
        self.t = t
        self.cap = nbytes
        self.off = 0

    def mark(self):
        return self.off

    def reset(self, m):
        self.off = m

    def alloc(self, free_shape, dtype):
        esz = 4 if dtype == F32 else 2
        nel = 1
        for s_ in free_shape:
            nel *= s_
        nbytes = nel * esz
        off = (self.off + 63) // 64 * 64
        assert off + nbytes <= self.cap, ("arena overflow", off, nbytes, self.cap)
        self.off = off + nbytes
        v = self.t[:, off // 2:(off + nbytes) // 2]
        if dtype == F32:
            v = v.bitcast(F32)
        if len(free_shape) == 2:
            v = v.rearrange("p (a b) -> p a b", a=free_shape[0], b=free_shape[1])
        elif len(free_shape) == 3:
            v = v.rearrange("p (a b c) -> p a b c", a=free_shape[0], b=free_shape[1], c=free_shape[2])
        return v


def _t5_bucket(n):
    n = np.maximum(n, 0)
    nf = np.maximum(n, 1).astype(np.float32)
    large = 16 + (np.log(nf / np.float32(16)) / np.float32(math.log(8.0)) * np.float32(16)).astype(np.int32)
    large = np.minimum(large, 31)
    return np.where(n < 16, n, large)


def _host_consts():
    c = {}
    c["c_identb"] = np.eye(128, dtype=np.float32).astype(ml_dtypes.bfloat16)
    c["c_J"] = np.eye(128, dtype=np.float32)[::-1].copy()
    G = np.zeros((33, 384), np.float32)
    for i in range(383):
        d = i - 127
        if d < 0:
            G[32, i] = 1.0
        else:
            b = int(_t5_bucket(np.array([d]))[0])
            G[b, i] += 1.0
            G[31, i] -= 1.0
    c["c_G"] = G
    E = np.zeros((16, NH, S), np.float32)
    for nb in range(16):
        E[nb, :, nb * 256:(nb + 1) * 256] = 1.0
    c["c_E"] = E.reshape(16, NH * S).astype(ml_dtypes.bfloat16)
    neg1 = np.zeros((16, 16), np.float32)
    ownfix = np.full((16, 16), NEG, np.float32)
    static = np.full((16, 16), NEG, np.float32)
    for b in range(16):
        neg1[b, b:] = -1e9
        ownfix[b, b] = 0.0
        static[b, :b + 1] = 0.0
    c["c_mtab"] = np.broadcast_to(
        np.concatenate([neg1.reshape(-1), ownfix.reshape(-1), static.reshape(-1)])[None, :], (128, 768)).copy()
    return c


def build_nc(debug=False, phases=(1, 2, 3)):
    nc = bass.Bass("TRN2", target_bir_lowering=False)
    dt_in = lambda name, shape, dt=F32: nc.dram_tensor(name, list(shape), dt, kind="ExternalInput").ap()
    x_d = dt_in("x", [S, D])
    w_in_d = dt_in("w_in", [D, DIN])
    wpa_d = dt_in("w_proj_attn", [512, D])
    wpr_d = dt_in("w_proj_rnn", [D, D])
    wout_d = dt_in("w_out", [D, D])
    w1_d = dt_in("w_ff1", [D, DFF])
    w2_d = dt_in("w_ff2", [DFF, D])
    nw_d = dt_in("nw_bc", [128, 2 * D])
    nwc_d = dt_in("nw_col", [128, 16])
    qk_d = dt_in("qk_col", [64, 2])
    relb_d = dt_in("rel_bias", [32, NH])
    pvec_d = dt_in("pvec", [128, 8 * 11])
    wrga_d = dt_in("w_rg_a", [16, 64, 64])
    wrgi_d = dt_in("w_rg_i", [16, 64, 64])
    cid_d = dt_in("c_identb", [128, 128], BF16)
    cJ_d = dt_in("c_J", [128, 128])
    cG_d = dt_in("c_G", [33, 384])
    cE_d = dt_in("c_E", [16, NH * S], BF16)
    cm_d = dt_in("c_mtab", [128, 768])
    out_d = nc.dram_tensor("out", [S, D], F32, kind="ExternalOutput").ap()
    skind = "ExternalOutput" if debug else "Internal"
    scr_w = nc.dram_tensor("scr_w", [NH, 384], F32, kind="Internal").ap()
    scr_o = nc.dram_tensor("scr_o", [NCH, 128, 4 * CH], BF16, kind=skind).ap()
    scr_h = nc.dram_tensor("scr_h", [NCH, 128, 8 * CH], BF16, kind="Internal").ap()

    P = Prog()
    ARENA_BYTES = 212480
    st = contextlib.ExitStack()
    with st:
        arena_t = st.enter_context(nc.sbuf_tensor("arena", [128, ARENA_BYTES // 2], BF16))
        AR = Arena(arena_t, ARENA_BYTES)
        banks = [st.enter_context(nc.psum_tensor("bank%d" % i, [128, 512], F32)) for i in range(8)]
        bankf = [b[:, :] for b in banks]
        bankb = [b[:, :].bitcast(BF16) for b in banks]

        def _fsize(ap):
            n_ = 1
            for d_ in ap.shape[1:]:
                n_ *= d_
            return n_

        def op(eng, meth, reads, writes, cost=None, **kw):
            if cost is None:
                if eng == 'tensor':
                    nmov = _fsize(kw['rhs']) if 'rhs' in kw else 128
                    cost = max(nmov, 64) / 2.4 + 3.0
                    if 'rhs' in kw and kw['rhs'].dtype == F32:
                        cost *= 4
                else:
                    ref_ap = kw.get('out', kw.get('ap', kw.get('in_')))
                    nel = _fsize(ref_ap)
                    if eng == 'scalar':
                        cost = (nel + 175) / 1.4
                        if 'out' in kw and kw['out'].dtype == F32 and nel >= 256:
                            cost *= 1.35
                        elif nel >= 1024:
                            cost *= 1.15
                    elif eng == 'vector':
                        f_ = 1.0
                        if meth in ('tensor_tensor', 'scalar_tensor_tensor'):
                            f_ = 2.0 if nel >= 1024 else 1.2
                        if meth == 'tensor_tensor_scan':
                            f_ = 2.3
                        if meth == 'reciprocal':
                            f_ = 8.0
                        cost = (nel * f_ + 60) / 0.96
                    else:
                        f_ = 3.2 if meth == 'tensor_tensor' else 1.4
                        cost = (nel * f_ + 150) / 1.4
                        if kw.get('op', None) == ALU.pow:
                            cost = nel * 160.0 + 1500.0
            fam = None
            if eng == 'scalar' and 'func' in kw:
                fam = {AF.Exp: 'exp', AF.Tanh: 'tanh', AF.Ln: 'ln', AF.Sqrt: 'sqrt'}.get(kw['func'])
            return P.add(eng, lambda e, m=meth, kw=kw: getattr(e, m)(**kw), reads, writes, cost=cost, fam=fam)

        def dma(q, out, in_, dsem, reads, writes, nowaw=False, **kw):
            esz = 4 if in_.dtype == F32 else 2
            nb = esz * in_.shape[0] * _fsize(in_)
            return P.add(q, lambda e, kw=kw: e.dma_start(out=out, in_=in_, **kw), reads, writes, dsem=dsem,
                         cost=(1200.0 if q == 'gpsimd' else 60.0), nbytes=nb, nowaw=nowaw)

        identb = AR.alloc([128], BF16)
        nw = AR.alloc([2 * D], F32)
        pvec = AR.alloc([8, 11], F32)
        nwc = AR.alloc([16], F32)
        nsp = AR.alloc([8, 2], F32)
        hbias = AR.alloc([8, 4], F32)
        tmp8 = AR.alloc([8], F32)
        rstd1 = AR.alloc([NT], F32)
        mP = AR.mark()
        mtab = AR.alloc([3, 16, 16], F32)
        qkcol = AR.alloc([2], F32)
        qkw = AR.alloc([1], F32)
        BT = AR.alloc([NH, 256], BF16)
        dma('sync', identb, cid_d, 'c0a', [], ['identb'])
        dma('sync', nw, nw_d, 'c0b', [], ['nw'])
        dma('sync', nwc, nwc_d, 'c0f', [], ['nwc'])
        dma('sync', pvec, pvec_d.rearrange("p (t j) -> p t j", t=8, j=11), 'c0c', [], ['pvec'])
        dma('sync', mtab, cm_d.rearrange("p (a b c) -> p a b c", a=3, b=16, c=16), 'c0d', [], ['mtab'])
        dma('sync', qkcol[0:64, :], qk_d, 'c0e', [], ['qkcol'])
        op('vector', 'scalar_tensor_tensor', ['qkcol'], ['qkw'], out=qkw[0:64, :], in0=qkcol[0:64, 0:1], scalar=DH ** -0.5,
           in1=qkcol[0:64, 1:2], op0=ALU.mult, op1=ALU.mult)
        op('scalar', 'activation', ['pvec'], ['tmp8'], out=tmp8, in_=pvec[:, :, 7], func=AF.Exp, scale=-1.0)
        op('scalar', 'activation', ['tmp8'], ['tmp8'], out=tmp8, in_=tmp8, func=AF.Ln, bias=1.0)
        op('vector', 'tensor_scalar', ['tmp8'], ['nsp'], out=nsp[:, :, 0], in0=tmp8, scalar1=-4.0, scalar2=None, op0=ALU.mult)
        op('vector', 'tensor_scalar', ['tmp8'], ['nsp'], out=nsp[:, :, 1], in0=tmp8, scalar1=-8.0, scalar2=None, op0=ALU.mult)
        for jj, src in enumerate((5, 6, 8, 9)):
            op('vector', 'tensor_scalar', ['pvec'], ['hbias'], out=hbias[:, :, jj], in0=pvec[:, :, src], scalar1=0.5, scalar2=None, op0=ALU.mult)
        m0 = AR.mark()
        tabx = AR.alloc([NH], F32)
        Gs = AR.alloc([384], F32)
        Js = AR.alloc([128], F32)
        wrow = AR.alloc([384], F32)
        hk = AR.alloc([2, 256], F32)
        op('vector', 'memset', [], ['tabx'], ap=tabx[32:33, :], constant=NEG)
        dma('sync', tabx[0:32, :], relb_d, 'c1a', ['tabx'], ['tabx'])
        dma('sync', Gs[0:33, :], cG_d, 'c1b', [], ['Gs'])
        dma('sync', Js, cJ_d, 'c1c', [], ['Js'])
        op('tensor', 'matmul', ['tabx', 'Gs'], ['B0'], out=bankf[0][0:8, 0:384], lhsT=tabx[0:33, :], rhs=Gs[0:33, :], start=True, stop=True)
        op('vector', 'tensor_copy', ['B0'], ['wrow'], out=wrow[0:8, :], in_=bankf[0][0:8, 0:384])
        dma('sync', scr_w, wrow[0:8, :], 'c2', ['wrow'], ['scr_w'])
        for g in range(4):
            hank_src = bass.AP(scr_w.tensor, 2 * g * 384, [[1, 128], [384, 2], [1, 256]])
            dma('sync', hk, hank_src, 'c3', ['scr_w'], ['hk'])
            op('tensor', 'matmul', ['hk', 'Js'], ['B%d' % (1 + g)], out=bankf[1 + g], lhsT=Js, rhs=hk,
               start=True, stop=True)
            op('vector', 'tensor_copy', ['B%d' % (1 + g)], ['BT'], out=BT[:, 2 * g:2 * g + 2, :], in_=bankf[1 + g].rearrange("p (a b) -> p a b", a=2, b=256))
        assert AR.off - m0 <= 8192
        AR.reset(m0)

        tok_tiles = x_d.rearrange("(n p) d -> n p d", p=128)
        out_tiles = out_d.rearrange("(n p) d -> n p d", p=128)

        def frontend(tg, xk, xt, hb, stat, nwoff, hT, tl, rs=None, rs_key=None, compute=True, hkey=None, nw_folded=False):
            if rs is None:
                rs, rs_key = stat[:, 2:3], tg + 'stat'
            if compute:
                op('scalar', 'activation', [xk], [tg + 'hb', tg + 'stat'], out=hb, in_=xt, func=AF.Square, accum_out=stat[:, 0:1])
                op('scalar', 'activation', [tg + 'stat'], [tg + 'stat'], out=stat[:, 1:2], in_=stat[:, 0:1], func=AF.Ln, scale=1.0 / D, bias=EPS)
                op('scalar', 'activation', [tg + 'stat'], [rs_key], out=rs, in_=stat[:, 1:2], func=AF.Exp, scale=-0.5)
            if nw_folded:
                op('scalar', 'activation', [xk, rs_key], [tg + 'hb'], out=hb, in_=xt, func=AF.Identity, scale=rs)
            else:
                op('vector', 'scalar_tensor_tensor', [xk, rs_key, 'nw'], [tg + 'hb'], out=hb, in0=xt, scalar=rs,
                   in1=nw[:, nwoff:nwoff + D], op0=ALU.mult, op1=ALU.mult)
            for dt_ in range(8):
                op('tensor', 'transpose', [tg + 'hb', 'identb'], ['B0'], out=bankb[0][:, dt_ * 128:(dt_ + 1) * 128],
                   in_=hb[:, dt_ * 128:(dt_ + 1) * 128], identity=identb)
            op('scalar', 'copy', ['B0'], [(hkey or tg) + 'hT%d' % tl], out=hT[:, :, tl * 128:(tl + 1) * 128],
               in_=bankb[0].rearrange("p (a b) -> p a b", a=8, b=128))

        mA = AR.mark()
        if 1 in phases:
            oT = [AR.alloc([4, CH], BF16) for _ in range(2)]
            wqkv = AR.alloc([8, 1536], BF16)
            kT = AR.alloc([NH, S], BF16)
            Vg = AR.alloc([NT, NH, 65], BF16)
            xs = [AR.alloc([D], F32) for _ in range(2)]
            hb = AR.alloc([D], BF16)
            stat = AR.alloc([4], F32)
            hTA = [AR.alloc([8, CH], BF16) for _ in range(2)]
            sq = AR.alloc([2 * 512], F32)
            ssqk = AR.alloc([32], F32)
            qa = AR.alloc([NH, 64], BF16)
            ktm = AR.alloc([512], BF16)
            qT = [AR.alloc([NH, CH], BF16) for _ in range(2)]
            kmsum = AR.alloc([NH], F32)
            kmT = AR.alloc([NH, 16], BF16)
            gm = AR.alloc([NH, 16], F32)
            top8 = AR.alloc([NH, 8], F32)
            m01 = AR.alloc([NH, 16], F32)
            maskb = AR.alloc([NH, 16], BF16)
            NPT = 10
            PT = [AR.alloc([512], BF16) for _ in range(NPT)]
            otm = [AR.alloc([512], BF16) for _ in range(2)]
            rcp = AR.alloc([4], F32)

            w_in_v = w_in_d.rearrange("(t p) c -> p t c", p=128)
            for j in range(3):
                dma('gpsimd', wqkv[:, :, j * 512:(j + 1) * 512], w_in_v[:, :, j * 512:(j + 1) * 512], 'wqkv%d' % j, [], ['wqkv%d' % j], nowaw=True)
            dma('sync', kT[64:80, :, :], cE_d.rearrange("n (h s) -> n h s", h=NH, s=S), 'c4', [], ['kTE'])
            op('gpsimd', 'memset', [], ['Vg'], ap=Vg.rearrange("p a b c -> p (a b c)"), constant=1.0)
            op('vector', 'memset', [], ['kmT'], ap=kmT.rearrange("p a b -> p (a b)"), constant=0.0)

            def load_x(t):
                dma('sync', xs[t % 2], tok_tiles[t], 'xA%d' % (t % 2), [], ['Ax%d' % (t % 2)])

            SB = [1, 2, 3, 4, 5]
            sctr_box = [0]
            load_x(0)
            def LATE(t):
                return t // TPC >= 4

            def FE(c, tls=range(TPC)):
                qTc = qT[c % 2]
                qp = 'q%d_' % (c % 2)
                hT = hTA[c % 2]
                hp = 'A%d' % (c % 2)
                for tl in tls:
                    t = c * TPC + tl
                    if t + 1 < NT:
                        load_x(t + 1)
                    frontend('A', 'Ax%d' % (t % 2), xs[t % 2], hb, stat, 0, hT, tl, rs=rstd1[:, t:t + 1], rs_key='rstd1_%d' % t, hkey=hp)
                    for j, bk in enumerate(('B1', 'B2', 'B3')):
                        for dt_ in range(8):
                            op('tensor', 'matmul', [hp + 'hT%d' % tl, 'wqkv%d' % j], [bk], out=bankf[1 + j], lhsT=hT[:, dt_, tl * 128:(tl + 1) * 128],
                               rhs=wqkv[:, dt_, j * 512:(j + 1) * 512], start=(dt_ == 0), stop=(dt_ == 7))
                    op('scalar', 'activation', ['B1'], ['sq0'], out=sq[:, 0:512], in_=bankf[1], func=AF.Square)
                    op('scalar', 'activation', ['B2'], ['sq1'], out=sq[:, 512:1024], in_=bankf[2], func=AF.Square)
                    op('vector', 'tensor_reduce', ['sq0', 'sq1'], ['ssqk'], out=ssqk[:, 0:16], in_=sq.rearrange("p (a b) -> p a b", a=16, b=64),
                       axis=AX.X, op=ALU.add)
                    op('scalar', 'activation', ['ssqk'], ['ssqk'], out=ssqk[:, 16:32], in_=ssqk[:, 0:16], func=AF.Ln, scale=1.0 / DH, bias=EPS)
                    op('scalar', 'activation', ['ssqk'], ['ssqk'], out=ssqk[:, 0:16], in_=ssqk[:, 16:32], func=AF.Exp, scale=-0.5)
                    op('vector', 'tensor_tensor', ['B1', 'ssqk'], ['qa'], out=qa, in0=bankf[1].rearrange("p (a b) -> p a b", a=8, b=64),
                       in1=ssqk[:, 0:8].unsqueeze(2).to_broadcast([128, 8, 64]), op=ALU.mult)
                    op('vector', 'tensor_tensor', ['B2', 'ssqk'], ['ktm'], out=ktm.rearrange("p (a b) -> p a b", a=8, b=64),
                       in0=bankf[2].rearrange("p (a b) -> p a b", a=8, b=64),
                       in1=ssqk[:, 8:16].unsqueeze(2).to_broadcast([128, 8, 64]), op=ALU.mult)
                    op('scalar', 'copy', ['B3', 'Vg'], ['V%d' % t], out=Vg[:, t, :, 0:64], in_=bankf[3].rearrange("p (a b) -> p a b", a=8, b=64))
                    for h in range(NH):
                        op('tensor', 'transpose', ['ktm', 'identb'], ['B4'], out=bankb[4][0:64, h * 128:(h + 1) * 128],
                           in_=ktm[:, h * 64:(h + 1) * 64], identity=identb)
                    if LATE(t):
                        op('vector', 'tensor_scalar', ['B4', 'qkw'], ['kT%d' % t], out=kT[0:64, :, t * 128:(t + 1) * 128],
                           in0=bankb[4][0:64, :].rearrange("p (a b) -> p a b", a=8, b=128), scalar1=qkw[0:64, 0:1], scalar2=None, op0=ALU.mult)
                    else:
                        op('scalar', 'activation', ['B4', 'qkw'], ['kT%d' % t], out=kT[0:64, :, t * 128:(t + 1) * 128],
                           in_=bankb[4][0:64, :].rearrange("p (a b) -> p a b", a=8, b=128), func=AF.Identity, scale=qkw[0:64, 0:1])
                    for h in range(NH):
                        op('tensor', 'transpose', ['qa', 'identb'], ['B5'], out=bankb[5][0:64, h * 128:(h + 1) * 128],
                           in_=qa[:, h, :], identity=identb)
                    op('vector', 'tensor_copy', ['B5'], [qp + 'qT%d' % tl], out=qTc[0:64, :, tl * 128:(tl + 1) * 128],
                       in_=bankb[5][0:64, :].rearrange("p (a b) -> p a b", a=8, b=128))
                    if tl == TPC - 1:
                        dma('sync', scr_h[c], hT.rearrange("p a b -> p (a b)"), 'sh%d' % (c % 2), [hp + 'hT%d' % i_ for i_ in range(TPC)], ['scr_h%d' % c])
                    b = t // 2
                    if t % 2 == 1 and b < NB - 1:
                        op('vector', 'tensor_reduce', ['kT%d' % (t - 1), 'kT%d' % t], ['kmsum'], out=kmsum[0:64, :],
                           in_=kT[0:64, :, b * 256:(b + 1) * 256], axis=AX.X, op=ALU.add)
                        op('vector', 'tensor_scalar', ['kmsum'], ['kmT'], out=kmT[0:64, :, b], in0=kmsum[0:64, :], scalar1=1.0 / 256,
                           scalar2=None, op0=ALU.mult)

            def GM(c):
                qTc = qT[c % 2]
                qp = 'q%d_' % (c % 2)
                for tl in range(TPC):
                    t = c * TPC + tl
                    b = t // 2
                    if b >= 3:
                        for h in range(NH):
                            op('tensor', 'matmul', [qp + 'qT%d' % tl, 'kmT'], ['B0'], out=bankf[0][:, h * 16:(h + 1) * 16],
                               lhsT=qTc[0:64, h, tl * 128:(tl + 1) * 128], rhs=kmT[0:64, h, :], start=True, stop=True)
                        op('vector', 'tensor_tensor', ['B0', 'mtab'], ['gm'], out=gm, in0=bankf[0][:, 0:128].rearrange("p (a b) -> p a b", a=8, b=16),
                           in1=mtab[:, 0, b, :].unsqueeze(1).to_broadcast([128, 8, 16]), op=ALU.add)
                        for h in range(NH):
                            op('vector', 'max', ['gm'], ['top8'], out=top8[:, h, :], in_=gm[:, h, :])
                        op('vector', 'tensor_tensor', ['gm', 'top8'], ['m01'], out=m01, in0=gm,
                           in1=top8[:, :, 2:3].to_broadcast([128, 8, 16]), op=ALU.is_lt)
                        op('vector', 'tensor_tensor', ['m01', 'mtab'], ['maskb'], out=maskb, in0=m01,
                           in1=mtab[:, 1, b, :].unsqueeze(1).to_broadcast([128, 8, 16]), op=ALU.mult)
                    else:
                        op('vector', 'tensor_copy', ['mtab'], ['maskb'], out=maskb,
                           in_=mtab[:, 2, b, :].unsqueeze(1).to_broadcast([128, 8, 16]))
                    for h in range(NH):
                        op('tensor', 'transpose', ['maskb', 'identb'], ['B0'], out=bankb[0][0:16, h * 128:(h + 1) * 128],
                           in_=maskb[:, h, :], identity=identb)
                    op('vector', 'tensor_copy', ['B0'], [qp + 'qTm%d' % tl], out=qTc[64:80, :, tl * 128:(tl + 1) * 128],
                       in_=bankb[0][0:16, :].rearrange("p (a b) -> p a b", a=8, b=128))

            def ATT(c, bl, heads=range(NH), fin=True):
                qTc = qT[c % 2]
                qp = 'q%d_' % (c % 2)
                sctr = sctr_box[0]
                if True:
                    b = 2 * c + bl
                    q0 = bl * 256
                    tq = (2 * bl, 2 * bl + 1)
                    qkeys = [qp + 'qT%d' % tq[0], qp + 'qT%d' % tq[1], qp + 'qTm%d' % tq[0], qp + 'qTm%d' % tq[1]]
                    for h in heads:
                        par = h % 2
                        ob = 6 + par
                        obk = 'B%d' % ob
                        npair = b + 1
                        for j in range(npair):
                            sb_ = SB[sctr % len(SB)]
                            pt = PT[sctr % NPT]
                            ptk = 'PT%d' % (sctr % NPT)
                            sk = 'B%d' % sb_
                            sctr += 1
                            sctr_box[0] = sctr
                            k0, k1 = 2 * j, 2 * j + 1
                            last = (j == b)
                            prev = (j == b - 1)
                            op('tensor', 'matmul', ['kT%d' % k0, 'kTE'] + qkeys, [sk], out=bankf[sb_][:, 0:256],
                               lhsT=kT[0:80, h, k0 * 128:(k0 + 1) * 128], rhs=qTc[0:80, h, q0:q0 + 256], start=True, stop=not last)
                            if last:
                                op('tensor', 'matmul', ['BT', 'identb'], [sk], out=bankf[sb_][:, 0:256], lhsT=identb, rhs=BT[:, h, 0:256],
                                   start=False, stop=True)
                                op('tensor', 'matmul', ['kT%d' % k1, 'kTE'] + qkeys, [sk], out=bankf[sb_][:, 256:384],
                                   lhsT=kT[0:80, h, k1 * 128:(k1 + 1) * 128], rhs=qTc[0:80, h, q0 + 128:q0 + 256], start=True, stop=False)
                                op('tensor', 'matmul', ['BT', 'identb'], [sk], out=bankf[sb_][:, 256:384], lhsT=identb, rhs=BT[:, h, 0:128],
                                   start=False, stop=True)
                            else:
                                op('tensor', 'matmul', ['kT%d' % k1, 'kTE'] + qkeys, [sk], out=bankf[sb_][:, 256:512],
                                   lhsT=kT[0:80, h, k1 * 128:(k1 + 1) * 128], rhs=qTc[0:80, h, q0:q0 + 256], start=True, stop=not prev)
                                if prev:
                                    op('tensor', 'matmul', ['BT', 'identb'], [sk], out=bankf[sb_][:, 256:384], lhsT=identb, rhs=BT[:, h, 128:256],
                                       start=False, stop=True)
                            if last:
                                op('scalar', 'activation', [sk], [ptk], out=pt[:, 0:384], in_=bankf[sb_][:, 0:384], func=AF.Exp)
                            else:
                                op('scalar', 'activation', [sk], [ptk], out=pt, in_=bankf[sb_], func=AF.Exp)
                            first = (j == 0)
                            op('tensor', 'matmul', [ptk, 'V%d' % k0], [obk], out=bankf[ob][:, 0:65], lhsT=pt[:, 0:128],
                               rhs=Vg[:, k0, h, :], start=first, stop=last, skip_group_check=True)
                            op('tensor', 'matmul', [ptk, 'V%d' % k0], [obk], out=bankf[ob][:, 128:193], lhsT=pt[:, 128:256],
                               rhs=Vg[:, k0, h, :], start=False, stop=False, skip_group_check=True)
                            if not last:
                                op('tensor', 'matmul', [ptk, 'V%d' % k1], [obk], out=bankf[ob][:, 0:65], lhsT=pt[:, 256:384],
                                   rhs=Vg[:, k1, h, :], start=False, stop=False, skip_group_check=True)
                            op('tensor', 'matmul', [ptk, 'V%d' % k1], [obk], out=bankf[ob][:, 128:193], lhsT=(pt[:, 256:384] if last else pt[:, 384:512]),
                               rhs=Vg[:, k1, h, :], start=False, stop=last, skip_group_check=True)
                        for qi in range(2):
                            oc = qi * 128
                            op('vector', 'reciprocal', [obk], ['rcp%d' % qi], out=rcp[:, qi:qi + 1], in_=bankf[ob][:, oc + 64:oc + 65])
                            op('vector', 'tensor_scalar', [obk, 'rcp%d' % qi], ['otm%d_%d' % (qi, h)], out=otm[qi][:, h * 64:(h + 1) * 64],
                               in0=bankf[ob][:, oc:oc + 64], scalar1=rcp[:, qi:qi + 1], scalar2=None, op0=ALU.mult)
                    for qi in (range(2) if fin else ()):
                        okeys = ['otm%d_%d' % (qi, h) for h in range(NH)]
                        for g in range(4):
                            op('tensor', 'transpose', okeys + ['identb'], ['B0'], out=bankb[0][:, g * 128:(g + 1) * 128],
                               in_=otm[qi][:, g * 128:(g + 1) * 128], identity=identb)
                        col = q0 + qi * 128
                        op('vector', 'tensor_copy', ['B0'], ['oT%d' % (c % 2)], out=oT[c % 2][:, :, col:col + 128],
                           in_=bankb[0][:, 0:512].rearrange("p (a b) -> p a b", a=4, b=128))

            FE(0)
            GM(0)
            FINE = False
            for c in range(NCH):
                if FINE and c + 1 < NCH:
                    ATT(c, 0, range(0, 4), False)
                    FE(c + 1, [0])
                    ATT(c, 0, range(4, 8), True)
                    FE(c + 1, [1])
                    ATT(c, 1, range(0, 4), False)
                    FE(c + 1, [2])
                    ATT(c, 1, range(4, 8), True)
                    FE(c + 1, [3])
                    GM(c + 1)
                else:
                    ORD = 'A3'
                    nx = c + 1 < NCH
                    if ORD == 'A0':
                        ATT(c, 0)
                        if nx:
                            FE(c + 1)
                            GM(c + 1)
                        ATT(c, 1)
                    elif ORD == 'A1':
                        ATT(c, 0)
                        if nx:
                            FE(c + 1)
                        ATT(c, 1)
                        if nx:
                            GM(c + 1)
                    elif ORD == 'A2':
                        if nx:
                            FE(c + 1)
                            GM(c + 1)
                        ATT(c, 0)
                        ATT(c, 1)
                    elif ORD == 'A3':
                        ATT(c, 0, range(0, 4), False)
                        if nx:
                            FE(c + 1)
                            GM(c + 1)
                        ATT(c, 0, range(4, 8), True)
                        ATT(c, 1)
                    elif ORD.startswith('K'):
                        kk = int(ORD[1:])
                        ATT(c, 0, range(0, kk), False)
                        if nx:
                            FE(c + 1)
                            GM(c + 1)
                        ATT(c, 0, range(kk, 8), True)
                        ATT(c, 1)
                    elif ORD.startswith('P'):
                        pa, pb, pg = int(ORD[1:3]), int(ORD[3:5]), int(ORD[5:7])
                        cuts = sorted(set([0, pa, pb, pg, 8, 16]))
                        for i_ in range(len(cuts) - 1):
                            lo_, hi_ = cuts[i_], cuts[i_ + 1]
                            if lo_ == pa and nx:
                                FE(c + 1, [0, 1])
                            if lo_ == pb and nx:
                                FE(c + 1, [2, 3])
                            if lo_ == pg and nx:
                                GM(c + 1)
                            if hi_ > lo_:
                                bl_ = 0 if lo_ < 8 else 1
                                ATT(c, bl_, range(lo_ - 8 * bl_, hi_ - 8 * bl_), (hi_ % 8 == 0))
                    elif ORD == 'A4':
                        ATT(c, 0)
                        if nx:
                            FE(c + 1, [0, 1])
                        ATT(c, 1, range(0, 4), False)
                        if nx:
                            FE(c + 1, [2, 3])
                            GM(c + 1)
                        ATT(c, 1, range(4, 8), True)
                dma('sync', scr_o[c], oT[c % 2].rearrange("p a b -> p (a b)"), 'so%d' % (c % 2), ['oT%d' % (c % 2)], ['scr_o%d' % c])
        AR.reset(mA)

        if 2 in phases:
            P.barrier()
            AR.reset(mP)
            wr = AR.alloc([8, 4096], BF16)
            wpa = AR.alloc([4, D], BF16)
            wpr = AR.alloc([8, D], BF16)
            wo = AR.alloc([8, D], BF16)
            wbda = AR.alloc([8, 128], BF16)
            wbdi = AR.alloc([8, 128], BF16)
            xres = [AR.alloc([D], F32) for _ in range(3)]
            hTs = [AR.alloc([8, CH], BF16) for _ in range(2)]
            halo = AR.alloc([8, 3], F32)
            hstate = AR.alloc([8], F32)
            NR = 2
            xr = [AR.alloc([CH + 3], F32) for _ in range(NR)]
            cv = [AR.alloc([CH], F32) for _ in range(NR)]
            cvb = [AR.alloc([CH], BF16) for _ in range(NR)]
            rr = [AR.alloc([CH], F32) for _ in range(NR)]
            ii = [AR.alloc([CH], F32) for _ in range(NR)]
            aa = [AR.alloc([CH], F32) for _ in range(NR)]
            gq = [AR.alloc([CH], F32) for _ in range(NR)]
            ornTs = [AR.alloc([8, CH], BF16) for _ in range(2)]
            oaTs = [AR.alloc([4, CH], BF16) for _ in range(2)]
            t1 = [AR.alloc([CH], F32) for _ in range(1)]
            t2 = [AR.alloc([CH], F32) for _ in range(1)]
            mT = AR.alloc([8, CH], BF16)

            op('vector', 'memset', [], ['wbda'], ap=wbda.rearrange("p a b -> p (a b)"), constant=0.0)
            op('vector', 'memset', [], ['wbdi'], ap=wbdi.rearrange("p a b -> p (a b)"), constant=0.0)
            op('vector', 'memset', [], ['halo'], ap=halo.rearrange("p a b -> p (a b)"), constant=0.0)
            op('vector', 'memset', [], ['hstate'], ap=hstate, constant=0.0)
            w_in_v = w_in_d.rearrange("(t p) c -> p t c", p=128)
            for g in range(8):
                if g == 4:
                    for wsrc, wdst, nm in ((wrga_d, wbda, 'wbda'), (wrgi_d, wbdi, 'wbdi')):
                        v = wsrc.rearrange("(t two) d e -> two d t e", two=2)
                        dma('gpsimd', wdst[0:64, :, 0:64], v[0], nm, [nm], [nm])
                        dma('gpsimd', wdst[64:128, :, 64:128], v[1], nm, [nm], [nm + 'x'])
                    dma('gpsimd', wpa, wpa_d.rearrange("(t p) c -> p t c", p=128), 'wpa', [], ['wpa'], nowaw=True)
                    dma('gpsimd', wpr, wpr_d.rearrange("(t p) c -> p t c", p=128), 'wpr', [], ['wpr'], nowaw=True)
                dma('gpsimd', wr[:, :, g * 512:(g + 1) * 512], w_in_v[:, :, 1536 + g * 512:1536 + (g + 1) * 512], 'wr%d' % g, [], ['wr%d' % g], nowaw=True)
            dma('gpsimd', wo, wout_d.rearrange("(t p) c -> p t c", p=128), 'wo', [], ['wo'], nowaw=True)

            def FEB(c):
                dma('sync', hTs[c % 2].rearrange("p a b -> p (a b)"), scr_h[c], 'hB%d' % (c % 2), ['scr_h%d' % c], ['BhT%d' % (c % 2)])

            def rnnA(c, ct, hT, hTk):
                r_ = ct % NR
                sfx = '%d' % r_
                bx, by = (1, 2) if ct % 2 == 0 else (3, 4)
                for dt_ in range(8):
                    op('tensor', 'matmul', hTk + ['wr%d' % (ct // 4)], ['B%d' % bx], out=bankf[bx], lhsT=wr[:, dt_, ct * 128:(ct + 1) * 128], rhs=hT[:, dt_, :],
                       start=(dt_ == 0), stop=(dt_ == 7))
                for dt_ in range(8):
                    op('tensor', 'matmul', hTk + ['wr%d' % (2 + ct // 4)], ['B%d' % by], out=bankf[by], lhsT=wr[:, dt_, 1024 + ct * 128:1024 + (ct + 1) * 128],
                       rhs=hT[:, dt_, :], start=(dt_ == 0), stop=(dt_ == 7))
                op('vector', 'tensor_copy', ['halo'], ['xrh' + sfx], out=xr[r_][:, 0:3], in_=halo[:, ct, :])
                op('scalar', 'copy', ['B%d' % bx], ['xr' + sfx], out=xr[r_][:, 3:CH + 3], in_=bankf[bx])
                op('scalar', 'activation', ['xr' + sfx, 'pvec'], ['cv' + sfx], out=cv[r_], in_=xr[r_][:, 3:CH + 3], func=AF.Identity, scale=pvec[:, ct, 3:4],
                   bias=pvec[:, ct, 4:5])
                for j in range(3):
                    op('vector', 'scalar_tensor_tensor', ['xr' + sfx, 'xrh' + sfx, 'pvec', 'cv' + sfx], ['cv' + sfx], out=cv[r_], in0=xr[r_][:, j:j + CH],
                       scalar=pvec[:, ct, j:j + 1], in1=cv[r_], op0=ALU.mult, op1=ALU.add)
                op('vector', 'tensor_copy', ['xr' + sfx], ['halo'], out=halo[:, ct, :], in_=xr[r_][:, CH:CH + 3])
                op('scalar', 'copy', ['cv' + sfx], ['cvb' + sfx], out=cvb[r_], in_=cv[r_])
                op('tensor', 'matmul', ['cvb' + sfx, 'wbda', 'wbdax'], ['B5'], out=bankf[5], lhsT=wbda[:, ct, :], rhs=cvb[r_], start=True, stop=True)
                op('tensor', 'matmul', ['cvb' + sfx, 'wbdi', 'wbdix'], ['B6'], out=bankf[6], lhsT=wbdi[:, ct, :], rhs=cvb[r_], start=True, stop=True)
                op('scalar', 'activation', ['B5', 'hbias'], ['rr' + sfx], out=rr[r_], in_=bankf[5], func=AF.Tanh, scale=0.5, bias=hbias[:, ct, 0:1])
                op('scalar', 'activation', ['B6', 'hbias'], ['ii' + sfx], out=ii[r_], in_=bankf[6], func=AF.Tanh, scale=0.5, bias=hbias[:, ct, 1:2])
                op('scalar', 'activation', ['rr' + sfx, 'nsp'], ['aa' + sfx], out=aa[r_], in_=rr[r_], func=AF.Exp, scale=nsp[:, ct, 0:1], bias=nsp[:, ct, 0:1])
                op('scalar', 'activation', ['rr' + sfx, 'nsp'], ['rr' + sfx], out=rr[r_], in_=rr[r_], func=AF.Exp, scale=nsp[:, ct, 1:2], bias=nsp[:, ct, 1:2])
                op('scalar', 'activation', ['B%d' % by], ['gq' + sfx], out=gq[r_], in_=bankf[by], func=AF.Square, scale=math.sqrt(0.044715))
                op('vector', 'scalar_tensor_tensor', ['gq' + sfx, 'B%d' % by], ['gq' + sfx], out=gq[r_], in0=gq[r_], scalar=1.0, in1=bankf[by],
                   op0=ALU.add, op1=ALU.mult)
                op('scalar', 'activation', ['gq' + sfx], ['gq' + sfx], out=gq[r_], in_=gq[r_], func=AF.Tanh, scale=math.sqrt(2.0 / math.pi))
                op('vector', 'scalar_tensor_tensor', ['gq' + sfx, 'B%d' % by], ['gq' + sfx], out=gq[r_], in0=gq[r_], scalar=1.0, in1=bankf[by],
                   op0=ALU.add, op1=ALU.mult)

            def rnnB(c, ct):
                r_ = ct % NR
                sfx = '%d' % r_
                op('gpsimd', 'tensor_scalar', ['rr' + sfx], ['rr' + sfx], out=rr[r_], in0=rr[r_], scalar1=1.0, scalar2=0.0, op0=ALU.min, op1=ALU.max)
                op('scalar', 'activation', ['rr' + sfx], ['rr' + sfx], out=rr[r_], in_=rr[r_], func=AF.Sqrt, scale=-0.0625, bias=0.0625)

            def rnnC(c, ct):
                r_ = ct % NR
                sfx = '%d' % r_
                hh_ = xr[r_][:, 0:CH]
                op('vector', 'scalar_tensor_tensor', ['ii' + sfx, 'rr' + sfx], ['ii' + sfx], out=ii[r_], in0=ii[r_], scalar=1.0, in1=rr[r_], op0=ALU.add, op1=ALU.mult)
                op('vector', 'tensor_tensor', ['ii' + sfx, 'cv' + sfx], ['cv' + sfx], out=cv[r_], in0=ii[r_], in1=cv[r_], op=ALU.mult)
                op('vector', 'tensor_tensor_scan', ['aa' + sfx, 'cv' + sfx, 'hstate'], ['xr' + sfx, 'xrh' + sfx], out=hh_, data0=aa[r_], data1=cv[r_],
                   initial=hstate[:, ct:ct + 1], op0=ALU.mult, op1=ALU.add)
                op('vector', 'tensor_copy', ['xr' + sfx], ['hstate'], out=hstate[:, ct:ct + 1], in_=xr[r_][:, CH - 1:CH])
                op('gpsimd', 'tensor_tensor', ['xr' + sfx, 'xrh' + sfx, 'gq' + sfx], ['ornT%d_%d' % (c % 2, ct)], out=ornTs[c % 2][:, ct, :], in0=hh_,
                   in1=gq[r_], op=ALU.mult)

            def gating(c, m, hT, hTk):
                b1, b2, b3, b4 = (1, 2, 3, 4) if m % 2 == 0 else (5, 6, 7, 0)
                for dt_ in range(8):
                    op('tensor', 'matmul', hTk + ['wr%d' % (4 + m // 4)], ['B%d' % b1], out=bankf[b1], lhsT=wr[:, dt_, 2048 + m * 128:2048 + (m + 1) * 128],
                       rhs=hT[:, dt_, :], start=(dt_ == 0), stop=(dt_ == 7))
                for dt_ in range(8):
                    op('tensor', 'matmul', hTk + ['wr%d' % (6 + m // 4)], ['B%d' % b2], out=bankf[b2], lhsT=wr[:, dt_, 3072 + m * 128:3072 + (m + 1) * 128],
                       rhs=hT[:, dt_, :], start=(dt_ == 0), stop=(dt_ == 7))
                for kt in range(4):
                    op('tensor', 'matmul', ['oaT%d' % (c % 2), 'wpa'], ['B%d' % b3], out=bankf[b3], lhsT=wpa[:, kt, m * 128:(m + 1) * 128], rhs=oaTs[c % 2][:, kt, :],
                       start=(kt == 0), stop=(kt == 3))
                for kt in range(8):
                    op('tensor', 'matmul', ['ornT%d_%d' % (c % 2, i_) for i_ in range(8)] + ['wpr'], ['B%d' % b4], out=bankf[b4], lhsT=wpr[:, kt, m * 128:(m + 1) * 128],
                       rhs=ornTs[c % 2][:, kt, :], start=(kt == 0), stop=(kt == 7))
                op('scalar', 'activation', ['B%d' % b1, 'hbias'], ['t1'], out=t1[0], in_=bankf[b1], func=AF.Tanh, scale=0.5, bias=hbias[:, m, 2:3])
                op('scalar', 'activation', ['B%d' % b2, 'hbias'], ['t2'], out=t2[0], in_=bankf[b2], func=AF.Tanh, scale=0.5, bias=hbias[:, m, 3:4])
                op('vector', 'scalar_tensor_tensor', ['t1', 'B%d' % b3], ['t1'], out=t1[0], in0=t1[0], scalar=1.0, in1=bankf[b3], op0=ALU.add, op1=ALU.mult)
                op('vector', 'scalar_tensor_tensor', ['t2', 'B%d' % b4], ['t2'], out=t2[0], in0=t2[0], scalar=1.0, in1=bankf[b4], op0=ALU.add, op1=ALU.mult)
                op('vector' if m % 2 else 'gpsimd', 'tensor_tensor', ['t1', 't2'], ['mT%d' % m], out=mT[:, m, :], in0=t1[0], in1=t2[0], op=ALU.add)

            mTk = ['mT%d' % i for i in range(8)]
            def rnn_pair(c, p2):
                hT = hTs[c % 2]
                hTk = ['BhT%d' % (c % 2)]
                for ct in (2 * p2, 2 * p2 + 1):
                    rnnA(c, ct, hT, hTk)
                for ct in (2 * p2, 2 * p2 + 1):
                    rnnB(c, ct)
                for ct in (2 * p2, 2 * p2 + 1):
                    rnnC(c, ct)

            def load_res(t):
                dma('sync', xres[t % 3], tok_tiles[t], 'xr%d' % (t % 3), ['out_s%d' % (t % 3)], ['Bxres%d' % (t % 3)])

            def load_oa(c):
                dma('sync', oaTs[c % 2].rearrange("p a b -> p (a b)"), scr_o[c], 'oa%d' % (c % 2), ['scr_o%d' % c], ['oaT%d' % (c % 2)])

            load_oa(0)
            FEB(0)
            if NCH > 1:
                FEB(1)
            for t_ in range(3):
                load_res(t_)
            for p2 in range(4):
                rnn_pair(0, p2)
            for c in range(NCH):
                hT = hTs[c % 2]
                hTk = ['BhT%d' % (c % 2)]
                if c + 1 < NCH:
                    load_oa(c + 1)
                BO = '0'
                for m in range(8):
                    if BO == '1' and m % 2 == 0 and c + 1 < NCH:
                        rnn_pair(c + 1, m // 2)
                    gating(c, m, hT, hTk)
                    if BO == '0' and m % 2 == 1 and c + 1 < NCH:
                        rnn_pair(c + 1, m // 2)
                    if BO == '2' and m < 4 and c + 1 < NCH:
                        rnn_pair(c + 1, m)
                if c + 2 < NCH:
                    FEB(c + 2)
                for tl in range(TPC):
                    t = c * TPC + tl
                    sl = t % 3
                    for hf in range(2):
                        bk = (1, 2, 3, 4)[(2 * tl + hf) % 4]
                        for kt in range(8):
                            op('tensor', 'matmul', mTk + ['wo'], ['B%d' % bk], out=bankf[bk], lhsT=mT[:, kt, tl * 128:(tl + 1) * 128],
                               rhs=wo[:, kt, hf * 512:(hf + 1) * 512], start=(kt == 0), stop=(kt == 7))
                        op('vector', 'scalar_tensor_tensor', ['B%d' % bk, 'Bxres%d' % sl], ['Bxres%d' % sl], out=xres[sl][:, hf * 512:(hf + 1) * 512],
                           in0=bankf[bk], scalar=0.5, in1=xres[sl][:, hf * 512:(hf + 1) * 512], op0=ALU.mult, op1=ALU.add)
                    dma('sync', out_tiles[t], xres[sl], 'os%d' % sl, ['Bxres%d' % sl], ['out%d' % t, 'out_s%d' % sl])
                    if t + 3 < NT:
                        load_res(t + 3)

        if 3 in phases:
            P.barrier()
            AR.reset(mP)
            W1 = AR.alloc([8, DFF], BF16)
            W2 = AR.alloc([32, D], BF16)
            xs = [AR.alloc([D], F32) for _ in range(2)]
            xres = [AR.alloc([D], F32) for _ in range(2)]
            hb = AR.alloc([D], BF16)
            stat = AR.alloc([4], F32)
            hTs = [AR.alloc([8, CH], BF16) for _ in range(2)]
            uT = AR.alloc([32, CH], BF16)
            rl = [AR.alloc([CH], F32) for _ in range(2)]
            w1_v = w1_d.rearrange("(t p) c -> p t c", p=128)
            w2_v = w2_d.rearrange("(t p) c -> p t c", p=128)
            for g in range(8):
                dma('gpsimd', W1[:, :, g * 512:(g + 1) * 512], w1_v[:, :, g * 512:(g + 1) * 512], 'w1_%d' % g, [], ['w1_%d' % g], nowaw=True)
            for g in range(8):
                dma('gpsimd', W2[:, 4 * g:4 * g + 4, :], w2_v[:, 4 * g:4 * g + 4, :], 'w2_%d' % g, [], ['w2_%d' % g], nowaw=True)

            def load_xC(t):
                dma('sync', xs[t % 2], out_tiles[t], 'xC%d' % (t % 2), ['out%d' % t], ['Cx%d' % (t % 2)])

            def FEC(c):
                for tl in range(TPC):
                    t = c * TPC + tl
                    load_xC(t)
                    frontend('C', 'Cx%d' % (t % 2), xs[t % 2], hb, stat, D, hTs[c % 2], tl, hkey='C%d' % (c % 2))

            rctr = 0
            fctr = 0
            octr = 0
            FPOS = 'f31'
            FEC(0)
            for c in range(NCH):
                hT = hTs[c % 2]
                hTk = ['C%dhT%d' % (c % 2, i) for i in range(TPC)]
                for f in range(32):
                    bk = 1 + fctr % 4
                    rb = fctr % 2
                    fctr += 1
                    for dt_ in range(8):
                        op('tensor', 'matmul', hTk + ['w1_%d' % (f // 4)], ['B%d' % bk], out=bankf[bk], lhsT=W1[:, dt_, f * 128:(f + 1) * 128], rhs=hT[:, dt_, :],
                           start=(dt_ == 0), stop=(dt_ == 7))
                    op('scalar', 'activation', ['B%d' % bk], ['rl%d' % rb], out=rl[rb], in_=bankf[bk], func=AF.Relu)
                    op('gpsimd' if f % 3 == 2 else 'vector', 'tensor_tensor', ['rl%d' % rb], ['uT%d' % f], out=uT[:, f, :], in0=rl[rb], in1=rl[rb], op=ALU.mult)
                    if FPOS == 'f%d' % f and c + 1 < NCH:
                        FEC(c + 1)
                uTk = ['uT%d' % i for i in range(32)]
                for tl in range(TPC):
                    t = c * TPC + tl
                    sl = rctr % 2
                    rctr += 1
                    dma('sync', xres[sl], out_tiles[t], 'xq%d' % sl, ['out%d' % t, 'fin_s%d' % sl], ['Cxres%d' % sl])
                    for hf in range(2):
                        bk = 5 + octr % 3
                        octr += 1
                        for f in range(32):
                            op('tensor', 'matmul', uTk + ['w2_%d' % (f // 4)], ['B%d' % bk], out=bankf[bk], lhsT=uT[:, f, tl * 128:(tl + 1) * 128],
                               rhs=W2[:, f, hf * 512:(hf + 1) * 512], start=(f == 0), stop=(f == 31))
                        op('vector', 'tensor_tensor', ['B%d' % bk, 'Cxres%d' % sl], ['Cxres%d' % sl], out=xres[sl][:, hf * 512:(hf + 1) * 512], in0=bankf[bk],
                           in1=xres[sl][:, hf * 512:(hf + 1) * 512], op=ALU.add)
                    dma('sync', out_tiles[t], xres[sl], 'fs%d' % sl, ['Cxres%d' % sl], ['out%d' % t, 'fin_s%d' % sl])
                    if FPOS == 't%d' % tl and c + 1 < NCH:
                        FEC(c + 1)

        fk = ['scr_o%d' % c for c in range(NCH)] if 1 in phases else []
        P.emit(nc, final_wait_keys=fk + ['out%d' % t for t in range(NT)])
        if P.dbg is not None:
            for kk, v in sorted(P.dbg.items(), key=lambda t: -t[1])[:40]:
                print('GAP', kk, round(v / 1e3, 1))
        build_nc.stats = (len(P.ops), P.n_sems, P.n_waits, P.sim_ns, getattr(P, 'seg_end', None), getattr(P, 'seg_busy', None))
    return nc


def _prep_inputs(inputs):
    f = lambda a: np.ascontiguousarray(np.asarray(a, dtype=np.float32))
    shared = {}
    shared["w_in"] = f(inputs["w_in"][0])
    shared["w_proj_attn"] = f(inputs["w_proj_attn"][0])
    shared["w_proj_rnn"] = f(inputs["w_proj_rnn"][0])
    shared["w_out"] = f(inputs["w_out"][0])
    shared["w_ff1"] = f(inputs["w_ff1"][0])
    shared["w_ff2"] = f(inputs["w_ff2"][0])
    nwcat = np.concatenate([f(inputs["norm1_w"][0]), f(inputs["norm2_w"][0])])
    shared["nw_bc"] = np.ascontiguousarray(np.broadcast_to(nwcat[None, :], (128, 2 * D)))
    shared["nw_col"] = np.ascontiguousarray(nwcat.reshape(2, 8, 128).transpose(2, 0, 1).reshape(128, 16))
    shared["qk_col"] = np.ascontiguousarray(np.stack([f(inputs["q_norm_w"][0]), f(inputs["k_norm_w"][0])], axis=1))
    shared["rel_bias"] = f(inputs["rel_bias"])
    vecs = [f(inputs["conv_w"][0][j]) for j in range(4)] + [f(inputs["conv_b"][0]), f(inputs["b_rg_a"][0]),
            f(inputs["b_rg_i"][0]), f(inputs["lru_lambda"][0]), f(inputs["b_gate"][0][0]), f(inputs["b_gate"][0][1])]
    vecs.append(np.zeros(D, np.float32))
    pv = np.stack(vecs, axis=1)
    pv = pv.reshape(8, 128, 11).transpose(1, 0, 2)
    shared["pvec"] = np.ascontiguousarray(pv.reshape(128, 88))
    shared["w_rg_a"] = f(inputs["w_rg_a"][0])
    shared["w_rg_i"] = f(inputs["w_rg_i"][0])
    shared.update(_host_consts())
    return shared


def kernel(**inputs):
    x = np.asarray(inputs["x"], dtype=np.float32)
    shared = _prep_inputs(inputs)
    nc = build_nc()
    in_maps = []
    for i in range(NCORES):
        m = dict(shared)
        m["x"] = np.ascontiguousarray(x[i])
        in_maps.append(m)
    res = run_bass_kernel_spmd(nc, in_maps, core_ids=list(range(NCORES)))
    return np.stack([np.asarray(r["out"], dtype=np.float32) for r in res.results], axis=0)
```

```python
import os
import math
import contextlib
import numpy as np
import ml_dtypes
import concourse.bass as bass
import concourse.mybir as mybir
from concourse.bass_utils import run_bass_kernel_spmd

F32 = mybir.dt.float32
BF16 = mybir.dt.bfloat16
AF = mybir.ActivationFunctionType
ALU = mybir.AluOpType
AX = mybir.AxisListType

S, D, NH, DH = 4096, 1024, 8, 64
DIN, DFF = 5632, 4096
NT, CH, NCH, TPC = 32, 512, 8, 4
NB = 16
EPS = 1e-6
NEG = -30000.0
NCORES = 8


class Prog:
    ENGS = ('sync', 'scalar', 'vector', 'gpsimd', 'tensor')
    LAT = 200.0

    def __init__(self):
        self.ops = []
        self.barriers = []
        self.nowaw = set()
        self.fam = []
        self.dbg = None

    def add(self, eng, fn, reads=(), writes=(), dsem=None, cost=100.0, nbytes=0, nowaw=False, fam=None):
        self.ops.append((eng, fn, tuple(reads), tuple(writes), dsem, float(cost), nbytes))
        self.fam.append(fam)
        if nowaw:
            self.nowaw.add(len(self.ops) - 1)
        return len(self.ops) - 1

    def barrier(self):
        self.barriers.append(len(self.ops))

    def _deps(self):
        ops = self.ops
        n = len(ops)
        last_w = {}
        readers = {}
        deps = [None] * n
        for i, (eng, fn, reads, writes, dsem, cost, nb) in enumerate(ops):
            d = set()
            for k in reads:
                j = last_w.get(k)
                if j is not None:
                    d.add(j)
                if len(k) == 2 and k[0] == 'B':
                    for r in readers.get(k, ()):
                        if ops[r][0] != eng:
                            d.add(r)
            if i not in self.nowaw:
                for k in writes:
                    j = last_w.get(k)
                    if j is not None:
                        d.add(j)
                    for r in readers.get(k, ()):
                        d.add(r)
            for k in reads:
                readers.setdefault(k, []).append(i)
            for k in writes:
                last_w[k] = i
                if i not in self.nowaw:
                    readers[k] = []
            d.discard(i)
            deps[i] = d
        return deps, last_w

    def _schedule(self, deps, window):
        ops = self.ops
        n = len(ops)
        bounds = [0] + list(self.barriers) + [n]
        order = {e: [] for e in self.ENGS}
        seg_orders = []
        finish = [0.0] * n
        free = {e: 0.0 for e in self.ENGS}
        dma_pipe = 0.0
        done = [False] * n
        LAT = self.LAT
        act_set = ('exp', 'tanh')
        for si in range(len(bounds) - 1):
            lo, hi = bounds[si], bounds[si + 1]
            pend = {e: [] for e in self.ENGS}
            for i in range(lo, hi):
                pend[ops[i][0]].append(i)
            if lo > 0:
                tf = max(finish[:lo])
                for e in self.ENGS:
                    free[e] = max(free[e], tf)
                dma_pipe = max(dma_pipe, tf)
            seg = {e: [] for e in self.ENGS}
            remaining = hi - lo
            while remaining:
                best = None
                for e in self.ENGS:
                    lst = pend[e]
                    if not lst:
                        continue
                    w = 1 if e == 'sync' else window
                    fe = free[e]
                    for pos in range(min(w, len(lst))):
                        i = lst[pos]
                        isdma = ops[i][4] is not None
                        if isdma and pos > 0:
                            break
                        st = fe
                        ready = True
                        for j in deps[i]:
                            if not done[j]:
                                ready = False
                                break
                            if ops[j][0] == e and ops[j][4] is None:
                                t = finish[j] + (0.0 if e == 'tensor' else 130.0)
                            else:
                                t = finish[j] + LAT
                            if t > st:
                                st = t
                        if ready:
                            if best is None or st < best[0] - 1e-6:
                                best = (st, e, pos, i)
                            if st <= fe + 1e-6:
                                break
                        if isdma:
                            break
                st, e, pos, i = best
                eng, fn, reads, writes, dsem, cost, nb = ops[i]
                if self.dbg is not None and st > free[e] + 1.0:
                    jb = max(deps[i], key=lambda j: finish[j] + (0.0 if ops[j][0] == e and ops[j][4] is None else LAT))
                    import re as _re
                    kk = (si, e, ops[jb][0], _re.sub(r'[0-9]+', '#', (ops[jb][3] or ('?',))[0]), _re.sub(r'[0-9]+', '#', (writes or ('?',))[0]))
                    self.dbg[kk] = self.dbg.get(kk, 0.0) + (st - free[e])
                if dsem is not None:
                    free[e] = st + cost
                    xfer = nb / 270.0
                    t0 = max(st + cost, dma_pipe)
                    dma_pipe = t0 + xfer
                    finish[i] = t0 + xfer + 2000.0
                else:
                    fam = self.fam[i]
                    if fam is not None and fam not in act_set:
                        cost += 1283.0
                        act_set = {'exp': act_set if 'exp' in act_set else ('exp', 'tanh'), 'tanh': ('exp', 'tanh'),
                                   'ln': ('exp', 'ln'), 'sqrt': ('sqrt',)}[fam]
                    finish[i] = st + cost
                    free[e] = finish[i]
                done[i] = True
                pend[e].pop(pos)
                seg[e].append(i)
                remaining -= 1
            seg_orders.append(seg)
            self.seg_busy = getattr(self, 'seg_busy', []) + [{e: round(sum(ops[i][5] for i in seg[e]) / 1e3) for e in self.ENGS}]
            self.seg_end = getattr(self, 'seg_end', []) + [max(finish[lo:hi])]
            for e in self.ENGS:
                order[e].extend(seg[e])
        self.sim_ns = max(finish) if n else 0.0
        return order, seg_orders

    def emit(self, nc, final_wait_keys=(), window=256):
        ops = self.ops
        n = len(ops)
        deps, last_w = self._deps()
        order, seg_orders = self._schedule(deps, window)
        final_deps = set(last_w[k] for k in final_wait_keys if k in last_w)
        lasts = {}
        dl = {}
        for si, seg in enumerate(seg_orders):
            if si > 0:
                for e in self.ENGS:
                    if seg[e]:
                        i = seg[e][0]
                        for j in list(lasts.values()) + list(dl.values()):
                            deps[i].add(j)
            for e in self.ENGS:
                for i in seg[e]:
                    if ops[i][4] is None:
                        lasts[e] = i
            for e in self.ENGS:
                for i in seg[e]:
                    if ops[i][4] is not None:
                        dl[ops[i][4]] = i

        def skip(j, i):
            return (ops[j][0] == 'tensor' and ops[i][0] == 'tensor'
                    and ops[j][4] is None and ops[i][4] is None)

        has_dep = [False] * n
        for i in range(n):
            for j in deps[i]:
                if not skip(j, i):
                    has_dep[j] = True
        for j in final_deps:
            has_dep[j] = True
        sig = [None] * n
        dq = {}
        for e in self.ENGS:
            c = 0
            for i in order[e]:
                dsem = ops[i][4]
                if dsem is not None:
                    assert dq.setdefault(dsem, e) == e, "a DMA semaphore must stay on one queue"
                    sig[i] = ('d:' + dsem, None)
                elif has_dep[i]:
                    c += 1
                    sig[i] = ('e:' + e, c)
        dcount = {}
        for e in self.ENGS:
            for i in order[e]:
                dsem = ops[i][4]
                if dsem is not None:
                    dcount[dsem] = dcount.get(dsem, 0) + 16
                    sig[i] = ('d:' + dsem, dcount[dsem])
        sem_names = sorted(set(s[0] for s in sig if s is not None))
        self.n_sems = len(sem_names)
        stack = contextlib.ExitStack()
        sems = {}
        for sname in sem_names:
            sems[sname] = stack.enter_context(nc.semaphore(sname.replace(':', '_')))
        final_waits = {}
        for j in final_deps:
            s = sig[j]
            final_waits[s[0]] = max(final_waits.get(s[0], 0), s[1])
        self.n_waits = 0

        def run_engine(ename, handle):
            waited = {}
            for i in order[ename]:
                eng, fn, reads, writes, dsem, cost, nb = ops[i]
                need = {}
                for j in deps[i]:
                    if skip(j, i):
                        continue
                    s = sig[j]
                    if s[1] > need.get(s[0], 0):
                        need[s[0]] = s[1]
                for sname, val in need.items():
                    if waited.get(sname, 0) >= val:
                        continue
                    handle.wait_ge(sems[sname], val)
                    self.n_waits += 1
                    waited[sname] = val
                ins = fn(handle)
                if sig[i] is not None:
                    ins.then_inc(sems[sig[i][0]], 16 if dsem is not None else 1)
            if ename == 'sync':
                for sname, val in final_waits.items():
                    handle.wait_ge(sems[sname], val)

        with stack:
            with nc.Block() as block:
                @block.sync
                def _(e):
                    run_engine('sync', e)

                @block.scalar
                def _(e):
                    run_engine('scalar', e)

                @block.vector
                def _(e):
                    run_engine('vector', e)

                @block.gpsimd
                def _(e):
                    run_engine('gpsimd', e)

                @block.tensor
                def _(e):
                    run_engine('tensor', e)


class Arena:
    def __init__(self, t, nbytes):
        self.t = t
        self.cap = nbytes
        self.off = 0

    def mark(self):
        return self.off

    def reset(self, m):
        self.off = m

    def alloc(self, free_shape, dtype):
        esz = 4 if dtype == F32 else 2
        nel = 1
        for s_ in free_shape:
            nel *= s_
        nbytes = nel * esz
        off = (self.off + 63) // 64 * 64
        assert off + nbytes <= self.cap, ("arena overflow", off, nbytes, self.cap)
        self.off = off + nbytes
        v = self.t[:, off // 2:(off + nbytes) // 2]
        if dtype == F32:
            v = v.bitcast(F32)
        if len(free_shape) == 2:
            v = v.rearrange("p (a b) -> p a b", a=free_shape[0], b=free_shape[1])
        elif len(free_shape) == 3:
            v = v.rearrange("p (a b c) -> p a b c", a=free_shape[0], b=free_shape[1], c=free_shape[2])
        return v


def _t5_bucket(n):
    n = np.maximum(n, 0)
    nf = np.maximum(n, 1).astype(np.float32)
    large = 16 + (np.log(nf / np.float32(16)) / np.float32(math.log(8.0)) * np.float32(16)).astype(np.int32)
    large = np.minimum(large, 31)
    return np.where(n < 16, n, large)


def _host_consts():
    c = {}
    c["c_identb"] = np.eye(128, dtype=np.float32).astype(ml_dtypes.bfloat16)
    c["c_J"] = np.eye(128, dtype=np.float32)[::-1].copy()
    G = np.zeros((33, 384), np.float32)
    for i in range(383):
        d = i - 127
        if d < 0:
            G[32, i] = 1.0
        else:
            b = int(_t5_bucket(np.array([d]))[0])
            G[b, i] += 1.0
            G[31, i] -= 1.0
    c["c_G"] = G
    E = np.zeros((16, NH, S), np.float32)
    for nb in range(16):
        E[nb, :, nb * 256:(nb + 1) * 256] = 1.0
    c["c_E"] = E.reshape(16, NH * S).astype(ml_dtypes.bfloat16)
    neg1 = np.zeros((16, 16), np.float32)
    ownfix = np.full((16, 16), NEG, np.float32)
    static = np.full((16, 16), NEG, np.float32)
    for b in range(16):
        neg1[b, b:] = -1e9
        ownfix[b, b] = 0.0
        static[b, :b + 1] = 0.0
    c["c_mtab"] = np.broadcast_to(
        np.concatenate([neg1.reshape(-1), ownfix.reshape(-1), static.reshape(-1)])[None, :], (128, 768)).copy()
    return c


def build_nc(debug=False, phases=(1, 2, 3)):
    nc = bass.Bass("TRN2", target_bir_lowering=False)
    dt_in = lambda name, shape, dt=F32: nc.dram_tensor(name, list(shape), dt, kind="ExternalInput").ap()
    x_d = dt_in("x", [S, D])
    w_in_d = dt_in("w_in", [D, DIN])
    wpa_d = dt_in("w_proj_attn", [512, D])
    wpr_d = dt_in("w_proj_rnn", [D, D])
    wout_d = dt_in("w_out", [D, D])
    w1_d = dt_in("w_ff1", [D, DFF])
    w2_d = dt_in("w_ff2", [DFF, D])
    nw_d = dt_in("nw_bc", [128, 2 * D])
    nwc_d = dt_in("nw_col", [128, 16])
    qk_d = dt_in("qk_col", [64, 2])
    relb_d = dt_in("rel_bias", [32, NH])
    pvec_d = dt_in("pvec", [128, 8 * 11])
    wrga_d = dt_in("w_rg_a", [16, 64, 64])
    wrgi_d = dt_in("w_rg_i", [16, 64, 64])
    cid_d = dt_in("c_identb", [128, 128], BF16)
    cJ_d = dt_in("c_J", [128, 128])
    cG_d = dt_in("c_G", [33, 384])
    cE_d = dt_in("c_E", [16, NH * S], BF16)
    cm_d = dt_in("c_mtab", [128, 768])
    out_d = nc.dram_tensor("out", [S, D], F32, kind="ExternalOutput").ap()
    skind = "ExternalOutput" if debug else "Internal"
    scr_w = nc.dram_tensor("scr_w", [NH, 384], F32, kind="Internal").ap()
    scr_o = nc.dram_tensor("scr_o", [NCH, 128, 4 * CH], BF16, kind=skind).ap()
    scr_h = nc.dram_tensor("scr_h", [NCH, 128, 8 * CH], BF16, kind="Internal").ap()

    P = Prog()
    ARENA_BYTES = 212480
    st = contextlib.ExitStack()
    with st:
        arena_t = st.enter_context(nc.sbuf_tensor("arena", [128, ARENA_BYTES // 2], BF16))
        AR = Arena(arena_t, ARENA_BYTES)
        banks = [st.enter_context(nc.psum_tensor("bank%d" % i, [128, 512], F32)) for i in range(8)]
        bankf = [b[:, :] for b in banks]
        bankb = [b[:, :].bitcast(BF16) for b in banks]

        def _fsize(ap):
            n_ = 1
            for d_ in ap.shape[1:]:
                n_ *= d_
            return n_

        def op(eng, meth, reads, writes, cost=None, **kw):
            if cost is None:
                if eng == 'tensor':
                    nmov = _fsize(kw['rhs']) if 'rhs' in kw else 128
                    cost = max(nmov, 64) / 2.4 + 3.0
                    if 'rhs' in kw and kw['rhs'].dtype == F32:
                        cost *= 4
                else:
                    ref_ap = kw.get('out', kw.get('ap', kw.get('in_')))
                    nel = _fsize(ref_ap)
                    if eng == 'scalar':
                        cost = (nel + 175) / 1.4
                        if 'out' in kw and kw['out'].dtype == F32 and nel >= 256:
                            cost *= 1.35
                        elif nel >= 1024:
                            cost *= 1.15
                    elif eng == 'vector':
                        f_ = 1.0
                        if meth in ('tensor_tensor', 'scalar_tensor_tensor'):
                            f_ = 2.0 if nel >= 1024 else 1.2
                        if meth == 'tensor_tensor_scan':
                            f_ = 2.3
                        if meth == 'reciprocal':
                            f_ = 8.0
                        cost = (nel * f_ + 60) / 0.96
                    else:
                        f_ = 3.2 if meth == 'tensor_tensor' else 1.4
                        cost = (nel * f_ + 150) / 1.4
                        if kw.get('op', None) == ALU.pow:
                            cost = nel * 160.0 + 1500.0
            fam = None
            if eng == 'scalar' and 'func' in kw:
                fam = {AF.Exp: 'exp', AF.Tanh: 'tanh', AF.Ln: 'ln', AF.Sqrt: 'sqrt'}.get(kw['func'])
            return P.add(eng, lambda e, m=meth, kw=kw: getattr(e, m)(**kw), reads, writes, cost=cost, fam=fam)

        def dma(q, out, in_, dsem, reads, writes, nowaw=False, **kw):
            esz = 4 if in_.dtype == F32 else 2
            nb = esz * in_.shape[0] * _fsize(in_)
            return P.add(q, lambda e, kw=kw: e.dma_start(out=out, in_=in_, **kw), reads, writes, dsem=dsem,
                         cost=(1200.0 if q == 'gpsimd' else 60.0), nbytes=nb, nowaw=nowaw)

        identb = AR.alloc([128], BF16)
        nw = AR.alloc([2 * D], F32)
        pvec = AR.alloc([8, 11], F32)
        nwc = AR.alloc([16], F32)
        nsp = AR.alloc([8, 2], F32)
        hbias = AR.alloc([8, 4], F32)
        tmp8 = AR.alloc([8], F32)
        rstd1 = AR.alloc([NT], F32)
        mP = AR.mark()
        mtab = AR.alloc([3, 16, 16], F32)
        qkcol = AR.alloc([2], F32)
        qkw = AR.alloc([1], F32)
        BT = AR.alloc([NH, 256], BF16)
        dma('sync', identb, cid_d, 'c0a', [], ['identb'])
        dma('sync', nw, nw_d, 'c0b', [], ['nw'])
        dma('sync', nwc, nwc_d, 'c0f', [], ['nwc'])
        dma('sync', pvec, pvec_d.rearrange("p (t j) -> p t j", t=8, j=11), 'c0c', [], ['pvec'])
        dma('sync', mtab, cm_d.rearrange("p (a b c) -> p a b c", a=3, b=16, c=16), 'c0d', [], ['mtab'])
        dma('sync', qkcol[0:64, :], qk_d, 'c0e', [], ['qkcol'])
        op('vector', 'scalar_tensor_tensor', ['qkcol'], ['qkw'], out=qkw[0:64, :], in0=qkcol[0:64, 0:1], scalar=DH ** -0.5,
           in1=qkcol[0:64, 1:2], op0=ALU.mult, op1=ALU.mult)
        op('scalar', 'activation', ['pvec'], ['tmp8'], out=tmp8, in_=pvec[:, :, 7], func=AF.Exp, scale=-1.0)
        op('scalar', 'activation', ['tmp8'], ['tmp8'], out=tmp8, in_=tmp8, func=AF.Ln, bias=1.0)
        op('vector', 'tensor_scalar', ['tmp8'], ['nsp'], out=nsp[:, :, 0], in0=tmp8, scalar1=-4.0, scalar2=None, op0=ALU.mult)
        op('vector', 'tensor_scalar', ['tmp8'], ['nsp'], out=nsp[:, :, 1], in0=tmp8, scalar1=-8.0, scalar2=None, op0=ALU.mult)
        for jj, src in enumerate((5, 6, 8, 9)):
            op('vector', 'tensor_scalar', ['pvec'], ['hbias'], out=hbias[:, :, jj], in0=pvec[:, :, src], scalar1=0.5, scalar2=None, op0=ALU.mult)
        m0 = AR.mark()
        tabx = AR.alloc([NH], F32)
        Gs = AR.alloc([384], F32)
        Js = AR.alloc([128], F32)
        wrow = AR.alloc([384], F32)
        hk = AR.alloc([2, 256], F32)
        op('vector', 'memset', [], ['tabx'], ap=tabx[32:33, :], constant=NEG)
        dma('sync', tabx[0:32, :], relb_d, 'c1a', ['tabx'], ['tabx'])
        dma('sync', Gs[0:33, :], cG_d, 'c1b', [], ['Gs'])
        dma('sync', Js, cJ_d, 'c1c', [], ['Js'])
        op('tensor', 'matmul', ['tabx', 'Gs'], ['B0'], out=bankf[0][0:8, 0:384], lhsT=tabx[0:33, :], rhs=Gs[0:33, :], start=True, stop=True)
        op('vector', 'tensor_copy', ['B0'], ['wrow'], out=wrow[0:8, :], in_=bankf[0][0:8, 0:384])
        dma('sync', scr_w, wrow[0:8, :], 'c2', ['wrow'], ['scr_w'])
        for g in range(4):
            hank_src = bass.AP(scr_w.tensor, 2 * g * 384, [[1, 128], [384, 2], [1, 256]])
            dma('sync', hk, hank_src, 'c3', ['scr_w'], ['hk'])
            op('tensor', 'matmul', ['hk', 'Js'], ['B%d' % (1 + g)], out=bankf[1 + g], lhsT=Js, rhs=hk,
               start=True, stop=True)
            op('vector', 'tensor_copy', ['B%d' % (1 + g)], ['BT'], out=BT[:, 2 * g:2 * g + 2, :], in_=bankf[1 + g].rearrange("p (a b) -> p a b", a=2, b=256))
        assert AR.off - m0 <= 8192
        AR.reset(m0)

        tok_tiles = x_d.rearrange("(n p) d -> n p d", p=128)
        out_tiles = out_d.rearrange("(n p) d -> n p d", p=128)

        def frontend(tg, xk, xt, hb, stat, nwoff, hT, tl, rs=None, rs_key=None, compute=True, hkey=None, nw_folded=False):
            if rs is None:
                rs, rs_key = stat[:, 2:3], tg + 'stat'
            if compute:
                op('scalar', 'activation', [xk], [tg + 'hb', tg + 'stat'], out=hb, in_=xt, func=AF.Square, accum_out=stat[:, 0:1])
                op('scalar', 'activation', [tg + 'stat'], [tg + 'stat'], out=stat[:, 1:2], in_=stat[:, 0:1], func=AF.Ln, scale=1.0 / D, bias=EPS)
                op('scalar', 'activation', [tg + 'stat'], [rs_key], out=rs, in_=stat[:, 1:2], func=AF.Exp, scale=-0.5)
            if nw_folded:
                op('scalar', 'activation', [xk, rs_key], [tg + 'hb'], out=hb, in_=xt, func=AF.Identity, scale=rs)
            else:
                op('vector', 'scalar_tensor_tensor', [xk, rs_key, 'nw'], [tg + 'hb'], out=hb, in0=xt, scalar=rs,
                   in1=nw[:, nwoff:nwoff + D], op0=ALU.mult, op1=ALU.mult)
            for dt_ in range(8):
                op('tensor', 'transpose', [tg + 'hb', 'identb'], ['B0'], out=bankb[0][:, dt_ * 128:(dt_ + 1) * 128],
                   in_=hb[:, dt_ * 128:(dt_ + 1) * 128], identity=identb)
            op('scalar', 'copy', ['B0'], [(hkey or tg) + 'hT%d' % tl], out=hT[:, :, tl * 128:(tl + 1) * 128],
               in_=bankb[0].rearrange("p (a b) -> p a b", a=8, b=128))

        mA = AR.mark()
        if 1 in phases:
            oT = [AR.alloc([4, CH], BF16) for _ in range(2)]
            wqkv = AR.alloc([8, 1536], BF16)
            kT = AR.alloc([NH, S], BF16)
            Vg = AR.alloc([NT, NH, 65], BF16)
            xs = [AR.alloc([D], F32) for _ in range(2)]
            hb = AR.alloc([D], BF16)
            stat = AR.alloc([4], F32)
            hTA = [AR.alloc([8, CH], BF16) for _ in range(2)]
            sq = AR.alloc([2 * 512], F32)
            ssqk = AR.alloc([32], F32)
            qa = AR.alloc([NH, 64], BF16)
            ktm = AR.alloc([512], BF16)
            qT = [AR.alloc([NH, CH], BF16) for _ in range(2)]
            kmsum = AR.alloc([NH], F32)
            kmT = AR.alloc([NH, 16], BF16)
            gm = AR.alloc([NH, 16], F32)
            top8 = AR.alloc([NH, 8], F32)
            m01 = AR.alloc([NH, 16], F32)
            maskb = AR.alloc([NH, 16], BF16)
            NPT = 10
            PT = [AR.alloc([512], BF16) for _ in range(NPT)]
            otm = [AR.alloc([512], BF16) for _ in range(2)]
            rcp = AR.alloc([4], F32)

            w_in_v = w_in_d.rearrange("(t p) c -> p t c", p=128)
            for j in range(3):
                dma('gpsimd', wqkv[:, :, j * 512:(j + 1) * 512], w_in_v[:, :, j * 512:(j + 1) * 512], 'wqkv%d' % j, [], ['wqkv%d' % j], nowaw=True)
            dma('sync', kT[64:80, :, :], cE_d.rearrange("n (h s) -> n h s", h=NH, s=S), 'c4', [], ['kTE'])
            op('gpsimd', 'memset', [], ['Vg'], ap=Vg.rearrange("p a b c -> p (a b c)"), constant=1.0)
            op('vector', 'memset', [], ['kmT'], ap=kmT.rearrange("p a b -> p (a b)"), constant=0.0)

            def load_x(t):
                dma('sync', xs[t % 2], tok_tiles[t], 'xA%d' % (t % 2), [], ['Ax%d' % (t % 2)])

            SB = [1, 2, 3, 4, 5]
            sctr_box = [0]
            load_x(0)
            def FE(c, tls=range(TPC)):
                qTc = qT[c % 2]
                qp = 'q%d_' % (c % 2)
                hT = hTA[c % 2]
                hp = 'A%d' % (c % 2)
                for tl in tls:
                    t = c * TPC + tl
                    if t + 1 < NT:
                        load_x(t + 1)
                    frontend('A', 'Ax%d' % (t % 2), xs[t % 2], hb, stat, 0, hT, tl, rs=rstd1[:, t:t + 1], rs_key='rstd1_%d' % t, hkey=hp)
                    for j, bk in enumerate(('B1', 'B2', 'B3')):
                        for dt_ in range(8):
                            op('tensor', 'matmul', [hp + 'hT%d' % tl, 'wqkv%d' % j], [bk], out=bankf[1 + j], lhsT=hT[:, dt_, tl * 128:(tl + 1) * 128],
                               rhs=wqkv[:, dt_, j * 512:(j + 1) * 512], start=(dt_ == 0), stop=(dt_ == 7))
                    op('scalar', 'activation', ['B1'], ['sq0'], out=sq[:, 0:512], in_=bankf[1], func=AF.Square)
                    op('scalar', 'activation', ['B2'], ['sq1'], out=sq[:, 512:1024], in_=bankf[2], func=AF.Square)
                    op('vector', 'tensor_reduce', ['sq0', 'sq1'], ['ssqk'], out=ssqk[:, 0:16], in_=sq.rearrange("p (a b) -> p a b", a=16, b=64),
                       axis=AX.X, op=ALU.add)
                    op('scalar', 'activation', ['ssqk'], ['ssqk'], out=ssqk[:, 16:32], in_=ssqk[:, 0:16], func=AF.Ln, scale=1.0 / DH, bias=EPS)
                    op('scalar', 'activation', ['ssqk'], ['ssqk'], out=ssqk[:, 0:16], in_=ssqk[:, 16:32], func=AF.Exp, scale=-0.5)
                    op('vector', 'tensor_tensor', ['B1', 'ssqk'], ['qa'], out=qa, in0=bankf[1].rearrange("p (a b) -> p a b", a=8, b=64),
                       in1=ssqk[:, 0:8].unsqueeze(2).to_broadcast([128, 8, 64]), op=ALU.mult)
                    op('vector', 'tensor_tensor', ['B2', 'ssqk'], ['ktm'], out=ktm.rearrange("p (a b) -> p a b", a=8, b=64),
                       in0=bankf[2].rearrange("p (a b) -> p a b", a=8, b=64),
                       in1=ssqk[:, 8:16].unsqueeze(2).to_broadcast([128, 8, 64]), op=ALU.mult)
                    op('scalar', 'copy', ['B3', 'Vg'], ['V%d' % t], out=Vg[:, t, :, 0:64], in_=bankf[3].rearrange("p (a b) -> p a b", a=8, b=64))
                    for h in range(NH):
                        op('tensor', 'transpose', ['ktm', 'identb'], ['B4'], out=bankb[4][0:64, h * 128:(h + 1) * 128],
                           in_=ktm[:, h * 64:(h + 1) * 64], identity=identb)
                    op('scalar', 'activation', ['B4', 'qkw'], ['kT%d' % t], out=kT[0:64, :, t * 128:(t + 1) * 128],
                       in_=bankb[4][0:64, :].rearrange("p (a b) -> p a b", a=8, b=128), func=AF.Identity, scale=qkw[0:64, 0:1])
                    for h in range(NH):
                        op('tensor', 'transpose', ['qa', 'identb'], ['B5'], out=bankb[5][0:64, h * 128:(h + 1) * 128],
                           in_=qa[:, h, :], identity=identb)
                    op('vector', 'tensor_copy', ['B5'], [qp + 'qT%d' % tl], out=qTc[0:64, :, tl * 128:(tl + 1) * 128],
                       in_=bankb[5][0:64, :].rearrange("p (a b) -> p a b", a=8, b=128))
                    if tl == TPC - 1:
                        dma('sync', scr_h[c], hT.rearrange("p a b -> p (a b)"), 'sh%d' % (c % 2), [hp + 'hT%d' % i_ for i_ in range(TPC)], ['scr_h%d' % c])
                    b = t // 2
                    if t % 2 == 1 and b < NB - 1:
                        op('vector', 'tensor_reduce', ['kT%d' % (t - 1), 'kT%d' % t], ['kmsum'], out=kmsum[0:64, :],
                           in_=kT[0:64, :, b * 256:(b + 1) * 256], axis=AX.X, op=ALU.add)
                        op('vector', 'tensor_scalar', ['kmsum'], ['kmT'], out=kmT[0:64, :, b], in0=kmsum[0:64, :], scalar1=1.0 / 256,
                           scalar2=None, op0=ALU.mult)

            def GM(c):
                qTc = qT[c % 2]
                qp = 'q%d_' % (c % 2)
                for tl in range(TPC):
                    t = c * TPC + tl
                    b = t // 2
                    if b >= 3:
                        for h in range(NH):
                            op('tensor', 'matmul', [qp + 'qT%d' % tl, 'kmT'], ['B0'], out=bankf[0][:, h * 16:(h + 1) * 16],
                               lhsT=qTc[0:64, h, tl * 128:(tl + 1) * 128], rhs=kmT[0:64, h, :], start=True, stop=True)
                        op('vector', 'tensor_tensor', ['B0', 'mtab'], ['gm'], out=gm, in0=bankf[0][:, 0:128].rearrange("p (a b) -> p a b", a=8, b=16),
                           in1=mtab[:, 0, b, :].unsqueeze(1).to_broadcast([128, 8, 16]), op=ALU.add)
                        for h in range(NH):
                            op('vector', 'max', ['gm'], ['top8'], out=top8[:, h, :], in_=gm[:, h, :])
                        op('vector', 'tensor_tensor', ['gm', 'top8'], ['m01'], out=m01, in0=gm,
                           in1=top8[:, :, 2:3].to_broadcast([128, 8, 16]), op=ALU.is_lt)
                        op('vector', 'tensor_tensor', ['m01', 'mtab'], ['maskb'], out=maskb, in0=m01,
                           in1=mtab[:, 1, b, :].unsqueeze(1).to_broadcast([128, 8, 16]), op=ALU.mult)
                    else:
                        op('vector', 'tensor_copy', ['mtab'], ['maskb'], out=maskb,
                           in_=mtab[:, 2, b, :].unsqueeze(1).to_broadcast([128, 8, 16]))
                    for h in range(NH):
                        op('tensor', 'transpose', ['maskb', 'identb'], ['B0'], out=bankb[0][0:16, h * 128:(h + 1) * 128],
                           in_=maskb[:, h, :], identity=identb)
                    op('vector', 'tensor_copy', ['B0'], [qp + 'qTm%d' % tl], out=qTc[64:80, :, tl * 128:(tl + 1) * 128],
                       in_=bankb[0][0:16, :].rearrange("p (a b) -> p a b", a=8, b=128))

            def ATT(c, bl, heads=range(NH), fin=True):
                qTc = qT[c % 2]
                qp = 'q%d_' % (c % 2)
                sctr = sctr_box[0]
                if True:
                    b = 2 * c + bl
                    q0 = bl * 256
                    tq = (2 * bl, 2 * bl + 1)
                    qkeys = [qp + 'qT%d' % tq[0], qp + 'qT%d' % tq[1], qp + 'qTm%d' % tq[0], qp + 'qTm%d' % tq[1]]
                    for h in heads:
                        par = h % 2
                        ob = 6 + par
                        obk = 'B%d' % ob
                        npair = b + 1
                        for j in range(npair):
                            sb_ = SB[sctr % len(SB)]
                            pt = PT[sctr % NPT]
                            ptk = 'PT%d' % (sctr % NPT)
                            sk = 'B%d' % sb_
                            sctr += 1
                            sctr_box[0] = sctr
                            k0, k1 = 2 * j, 2 * j + 1
                            last = (j == b)
                            prev = (j == b - 1)
                            op('tensor', 'matmul', ['kT%d' % k0, 'kTE'] + qkeys, [sk], out=bankf[sb_][:, 0:256],
                               lhsT=kT[0:80, h, k0 * 128:(k0 + 1) * 128], rhs=qTc[0:80, h, q0:q0 + 256], start=True, stop=not last)
                            if last:
                                op('tensor', 'matmul', ['BT', 'identb'], [sk], out=bankf[sb_][:, 0:256], lhsT=identb, rhs=BT[:, h, 0:256],
                                   start=False, stop=True)
                                op('tensor', 'matmul', ['kT%d' % k1, 'kTE'] + qkeys, [sk], out=bankf[sb_][:, 256:384],
                                   lhsT=kT[0:80, h, k1 * 128:(k1 + 1) * 128], rhs=qTc[0:80, h, q0 + 128:q0 + 256], start=True, stop=False)
                                op('tensor', 'matmul', ['BT', 'identb'], [sk], out=bankf[sb_][:, 256:384], lhsT=identb, rhs=BT[:, h, 0:128],
                                   start=False, stop=True)
                            else:
                                op('tensor', 'matmul', ['kT%d' % k1, 'kTE'] + qkeys, [sk], out=bankf[sb_][:, 256:512],
                                   lhsT=kT[0:80, h, k1 * 128:(k1 + 1) * 128], rhs=qTc[0:80, h, q0:q0 + 256], start=True, stop=not prev)
                                if prev:
                                    op('tensor', 'matmul', ['BT', 'identb'], [sk], out=bankf[sb_][:, 256:384], lhsT=identb, rhs=BT[:, h, 128:256],
                                       start=False, stop=True)
                            if last:
                                op('scalar', 'activation', [sk], [ptk], out=pt[:, 0:384], in_=bankf[sb_][:, 0:384], func=AF.Exp)
                            else:
                                op('scalar', 'activation', [sk], [ptk], out=pt, in_=bankf[sb_], func=AF.Exp)
                            first = (j == 0)
                            op('tensor', 'matmul', [ptk, 'V%d' % k0], [obk], out=bankf[ob][:, 0:65], lhsT=pt[:, 0:128],
                               rhs=Vg[:, k0, h, :], start=first, stop=last, skip_group_check=True)
                            op('tensor', 'matmul', [ptk, 'V%d' % k0], [obk], out=bankf[ob][:, 128:193], lhsT=pt[:, 128:256],
                               rhs=Vg[:, k0, h, :], start=False, stop=False, skip_group_check=True)
                            if not last:
                                op('tensor', 'matmul', [ptk, 'V%d' % k1], [obk], out=bankf[ob][:, 0:65], lhsT=pt[:, 256:384],
                                   rhs=Vg[:, k1, h, :], start=False, stop=False, skip_group_check=True)
                            op('tensor', 'matmul', [ptk, 'V%d' % k1], [obk], out=bankf[ob][:, 128:193], lhsT=(pt[:, 256:384] if last else pt[:, 384:512]),
                               rhs=Vg[:, k1, h, :], start=False, stop=last, skip_group_check=True)
                        for qi in range(2):
                            oc = qi * 128
                            op('vector', 'reciprocal', [obk], ['rcp%d' % qi], out=rcp[:, qi:qi + 1], in_=bankf[ob][:, oc + 64:oc + 65])
                            op('vector', 'tensor_scalar', [obk, 'rcp%d' % qi], ['otm%d_%d' % (qi, h)], out=otm[qi][:, h * 64:(h + 1) * 64],
                               in0=bankf[ob][:, oc:oc + 64], scalar1=rcp[:, qi:qi + 1], scalar2=None, op0=ALU.mult)
                    for qi in (range(2) if fin else ()):
                        okeys = ['otm%d_%d' % (qi, h) for h in range(NH)]
                        for g in range(4):
                            op('tensor', 'transpose', okeys + ['identb'], ['B3'], out=bankb[3][:, g * 128:(g + 1) * 128],
                               in_=otm[qi][:, g * 128:(g + 1) * 128], identity=identb)
                        col = q0 + qi * 128
                        op('vector', 'tensor_copy', ['B3'], ['oT%d' % (c % 2)], out=oT[c % 2][:, :, col:col + 128],
                           in_=bankb[3][:, 0:512].rearrange("p (a b) -> p a b", a=4, b=128))

            FE(0)
            GM(0)
            FINE = False
            for c in range(NCH):
                if FINE and c + 1 < NCH:
                    ATT(c, 0, range(0, 4), False)
                    FE(c + 1, [0])
                    ATT(c, 0, range(4, 8), True)
                    FE(c + 1, [1])
                    ATT(c, 1, range(0, 4), False)
                    FE(c + 1, [2])
                    ATT(c, 1, range(4, 8), True)
                    FE(c + 1, [3])
                    GM(c + 1)
                else:
                    ORD = 'A3'
                    nx = c + 1 < NCH
                    if ORD == 'A0':
                        ATT(c, 0)
                        if nx:
                            FE(c + 1)
                            GM(c + 1)
                        ATT(c, 1)
                    elif ORD == 'A1':
                        ATT(c, 0)
                        if nx:
                            FE(c + 1)
                        ATT(c, 1)
                        if nx:
                            GM(c + 1)
                    elif ORD == 'A2':
                        if nx:
                            FE(c + 1)
                            GM(c + 1)
                        ATT(c, 0)
                        ATT(c, 1)
                    elif ORD == 'A3':
                        ATT(c, 0, range(0, 4), False)
                        if nx:
                            FE(c + 1)
                            GM(c + 1)
                        ATT(c, 0, range(4, 8), True)
                        ATT(c, 1)
                    elif ORD.startswith('K'):
                        kk = int(ORD[1:])
                        ATT(c, 0, range(0, kk), False)
                        if nx:
                            FE(c + 1)
                            GM(c + 1)
                        ATT(c, 0, range(kk, 8), True)
                        ATT(c, 1)
                    elif ORD.startswith('P'):
                        pa, pb, pg = int(ORD[1:3]), int(ORD[3:5]), int(ORD[5:7])
                        cuts = sorted(set([0, pa, pb, pg, 8, 16]))
                        for i_ in range(len(cuts) - 1):
                            lo_, hi_ = cuts[i_], cuts[i_ + 1]
                            if lo_ == pa and nx:
                                FE(c + 1, [0, 1])
                            if lo_ == pb and nx:
                                FE(c + 1, [2, 3])
                            if lo_ == pg and nx:
                                GM(c + 1)
                            if hi_ > lo_:
                                bl_ = 0 if lo_ < 8 else 1
                                ATT(c, bl_, range(lo_ - 8 * bl_, hi_ - 8 * bl_), (hi_ % 8 == 0))
                    elif ORD == 'A4':
                        ATT(c, 0)
                        if nx:
                            FE(c + 1, [0, 1])
                        ATT(c, 1, range(0, 4), False)
                        if nx:
                            FE(c + 1, [2, 3])
                            GM(c + 1)
                        ATT(c, 1, range(4, 8), True)
                dma('sync', scr_o[c], oT[c % 2].rearrange("p a b -> p (a b)"), 'so%d' % (c % 2), ['oT%d' % (c % 2)], ['scr_o%d' % c])
        AR.reset(mA)

        if 2 in phases:
            P.barrier()
            AR.reset(mP)
            wr = AR.alloc([8, 4096], BF16)
            wpa = AR.alloc([4, D], BF16)
            wpr = AR.alloc([8, D], BF16)
            wo = AR.alloc([8, D], BF16)
            wbda = AR.alloc([8, 128], BF16)
            wbdi = AR.alloc([8, 128], BF16)
            xres = [AR.alloc([D], F32) for _ in range(3)]
            hTs = [AR.alloc([8, CH], BF16) for _ in range(2)]
            halo = AR.alloc([8, 3], F32)
            hstate = AR.alloc([8], F32)
            NR = 2
            xr = [AR.alloc([CH + 3], F32) for _ in range(NR)]
            cv = [AR.alloc([CH], F32) for _ in range(NR)]
            cvb = [AR.alloc([CH], BF16) for _ in range(NR)]
            rr = [AR.alloc([CH], F32) for _ in range(NR)]
            ii = [AR.alloc([CH], F32) for _ in range(NR)]
            aa = [AR.alloc([CH], F32) for _ in range(NR)]
            gq = [AR.alloc([CH], F32) for _ in range(NR)]
            ornTs = [AR.alloc([8, CH], BF16) for _ in range(2)]
            oaTs = [AR.alloc([4, CH], BF16) for _ in range(2)]
            t1 = [AR.alloc([CH], F32) for _ in range(1)]
            t2 = [AR.alloc([CH], F32) for _ in range(1)]
            mT = AR.alloc([8, CH], BF16)

            op('vector', 'memset', [], ['wbda'], ap=wbda.rearrange("p a b -> p (a b)"), constant=0.0)
            op('vector', 'memset', [], ['wbdi'], ap=wbdi.rearrange("p a b -> p (a b)"), constant=0.0)
            op('vector', 'memset', [], ['halo'], ap=halo.rearrange("p a b -> p (a b)"), constant=0.0)
            op('vector', 'memset', [], ['hstate'], ap=hstate, constant=0.0)
            w_in_v = w_in_d.rearrange("(t p) c -> p t c", p=128)
            for g in range(8):
                if g == 4:
                    for wsrc, wdst, nm in ((wrga_d, wbda, 'wbda'), (wrgi_d, wbdi, 'wbdi')):
                        v = wsrc.rearrange("(t two) d e -> two d t e", two=2)
                        dma('gpsimd', wdst[0:64, :, 0:64], v[0], nm, [nm], [nm])
                        dma('gpsimd', wdst[64:128, :, 64:128], v[1], nm, [nm], [nm + 'x'])
                    dma('gpsimd', wpa, wpa_d.rearrange("(t p) c -> p t c", p=128), 'wpa', [], ['wpa'], nowaw=True)
                    dma('gpsimd', wpr, wpr_d.rearrange("(t p) c -> p t c", p=128), 'wpr', [], ['wpr'], nowaw=True)
                dma('gpsimd', wr[:, :, g * 512:(g + 1) * 512], w_in_v[:, :, 1536 + g * 512:1536 + (g + 1) * 512], 'wr%d' % g, [], ['wr%d' % g], nowaw=True)
            dma('gpsimd', wo, wout_d.rearrange("(t p) c -> p t c", p=128), 'wo', [], ['wo'], nowaw=True)

            def FEB(c):
                dma('sync', hTs[c % 2].rearrange("p a b -> p (a b)"), scr_h[c], 'hB%d' % (c % 2), ['scr_h%d' % c], ['BhT%d' % (c % 2)])

            def rnnA(c, ct, hT, hTk):
                r_ = ct % NR
                sfx = '%d' % r_
                bx, by = (1, 2) if ct % 2 == 0 else (3, 4)
                for dt_ in range(8):
                    op('tensor', 'matmul', hTk + ['wr%d' % (ct // 4)], ['B%d' % bx], out=bankf[bx], lhsT=wr[:, dt_, ct * 128:(ct + 1) * 128], rhs=hT[:, dt_, :],
                       start=(dt_ == 0), stop=(dt_ == 7))
                for dt_ in range(8):
                    op('tensor', 'matmul', hTk + ['wr%d' % (2 + ct // 4)], ['B%d' % by], out=bankf[by], lhsT=wr[:, dt_, 1024 + ct * 128:1024 + (ct + 1) * 128],
                       rhs=hT[:, dt_, :], start=(dt_ == 0), stop=(dt_ == 7))
                op('vector', 'tensor_copy', ['halo'], ['xrh' + sfx], out=xr[r_][:, 0:3], in_=halo[:, ct, :])
                op('scalar', 'copy', ['B%d' % bx], ['xr' + sfx], out=xr[r_][:, 3:CH + 3], in_=bankf[bx])
                op('scalar', 'activation', ['xr' + sfx, 'pvec'], ['cv' + sfx], out=cv[r_], in_=xr[r_][:, 3:CH + 3], func=AF.Identity, scale=pvec[:, ct, 3:4],
                   bias=pvec[:, ct, 4:5])
                for j in range(3):
                    op('vector', 'scalar_tensor_tensor', ['xr' + sfx, 'xrh' + sfx, 'pvec', 'cv' + sfx], ['cv' + sfx], out=cv[r_], in0=xr[r_][:, j:j + CH],
                       scalar=pvec[:, ct, j:j + 1], in1=cv[r_], op0=ALU.mult, op1=ALU.add)
                op('vector', 'tensor_copy', ['xr' + sfx], ['halo'], out=halo[:, ct, :], in_=xr[r_][:, CH:CH + 3])
                op('scalar', 'copy', ['cv' + sfx], ['cvb' + sfx], out=cvb[r_], in_=cv[r_])
                op('tensor', 'matmul', ['cvb' + sfx, 'wbda', 'wbdax'], ['B5'], out=bankf[5], lhsT=wbda[:, ct, :], rhs=cvb[r_], start=True, stop=True)
                op('tensor', 'matmul', ['cvb' + sfx, 'wbdi', 'wbdix'], ['B6'], out=bankf[6], lhsT=wbdi[:, ct, :], rhs=cvb[r_], start=True, stop=True)
                op('scalar', 'activation', ['B5', 'hbias'], ['rr' + sfx], out=rr[r_], in_=bankf[5], func=AF.Tanh, scale=0.5, bias=hbias[:, ct, 0:1])
                op('scalar', 'activation', ['B6', 'hbias'], ['ii' + sfx], out=ii[r_], in_=bankf[6], func=AF.Tanh, scale=0.5, bias=hbias[:, ct, 1:2])
                op('scalar', 'activation', ['rr' + sfx, 'nsp'], ['aa' + sfx], out=aa[r_], in_=rr[r_], func=AF.Exp, scale=nsp[:, ct, 0:1], bias=nsp[:, ct, 0:1])
                op('scalar', 'activation', ['rr' + sfx, 'nsp'], ['rr' + sfx], out=rr[r_], in_=rr[r_], func=AF.Exp, scale=nsp[:, ct, 1:2], bias=nsp[:, ct, 1:2])
                op('scalar', 'activation', ['B%d' % by], ['gq' + sfx], out=gq[r_], in_=bankf[by], func=AF.Square, scale=math.sqrt(0.044715))
                op('vector', 'scalar_tensor_tensor', ['gq' + sfx, 'B%d' % by], ['gq' + sfx], out=gq[r_], in0=gq[r_], scalar=1.0, in1=bankf[by],
                   op0=ALU.add, op1=ALU.mult)
                op('scalar', 'activation', ['gq' + sfx], ['gq' + sfx], out=gq[r_], in_=gq[r_], func=AF.Tanh, scale=math.sqrt(2.0 / math.pi))
                op('vector', 'scalar_tensor_tensor', ['gq' + sfx, 'B%d' % by], ['gq' + sfx], out=gq[r_], in0=gq[r_], scalar=1.0, in1=bankf[by],
                   op0=ALU.add, op1=ALU.mult)

            def rnnB(c, ct):
                r_ = ct % NR
                sfx = '%d' % r_
                op('gpsimd', 'tensor_scalar', ['rr' + sfx], ['rr' + sfx], out=rr[r_], in0=rr[r_], scalar1=1.0, scalar2=0.0, op0=ALU.min, op1=ALU.max)
                op('scalar', 'activation', ['rr' + sfx], ['rr' + sfx], out=rr[r_], in_=rr[r_], func=AF.Sqrt, scale=-0.0625, bias=0.0625)

            def rnnC(c, ct):
                r_ = ct % NR
                sfx = '%d' % r_
                hh_ = xr[r_][:, 0:CH]
                op('vector', 'scalar_tensor_tensor', ['ii' + sfx, 'rr' + sfx], ['ii' + sfx], out=ii[r_], in0=ii[r_], scalar=1.0, in1=rr[r_], op0=ALU.add, op1=ALU.mult)
                op('vector', 'tensor_tensor', ['ii' + sfx, 'cv' + sfx], ['cv' + sfx], out=cv[r_], in0=ii[r_], in1=cv[r_], op=ALU.mult)
                op('vector', 'tensor_tensor_scan', ['aa' + sfx, 'cv' + sfx, 'hstate'], ['xr' + sfx, 'xrh' + sfx], out=hh_, data0=aa[r_], data1=cv[r_],
                   initial=hstate[:, ct:ct + 1], op0=ALU.mult, op1=ALU.add)
                op('vector', 'tensor_copy', ['xr' + sfx], ['hstate'], out=hstate[:, ct:ct + 1], in_=xr[r_][:, CH - 1:CH])
                op('gpsimd', 'tensor_tensor', ['xr' + sfx, 'xrh' + sfx, 'gq' + sfx], ['ornT%d_%d' % (c % 2, ct)], out=ornTs[c % 2][:, ct, :], in0=hh_,
                   in1=gq[r_], op=ALU.mult)

            def gating(c, m, hT, hTk):
                b1, b2, b3, b4 = (1, 2, 3, 4) if m % 2 == 0 else (5, 6, 7, 0)
                for dt_ in range(8):
                    op('tensor', 'matmul', hTk + ['wr%d' % (4 + m // 4)], ['B%d' % b1], out=bankf[b1], lhsT=wr[:, dt_, 2048 + m * 128:2048 + (m + 1) * 128],
                       rhs=hT[:, dt_, :], start=(dt_ == 0), stop=(dt_ == 7))
                for dt_ in range(8):
                    op('tensor', 'matmul', hTk + ['wr%d' % (6 + m // 4)], ['B%d' % b2], out=bankf[b2], lhsT=wr[:, dt_, 3072 + m * 128:3072 + (m + 1) * 128],
                       rhs=hT[:, dt_, :], start=(dt_ == 0), stop=(dt_ == 7))
                for kt in range(4):
                    op('tensor', 'matmul', ['oaT%d' % (c % 2), 'wpa'], ['B%d' % b3], out=bankf[b3], lhsT=wpa[:, kt, m * 128:(m + 1) * 128], rhs=oaTs[c % 2][:, kt, :],
                       start=(kt == 0), stop=(kt == 3))
                for kt in range(8):
                    op('tensor', 'matmul', ['ornT%d_%d' % (c % 2, i_) for i_ in range(8)] + ['wpr'], ['B%d' % b4], out=bankf[b4], lhsT=wpr[:, kt, m * 128:(m + 1) * 128],
                       rhs=ornTs[c % 2][:, kt, :], start=(kt == 0), stop=(kt == 7))
                op('scalar', 'activation', ['B%d' % b1, 'hbias'], ['t1'], out=t1[0], in_=bankf[b1], func=AF.Tanh, scale=0.5, bias=hbias[:, m, 2:3])
                op('scalar', 'activation', ['B%d' % b2, 'hbias'], ['t2'], out=t2[0], in_=bankf[b2], func=AF.Tanh, scale=0.5, bias=hbias[:, m, 3:4])
                op('vector', 'scalar_tensor_tensor', ['t1', 'B%d' % b3], ['t1'], out=t1[0], in0=t1[0], scalar=1.0, in1=bankf[b3], op0=ALU.add, op1=ALU.mult)
                op('vector', 'scalar_tensor_tensor', ['t2', 'B%d' % b4], ['t2'], out=t2[0], in0=t2[0], scalar=1.0, in1=bankf[b4], op0=ALU.add, op1=ALU.mult)
                op('vector' if m % 2 else 'gpsimd', 'tensor_tensor', ['t1', 't2'], ['mT%d' % m], out=mT[:, m, :], in0=t1[0], in1=t2[0], op=ALU.add)

            mTk = ['mT%d' % i for i in range(8)]
            def rnn_pair(c, p2):
                hT = hTs[c % 2]
                hTk = ['BhT%d' % (c % 2)]
                for ct in (2 * p2, 2 * p2 + 1):
                    rnnA(c, ct, hT, hTk)
                for ct in (2 * p2, 2 * p2 + 1):
                    rnnB(c, ct)
                for ct in (2 * p2, 2 * p2 + 1):
                    rnnC(c, ct)

            def load_res(t):
                dma('sync', xres[t % 3], tok_tiles[t], 'xr%d' % (t % 3), ['out_s%d' % (t % 3)], ['Bxres%d' % (t % 3)])

            def load_oa(c):
                dma('sync', oaTs[c % 2].rearrange("p a b -> p (a b)"), scr_o[c], 'oa%d' % (c % 2), ['scr_o%d' % c], ['oaT%d' % (c % 2)])

            load_oa(0)
            FEB(0)
            if NCH > 1:
                FEB(1)
            for t_ in range(3):
                load_res(t_)
            for p2 in range(4):
                rnn_pair(0, p2)
            for c in range(NCH):
                hT = hTs[c % 2]
                hTk = ['BhT%d' % (c % 2)]
                if c + 1 < NCH:
                    load_oa(c + 1)
                BO = '0'
                for m in range(8):
                    if BO == '1' and m % 2 == 0 and c + 1 < NCH:
                        rnn_pair(c + 1, m // 2)
                    gating(c, m, hT, hTk)
                    if BO == '0' and m % 2 == 1 and c + 1 < NCH:
                        rnn_pair(c + 1, m // 2)
                    if BO == '2' and m < 4 and c + 1 < NCH:
                        rnn_pair(c + 1, m)
                if c + 2 < NCH:
                    FEB(c + 2)
                for tl in range(TPC):
                    t = c * TPC + tl
                    sl = t % 3
                    for hf in range(2):
                        bk = (1, 2, 3, 4)[(2 * tl + hf) % 4]
                        for kt in range(8):
                            op('tensor', 'matmul', mTk + ['wo'], ['B%d' % bk], out=bankf[bk], lhsT=mT[:, kt, tl * 128:(tl + 1) * 128],
                               rhs=wo[:, kt, hf * 512:(hf + 1) * 512], start=(kt == 0), stop=(kt == 7))
                        op('vector', 'scalar_tensor_tensor', ['B%d' % bk, 'Bxres%d' % sl], ['Bxres%d' % sl], out=xres[sl][:, hf * 512:(hf + 1) * 512],
                           in0=bankf[bk], scalar=0.5, in1=xres[sl][:, hf * 512:(hf + 1) * 512], op0=ALU.mult, op1=ALU.add)
                    dma('sync', out_tiles[t], xres[sl], 'os%d' % sl, ['Bxres%d' % sl], ['out%d' % t, 'out_s%d' % sl])
                    if t + 3 < NT:
                        load_res(t + 3)

        if 3 in phases:
            P.barrier()
            AR.reset(mP)
            W1 = AR.alloc([8, DFF], BF16)
            W2 = AR.alloc([32, D], BF16)
            xs = [AR.alloc([D], F32) for _ in range(2)]
            xres = [AR.alloc([D], F32) for _ in range(2)]
            hb = AR.alloc([D], BF16)
            stat = AR.alloc([4], F32)
            hTs = [AR.alloc([8, CH], BF16) for _ in range(2)]
            uT = AR.alloc([32, CH], BF16)
            rl = [AR.alloc([CH], F32) for _ in range(2)]
            w1_v = w1_d.rearrange("(t p) c -> p t c", p=128)
            w2_v = w2_d.rearrange("(t p) c -> p t c", p=128)
            for g in range(8):
                dma('gpsimd', W1[:, :, g * 512:(g + 1) * 512], w1_v[:, :, g * 512:(g + 1) * 512], 'w1_%d' % g, [], ['w1_%d' % g], nowaw=True)
            for g in range(8):
                dma('gpsimd', W2[:, 4 * g:4 * g + 4, :], w2_v[:, 4 * g:4 * g + 4, :], 'w2_%d' % g, [], ['w2_%d' % g], nowaw=True)

            def load_xC(t):
                dma('sync', xs[t % 2], out_tiles[t], 'xC%d' % (t % 2), ['out%d' % t], ['Cx%d' % (t % 2)])

            def FEC(c):
                for tl in range(TPC):
                    t = c * TPC + tl
                    load_xC(t)
                    frontend('C', 'Cx%d' % (t % 2), xs[t % 2], hb, stat, D, hTs[c % 2], tl, hkey='C%d' % (c % 2))

            rctr = 0
            fctr = 0
            octr = 0
            FPOS = 'f31'
            FEC(0)
            for c in range(NCH):
                hT = hTs[c % 2]
                hTk = ['C%dhT%d' % (c % 2, i) for i in range(TPC)]
                for f in range(32):
                    bk = 1 + fctr % 4
                    rb = fctr % 2
                    fctr += 1
                    for dt_ in range(8):
                        op('tensor', 'matmul', hTk + ['w1_%d' % (f // 4)], ['B%d' % bk], out=bankf[bk], lhsT=W1[:, dt_, f * 128:(f + 1) * 128], rhs=hT[:, dt_, :],
                           start=(dt_ == 0), stop=(dt_ == 7))
                    op('scalar', 'activation', ['B%d' % bk], ['rl%d' % rb], out=rl[rb], in_=bankf[bk], func=AF.Relu)
                    op('gpsimd' if f % 3 == 2 else 'vector', 'tensor_tensor', ['rl%d' % rb], ['uT%d' % f], out=uT[:, f, :], in0=rl[rb], in1=rl[rb], op=ALU.mult)
                    if FPOS == 'f%d' % f and c + 1 < NCH:
                        FEC(c + 1)
                uTk = ['uT%d' % i for i in range(32)]
                for tl in range(TPC):
                    t = c * TPC + tl
                    sl = rctr % 2
                    rctr += 1
                    dma('sync', xres[sl], out_tiles[t], 'xq%d' % sl, ['out%d' % t, 'fin_s%d' % sl], ['Cxres%d' % sl])
                    for hf in range(2):
                        bk = 5 + octr % 3
                        octr += 1
                        for f in range(32):
                            op('tensor', 'matmul', uTk + ['w2_%d' % (f // 4)], ['B%d' % bk], out=bankf[bk], lhsT=uT[:, f, tl * 128:(tl + 1) * 128],
                               rhs=W2[:, f, hf * 512:(hf + 1) * 512], start=(f == 0), stop=(f == 31))
                        op('vector', 'tensor_tensor', ['B%d' % bk, 'Cxres%d' % sl], ['Cxres%d' % sl], out=xres[sl][:, hf * 512:(hf + 1) * 512], in0=bankf[bk],
                           in1=xres[sl][:, hf * 512:(hf + 1) * 512], op=ALU.add)
                    dma('sync', out_tiles[t], xres[sl], 'fs%d' % sl, ['Cxres%d' % sl], ['out%d' % t, 'fin_s%d' % sl])
                    if FPOS == 't%d' % tl and c + 1 < NCH:
                        FEC(c + 1)

        fk = ['scr_o%d' % c for c in range(NCH)] if 1 in phases else []
        P.emit(nc, final_wait_keys=fk + ['out%d' % t for t in range(NT)])
        if P.dbg is not None:
            for kk, v in sorted(P.dbg.items(), key=lambda t: -t[1])[:40]:
                print('GAP', kk, round(v / 1e3, 1))
        build_nc.stats = (len(P.ops), P.n_sems, P.n_waits, P.sim_ns, getattr(P, 'seg_end', None), getattr(P, 'seg_busy', None))
    return nc


def _prep_inputs(inputs):
    f = lambda a: np.ascontiguousarray(np.asarray(a, dtype=np.float32))
    shared = {}
    shared["w_in"] = f(inputs["w_in"][0])
    shared["w_proj_attn"] = f(inputs["w_proj_attn"][0])
    shared["w_proj_rnn"] = f(inputs["w_proj_rnn"][0])
    shared["w_out"] = f(inputs["w_out"][0])
    shared["w_ff1"] = f(inputs["w_ff1"][0])
    shared["w_ff2"] = f(inputs["w_ff2"][0])
    nwcat = np.concatenate([f(inputs["norm1_w"][0]), f(inputs["norm2_w"][0])])
    shared["nw_bc"] = np.ascontiguousarray(np.broadcast_to(nwcat[None, :], (128, 2 * D)))
    shared["nw_col"] = np.ascontiguousarray(nwcat.reshape(2, 8, 128).transpose(2, 0, 1).reshape(128, 16))
    shared["qk_col"] = np.ascontiguousarray(np.stack([f(inputs["q_norm_w"][0]), f(inputs["k_norm_w"][0])], axis=1))
    shared["rel_bias"] = f(inputs["rel_bias"])
    vecs = [f(inputs["conv_w"][0][j]) for j in range(4)] + [f(inputs["conv_b"][0]), f(inputs["b_rg_a"][0]),
            f(inputs["b_rg_i"][0]), f(inputs["lru_lambda"][0]), f(inputs["b_gate"][0][0]), f(inputs["b_gate"][0][1])]
    vecs.append(np.zeros(D, np.float32))
    pv = np.stack(vecs, axis=1)
    pv = pv.reshape(8, 128, 11).transpose(1, 0, 2)
    shared["pvec"] = np.ascontiguousarray(pv.reshape(128, 88))
    shared["w_rg_a"] = f(inputs["w_rg_a"][0])
    shared["w_rg_i"] = f(inputs["w_rg_i"][0])
    shared.update(_host_consts())
    return shared


def kernel(**inputs):
    x = np.asarray(inputs["x"], dtype=np.float32)
    shared = _prep_inputs(inputs)
    nc = build_nc()
    in_maps = []
    for i in range(NCORES):
        m = dict(shared)
        m["x"] = np.ascontiguousarray(x[i])
        in_maps.append(m)
    res = run_bass_kernel_spmd(nc, in_maps, core_ids=list(range(NCORES)))
    return np.stack([np.asarray(r["out"], dtype=np.float32) for r in res.results], axis=0)
```

```python
import os
import math
import contextlib
import numpy as np
import ml_dtypes
import concourse.bass as bass
import concourse.mybir as mybir
from concourse.bass_utils import run_bass_kernel_spmd

F32 = mybir.dt.float32
BF16 = mybir.dt.bfloat16
AF = mybir.ActivationFunctionType
ALU = mybir.AluOpType
AX = mybir.AxisListType

S, D, NH, DH = 4096, 1024, 8, 64
DIN, DFF = 5632, 4096
NT, CH, NCH, TPC = 32, 512, 8, 4
NB = 16
EPS = 1e-6
NEG = -30000.0
NCORES = 8


class Prog:
    ENGS = ('sync', 'scalar', 'vector', 'gpsimd', 'tensor')
    LAT = 200.0

    def __init__(self):
        self.ops = []
        self.barriers = []
        self.nowaw = set()
        self.fam = []
        self.dbg = None

    def add(self, eng, fn, reads=(), writes=(), dsem=None, cost=100.0, nbytes=0, nowaw=False, fam=None):
        self.ops.append((eng, fn, tuple(reads), tuple(writes), dsem, float(cost), nbytes))
        self.fam.append(fam)
        if nowaw:
            self.nowaw.add(len(self.ops) - 1)
        return len(self.ops) - 1

    def barrier(self):
        self.barriers.append(len(self.ops))

    def _deps(self):
        ops = self.ops
        n = len(ops)
        last_w = {}
        readers = {}
        deps = [None] * n
        for i, (eng, fn, reads, writes, dsem, cost, nb) in enumerate(ops):
            d = set()
            for k in reads:
                j = last_w.get(k)
                if j is not None:
                    d.add(j)
                if len(k) == 2 and k[0] == 'B':
                    for r in readers.get(k, ()):
                        if ops[r][0] != eng:
                            d.add(r)
            if i not in self.nowaw:
                for k in writes:
                    j = last_w.get(k)
                    if j is not None:
                        d.add(j)
                    for r in readers.get(k, ()):
                        d.add(r)
            for k in reads:
                readers.setdefault(k, []).append(i)
            for k in writes:
                last_w[k] = i
                if i not in self.nowaw:
                    readers[k] = []
            d.discard(i)
            deps[i] = d
        return deps, last_w

    def _schedule(self, deps, window):
        ops = self.ops
        n = len(ops)
        bounds = [0] + list(self.barriers) + [n]
        order = {e: [] for e in self.ENGS}
        seg_orders = []
        finish = [0.0] * n
        free = {e: 0.0 for e in self.ENGS}
        dma_pipe = 0.0
        done = [False] * n
        LAT = self.LAT
        act_set = ('exp', 'tanh')
        for si in range(len(bounds) - 1):
            lo, hi = bounds[si], bounds[si + 1]
            pend = {e: [] for e in self.ENGS}
            for i in range(lo, hi):
                pend[ops[i][0]].append(i)
            if lo > 0:
                tf = max(finish[:lo])
                for e in self.ENGS:
                    free[e] = max(free[e], tf)
                dma_pipe = max(dma_pipe, tf)
            seg = {e: [] for e in self.ENGS}
            remaining = hi - lo
            while remaining:
                best = None
                for e in self.ENGS:
                    lst = pend[e]
                    if not lst:
                        continue
                    w = 1 if e == 'sync' else window
                    fe = free[e]
                    for pos in range(min(w, len(lst))):
                        i = lst[pos]
                        isdma = ops[i][4] is not None
                        if isdma and pos > 0:
                            break
                        st = fe
                        ready = True
                        for j in deps[i]:
                            if not done[j]:
                                ready = False
                                break
                            if ops[j][0] == e and ops[j][4] is None:
                                t = finish[j] + (0.0 if e == 'tensor' else 130.0)
                            else:
                                t = finish[j] + LAT
                            if t > st:
                                st = t
                        if ready:
                            if best is None or st < best[0] - 1e-6:
                                best = (st, e, pos, i)
                            if st <= fe + 1e-6:
                                break
                        if isdma:
                            break
                st, e, pos, i = best
                eng, fn, reads, writes, dsem, cost, nb = ops[i]
                if self.dbg is not None and st > free[e] + 1.0:
                    jb = max(deps[i], key=lambda j: finish[j] + (0.0 if ops[j][0] == e and ops[j][4] is None else LAT))
                    import re as _re
                    kk = (si, e, ops[jb][0], _re.sub(r'[0-9]+', '#', (ops[jb][3] or ('?',))[0]), _re.sub(r'[0-9]+', '#', (writes or ('?',))[0]))
                    self.dbg[kk] = self.dbg.get(kk, 0.0) + (st - free[e])
                if dsem is not None:
                    free[e] = st + cost
                    xfer = nb / 270.0
                    t0 = max(st + cost, dma_pipe)
                    dma_pipe = t0 + xfer
                    finish[i] = t0 + xfer + 2000.0
                else:
                    fam = self.fam[i]
                    if fam is not None and fam not in act_set:
                        cost += 1283.0
                        act_set = {'exp': act_set if 'exp' in act_set else ('exp', 'tanh'), 'tanh': ('exp', 'tanh'),
                                   'ln': ('exp', 'ln'), 'sqrt': ('sqrt',)}[fam]
                    finish[i] = st + cost
                    free[e] = finish[i]
                done[i] = True
                pend[e].pop(pos)
                seg[e].append(i)
                remaining -= 1
            seg_orders.append(seg)
            self.seg_busy = getattr(self, 'seg_busy', []) + [{e: round(sum(ops[i][5] for i in seg[e]) / 1e3) for e in self.ENGS}]
            self.seg_end = getattr(self, 'seg_end', []) + [max(finish[lo:hi])]
            for e in self.ENGS:
                order[e].extend(seg[e])
        self.sim_ns = max(finish) if n else 0.0
        return order, seg_orders

    def emit(self, nc, final_wait_keys=(), window=256):
        ops = self.ops
        n = len(ops)
        deps, last_w = self._deps()
        order, seg_orders = self._schedule(deps, window)
        final_deps = set(last_w[k] for k in final_wait_keys if k in last_w)
        lasts = {}
        dl = {}
        for si, seg in enumerate(seg_orders):
            if si > 0:
                for e in self.ENGS:
                    if seg[e]:
                        i = seg[e][0]
                        for j in list(lasts.values()) + list(dl.values()):
                            deps[i].add(j)
            for e in self.ENGS:
                for i in seg[e]:
                    if ops[i][4] is None:
                        lasts[e] = i
            for e in self.ENGS:
                for i in seg[e]:
                    if ops[i][4] is not None:
                        dl[ops[i][4]] = i

        def skip(j, i):
            return (ops[j][0] == 'tensor' and ops[i][0] == 'tensor'
                    and ops[j][4] is None and ops[i][4] is None)

        has_dep = [False] * n
        for i in range(n):
            for j in deps[i]:
                if not skip(j, i):
                    has_dep[j] = True
        for j in final_deps:
            has_dep[j] = True
        sig = [None] * n
        dq = {}
        for e in self.ENGS:
            c = 0
            for i in order[e]:
                dsem = ops[i][4]
                if dsem is not None:
                    assert dq.setdefault(dsem, e) == e, "a DMA semaphore must stay on one queue"
                    sig[i] = ('d:' + dsem, None)
                elif has_dep[i]:
                    c += 1
                    sig[i] = ('e:' + e, c)
        dcount = {}
        for e in self.ENGS:
            for i in order[e]:
                dsem = ops[i][4]
                if dsem is not None:
                    dcount[dsem] = dcount.get(dsem, 0) + 16
                    sig[i] = ('d:' + dsem, dcount[dsem])
        sem_names = sorted(set(s[0] for s in sig if s is not None))
        self.n_sems = len(sem_names)
        stack = contextlib.ExitStack()
        sems = {}
        for sname in sem_names:
            sems[sname] = stack.enter_context(nc.semaphore(sname.replace(':', '_')))
        final_waits = {}
        for j in final_deps:
            s = sig[j]
            final_waits[s[0]] = max(final_waits.get(s[0], 0), s[1])
        self.n_waits = 0

        def run_engine(ename, handle):
            waited = {}
            for i in order[ename]:
                eng, fn, reads, writes, dsem, cost, nb = ops[i]
                need = {}
                for j in deps[i]:
                    if skip(j, i):
                        continue
                    s = sig[j]
                    if s[1] > need.get(s[0], 0):
                        need[s[0]] = s[1]
                for sname, val in need.items():
                    if waited.get(sname, 0) >= val:
                        continue
                    handle.wait_ge(sems[sname], val)
                    self.n_waits += 1
                    waited[sname] = val
                ins = fn(handle)
                if sig[i] is not None:
                    ins.then_inc(sems[sig[i][0]], 16 if dsem is not None else 1)
            if ename == 'sync':
                for sname, val in final_waits.items():
                    handle.wait_ge(sems[sname], val)

        with stack:
            with nc.Block() as block:
                @block.sync
                def _(e):
                    run_engine('sync', e)

                @block.scalar
                def _(e):
                    run_engine('scalar', e)

                @block.vector
                def _(e):
                    run_engine('vector', e)

                @block.gpsimd
                def _(e):
                    run_engine('gpsimd', e)

                @block.tensor
                def _(e):
                    run_engine('tensor', e)


class Arena:
    def __init__(self, t, nbytes):
        self.t = t
        self.cap = nbytes
        self.off = 0

    def mark(self):
        return self.off

    def reset(self, m):
        self.off = m

    def alloc(self, free_shape, dtype):
        esz = 4 if dtype == F32 else 2
        nel = 1
        for s_ in free_shape:
            nel *= s_
        nbytes = nel * esz
        off = (self.off + 63) // 64 * 64
        assert off + nbytes <= self.cap, ("arena overflow", off, nbytes, self.cap)
        self.off = off + nbytes
        v = self.t[:, off // 2:(off + nbytes) // 2]
        if dtype == F32:
            v = v.bitcast(F32)
        if len(free_shape) == 2:
            v = v.rearrange("p (a b) -> p a b", a=free_shape[0], b=free_shape[1])
        elif len(free_shape) == 3:
            v = v.rearrange("p (a b c) -> p a b c", a=free_shape[0], b=free_shape[1], c=free_shape[2])
        return v


def _t5_bucket(n):
    n = np.maximum(n, 0)
    nf = np.maximum(n, 1).astype(np.float32)
    large = 16 + (np.log(nf / np.float32(16)) / np.float32(math.log(8.0)) * np.float32(16)).astype(np.int32)
    large = np.minimum(large, 31)
    return np.where(n < 16, n, large)


def _host_consts():
    c = {}
    c["c_identb"] = np.eye(128, dtype=np.float32).astype(ml_dtypes.bfloat16)
    c["c_J"] = np.eye(128, dtype=np.float32)[::-1].copy()
    G = np.zeros((33, 384), np.float32)
    for i in range(383):
        d = i - 127
        if d < 0:
            G[32, i] = 1.0
        else:
            b = int(_t5_bucket(np.array([d]))[0])
            G[b, i] += 1.0
            G[31, i] -= 1.0
    c["c_G"] = G
    E = np.zeros((16, NH, S), np.float32)
    for nb in range(16):
        E[nb, :, nb * 256:(nb + 1) * 256] = 1.0
    c["c_E"] = E.reshape(16, NH * S).astype(ml_dtypes.bfloat16)
    neg1 = np.zeros((16, 16), np.float32)
    ownfix = np.full((16, 16), NEG, np.float32)
    static = np.full((16, 16), NEG, np.float32)
    for b in range(16):
        neg1[b, b:] = -1e9
        ownfix[b, b] = 0.0
        static[b, :b + 1] = 0.0
    c["c_mtab"] = np.broadcast_to(
        np.concatenate([neg1.reshape(-1), ownfix.reshape(-1), static.reshape(-1)])[None, :], (128, 768)).copy()
    return c


def build_nc(debug=False, phases=(1, 2, 3)):
    nc = bass.Bass("TRN2", target_bir_lowering=False)
    dt_in = lambda name, shape, dt=F32: nc.dram_tensor(name, list(shape), dt, kind="ExternalInput").ap()
    x_d = dt_in("x", [S, D])
    w_in_d = dt_in("w_in", [D, DIN])
    wpa_d = dt_in("w_proj_attn", [512, D])
    wpr_d = dt_in("w_proj_rnn", [D, D])
    wout_d = dt_in("w_out", [D, D])
    w1_d = dt_in("w_ff1", [D, DFF])
    w2_d = dt_in("w_ff2", [DFF, D])
    nw_d = dt_in("nw_bc", [128, 2 * D])
    nwc_d = dt_in("nw_col", [128, 16])
    qk_d = dt_in("qk_col", [64, 2])
    relb_d = dt_in("rel_bias", [32, NH])
    pvec_d = dt_in("pvec", [128, 8 * 11])
    wrga_d = dt_in("w_rg_a", [16, 64, 64])
    wrgi_d = dt_in("w_rg_i", [16, 64, 64])
    cid_d = dt_in("c_identb", [128, 128], BF16)
    cJ_d = dt_in("c_J", [128, 128])
    cG_d = dt_in("c_G", [33, 384])
    cE_d = dt_in("c_E", [16, NH * S], BF16)
    cm_d = dt_in("c_mtab", [128, 768])
    out_d = nc.dram_tensor("out", [S, D], F32, kind="ExternalOutput").ap()
    skind = "ExternalOutput" if debug else "Internal"
    scr_w = nc.dram_tensor("scr_w", [NH, 384], F32, kind="Internal").ap()
    scr_o = nc.dram_tensor("scr_o", [NCH, 128, 4 * CH], BF16, kind=skind).ap()
    scr_h = nc.dram_tensor("scr_h", [NCH, 128, 8 * CH], BF16, kind="Internal").ap()

    P = Prog()
    ARENA_BYTES = 212480
    st = contextlib.ExitStack()
    with st:
        arena_t = st.enter_context(nc.sbuf_tensor("arena", [128, ARENA_BYTES // 2], BF16))
        AR = Arena(arena_t, ARENA_BYTES)
        banks = [st.enter_context(nc.psum_tensor("bank%d" % i, [128, 512], F32)) for i in range(8)]
        bankf = [b[:, :] for b in banks]
        bankb = [b[:, :].bitcast(BF16) for b in banks]

        def _fsize(ap):
            n_ = 1
            for d_ in ap.shape[1:]:
                n_ *= d_
            return n_

        def op(eng, meth, reads, writes, cost=None, **kw):
            if cost is None:
                if eng == 'tensor':
                    nmov = _fsize(kw['rhs']) if 'rhs' in kw else 128
                    cost = max(nmov, 64) / 2.4 + 3.0
                    if 'rhs' in kw and kw['rhs'].dtype == F32:
                        cost *= 4
                else:
                    ref_ap = kw.get('out', kw.get('ap', kw.get('in_')))
                    nel = _fsize(ref_ap)
                    if eng == 'scalar':
                        cost = (nel + 175) / 1.4
                        if 'out' in kw and kw['out'].dtype == F32 and nel >= 256:
                            cost *= 1.35
                        elif nel >= 1024:
                            cost *= 1.15
                    elif eng == 'vector':
                        f_ = 1.0
                        if meth in ('tensor_tensor', 'scalar_tensor_tensor'):
                            f_ = 2.0 if nel >= 1024 else 1.2
                        if meth == 'tensor_tensor_scan':
                            f_ = 2.3
                        if meth == 'reciprocal':
                            f_ = 8.0
                        cost = (nel * f_ + 60) / 0.96
                    else:
                        f_ = 3.2 if meth == 'tensor_tensor' else 1.4
                        cost = (nel * f_ + 150) / 1.4
                        if kw.get('op', None) == ALU.pow:
                            cost = nel * 160.0 + 1500.0
            fam = None
            if eng == 'scalar' and 'func' in kw:
                fam = {AF.Exp: 'exp', AF.Tanh: 'tanh', AF.Ln: 'ln', AF.Sqrt: 'sqrt'}.get(kw['func'])
            return P.add(eng, lambda e, m=meth, kw=kw: getattr(e, m)(**kw), reads, writes, cost=cost, fam=fam)

        def dma(q, out, in_, dsem, reads, writes, nowaw=False, **kw):
            esz = 4 if in_.dtype == F32 else 2
            nb = esz * in_.shape[0] * _fsize(in_)
            return P.add(q, lambda e, kw=kw: e.dma_start(out=out, in_=in_, **kw), reads, writes, dsem=dsem,
                         cost=(1200.0 if q == 'gpsimd' else 60.0), nbytes=nb, nowaw=nowaw)

        identb = AR.alloc([128], BF16)
        nw = AR.alloc([2 * D], F32)
        pvec = AR.alloc([8, 11], F32)
        nwc = AR.alloc([16], F32)
        nsp = AR.alloc([8, 2], F32)
        hbias = AR.alloc([8, 4], F32)
        tmp8 = AR.alloc([8], F32)
        rstd1 = AR.alloc([NT], F32)
        mP = AR.mark()
        mtab = AR.alloc([3, 16, 16], F32)
        qkcol = AR.alloc([2], F32)
        qkw = AR.alloc([1], F32)
        BT = AR.alloc([NH, 256], BF16)
        dma('sync', identb, cid_d, 'c0a', [], ['identb'])
        dma('sync', nw, nw_d, 'c0b', [], ['nw'])
        dma('sync', nwc, nwc_d, 'c0f', [], ['nwc'])
        dma('sync', pvec, pvec_d.rearrange("p (t j) -> p t j", t=8, j=11), 'c0c', [], ['pvec'])
        dma('sync', mtab, cm_d.rearrange("p (a b c) -> p a b c", a=3, b=16, c=16), 'c0d', [], ['mtab'])
        dma('sync', qkcol[0:64, :], qk_d, 'c0e', [], ['qkcol'])
        op('vector', 'scalar_tensor_tensor', ['qkcol'], ['qkw'], out=qkw[0:64, :], in0=qkcol[0:64, 0:1], scalar=DH ** -0.5,
           in1=qkcol[0:64, 1:2], op0=ALU.mult, op1=ALU.mult)
        op('scalar', 'activation', ['pvec'], ['tmp8'], out=tmp8, in_=pvec[:, :, 7], func=AF.Exp, scale=-1.0)
        op('scalar', 'activation', ['tmp8'], ['tmp8'], out=tmp8, in_=tmp8, func=AF.Ln, bias=1.0)
        op('vector', 'tensor_scalar', ['tmp8'], ['nsp'], out=nsp[:, :, 0], in0=tmp8, scalar1=-4.0, scalar2=None, op0=ALU.mult)
        op('vector', 'tensor_scalar', ['tmp8'], ['nsp'], out=nsp[:, :, 1], in0=tmp8, scalar1=-8.0, scalar2=None, op0=ALU.mult)
        for jj, src in enumerate((5, 6, 8, 9)):
            op('vector', 'tensor_scalar', ['pvec'], ['hbias'], out=hbias[:, :, jj], in0=pvec[:, :, src], scalar1=0.5, scalar2=None, op0=ALU.mult)
        m0 = AR.mark()
        tabx = AR.alloc([NH], F32)
        Gs = AR.alloc([384], F32)
        Js = AR.alloc([128], F32)
        wrow = AR.alloc([384], F32)
        hk = AR.alloc([2, 256], F32)
        op('vector', 'memset', [], ['tabx'], ap=tabx[32:33, :], constant=NEG)
        dma('sync', tabx[0:32, :], relb_d, 'c1a', ['tabx'], ['tabx'])
        dma('sync', Gs[0:33, :], cG_d, 'c1b', [], ['Gs'])
        dma('sync', Js, cJ_d, 'c1c', [], ['Js'])
        op('tensor', 'matmul', ['tabx', 'Gs'], ['B0'], out=bankf[0][0:8, 0:384], lhsT=tabx[0:33, :], rhs=Gs[0:33, :], start=True, stop=True)
        op('vector', 'tensor_copy', ['B0'], ['wrow'], out=wrow[0:8, :], in_=bankf[0][0:8, 0:384])
        dma('sync', scr_w, wrow[0:8, :], 'c2', ['wrow'], ['scr_w'])
        for g in range(4):
            hank_src = bass.AP(scr_w.tensor, 2 * g * 384, [[1, 128], [384, 2], [1, 256]])
            dma('sync', hk, hank_src, 'c3', ['scr_w'], ['hk'])
            op('tensor', 'matmul', ['hk', 'Js'], ['B%d' % (1 + g)], out=bankf[1 + g], lhsT=Js, rhs=hk,
               start=True, stop=True)
            op('vector', 'tensor_copy', ['B%d' % (1 + g)], ['BT'], out=BT[:, 2 * g:2 * g + 2, :], in_=bankf[1 + g].rearrange("p (a b) -> p a b", a=2, b=256))
        assert AR.off - m0 <= 8192
        AR.reset(m0)

        tok_tiles = x_d.rearrange("(n p) d -> n p d", p=128)
        out_tiles = out_d.rearrange("(n p) d -> n p d", p=128)

        def frontend(tg, xk, xt, hb, stat, nwoff, hT, tl, rs=None, rs_key=None, compute=True, hkey=None, nw_folded=False):
            if rs is None:
                rs, rs_key = stat[:, 2:3], tg + 'stat'
            if compute:
                op('scalar', 'activation', [xk], [tg + 'hb', tg + 'stat'], out=hb, in_=xt, func=AF.Square, accum_out=stat[:, 0:1])
                op('scalar', 'activation', [tg + 'stat'], [tg + 'stat'], out=stat[:, 1:2], in_=stat[:, 0:1], func=AF.Ln, scale=1.0 / D, bias=EPS)
                op('scalar', 'activation', [tg + 'stat'], [rs_key], out=rs, in_=stat[:, 1:2], func=AF.Exp, scale=-0.5)
            if nw_folded:
                op('scalar', 'activation', [xk, rs_key], [tg + 'hb'], out=hb, in_=xt, func=AF.Identity, scale=rs)
            else:
                op('vector', 'scalar_tensor_tensor', [xk, rs_key, 'nw'], [tg + 'hb'], out=hb, in0=xt, scalar=rs,
                   in1=nw[:, nwoff:nwoff + D], op0=ALU.mult, op1=ALU.mult)
            for dt_ in range(8):
                op('tensor', 'transpose', [tg + 'hb', 'identb'], ['B0'], out=bankb[0][:, dt_ * 128:(dt_ + 1) * 128],
                   in_=hb[:, dt_ * 128:(dt_ + 1) * 128], identity=identb)
            op('scalar', 'copy', ['B0'], [(hkey or tg) + 'hT%d' % tl], out=hT[:, :, tl * 128:(tl + 1) * 128],
               in_=bankb[0].rearrange("p (a b) -> p a b", a=8, b=128))

        mA = AR.mark()
        if 1 in phases:
            oT = [AR.alloc([4, CH], BF16) for _ in range(2)]
            wqkv = AR.alloc([8, 1536], BF16)
            kT = AR.alloc([NH, S], BF16)
            Vg = AR.alloc([NT, NH, 65], BF16)
            xs = [AR.alloc([D], F32) for _ in range(2)]
            hb = AR.alloc([D], BF16)
            stat = AR.alloc([4], F32)
            hTA = [AR.alloc([8, CH], BF16) for _ in range(2)]
            sq = AR.alloc([2 * 512], F32)
            ssqk = AR.alloc([32], F32)
            qa = AR.alloc([NH, 64], BF16)
            ktm = AR.alloc([512], BF16)
            qT = [AR.alloc([NH, CH], BF16) for _ in range(2)]
            kmsum = AR.alloc([NH], F32)
            kmT = AR.alloc([NH, 16], BF16)
            gm = AR.alloc([NH, 16], F32)
            top8 = AR.alloc([NH, 8], F32)
            m01 = AR.alloc([NH, 16], F32)
            maskb = AR.alloc([NH, 16], BF16)
            NPT = 10
            PT = [AR.alloc([512], BF16) for _ in range(NPT)]
            otm = [AR.alloc([512], BF16) for _ in range(2)]
            rcp = AR.alloc([4], F32)

            w_in_v = w_in_d.rearrange("(t p) c -> p t c", p=128)
            for j in range(3):
                dma('gpsimd', wqkv[:, :, j * 512:(j + 1) * 512], w_in_v[:, :, j * 512:(j + 1) * 512], 'wqkv%d' % j, [], ['wqkv%d' % j], nowaw=True)
            dma('sync', kT[64:80, :, :], cE_d.rearrange("n (h s) -> n h s", h=NH, s=S), 'c4', [], ['kTE'])
            op('gpsimd', 'memset', [], ['Vg'], ap=Vg.rearrange("p a b c -> p (a b c)"), constant=1.0)
            op('vector', 'memset', [], ['kmT'], ap=kmT.rearrange("p a b -> p (a b)"), constant=0.0)

            def load_x(t):
                dma('sync', xs[t % 2], tok_tiles[t], 'xA%d' % (t % 2), [], ['Ax%d' % (t % 2)])

            SB = [1, 2, 3, 4, 5]
            sctr_box = [0]
            load_x(0)
            def FE(c, tls=range(TPC)):
                qTc = qT[c % 2]
                qp = 'q%d_' % (c % 2)
                hT = hTA[c % 2]
                hp = 'A%d' % (c % 2)
                for tl in tls:
                    t = c * TPC + tl
                    if t + 1 < NT:
                        load_x(t + 1)
                    frontend('A', 'Ax%d' % (t % 2), xs[t % 2], hb, stat, 0, hT, tl, rs=rstd1[:, t:t + 1], rs_key='rstd1_%d' % t, hkey=hp)
                    for j, bk in enumerate(('B1', 'B2', 'B3')):
                        for dt_ in range(8):
                            op('tensor', 'matmul', [hp + 'hT%d' % tl, 'wqkv%d' % j], [bk], out=bankf[1 + j], lhsT=hT[:, dt_, tl * 128:(tl + 1) * 128],
                               rhs=wqkv[:, dt_, j * 512:(j + 1) * 512], start=(dt_ == 0), stop=(dt_ == 7))
                    op('scalar', 'activation', ['B1'], ['sq0'], out=sq[:, 0:512], in_=bankf[1], func=AF.Square)
                    op('scalar', 'activation', ['B2'], ['sq1'], out=sq[:, 512:1024], in_=bankf[2], func=AF.Square)
                    op('vector', 'tensor_reduce', ['sq0', 'sq1'], ['ssqk'], out=ssqk[:, 0:16], in_=sq.rearrange("p (a b) -> p a b", a=16, b=64),
                       axis=AX.X, op=ALU.add)
                    op('scalar', 'activation', ['ssqk'], ['ssqk'], out=ssqk[:, 16:32], in_=ssqk[:, 0:16], func=AF.Ln, scale=1.0 / DH, bias=EPS)
                    op('scalar', 'activation', ['ssqk'], ['ssqk'], out=ssqk[:, 0:16], in_=ssqk[:, 16:32], func=AF.Exp, scale=-0.5)
                    op('vector', 'tensor_tensor', ['B1', 'ssqk'], ['qa'], out=qa, in0=bankf[1].rearrange("p (a b) -> p a b", a=8, b=64),
                       in1=ssqk[:, 0:8].unsqueeze(2).to_broadcast([128, 8, 64]), op=ALU.mult)
                    op('vector', 'tensor_tensor', ['B2', 'ssqk'], ['ktm'], out=ktm.rearrange("p (a b) -> p a b", a=8, b=64),
                       in0=bankf[2].rearrange("p (a b) -> p a b", a=8, b=64),
                       in1=ssqk[:, 8:16].unsqueeze(2).to_broadcast([128, 8, 64]), op=ALU.mult)
                    op('scalar', 'copy', ['B3', 'Vg'], ['V%d' % t], out=Vg[:, t, :, 0:64], in_=bankf[3].rearrange("p (a b) -> p a b", a=8, b=64))
                    for h in range(NH):
                        op('tensor', 'transpose', ['ktm', 'identb'], ['B4'], out=bankb[4][0:64, h * 128:(h + 1) * 128],
                           in_=ktm[:, h * 64:(h + 1) * 64], identity=identb)
                    op('scalar', 'activation', ['B4', 'qkw'], ['kT%d' % t], out=kT[0:64, :, t * 128:(t + 1) * 128],
                       in_=bankb[4][0:64, :].rearrange("p (a b) -> p a b", a=8, b=128), func=AF.Identity, scale=qkw[0:64, 0:1])
                    for h in range(NH):
                        op('tensor', 'transpose', ['qa', 'identb'], ['B5'], out=bankb[5][0:64, h * 128:(h + 1) * 128],
                           in_=qa[:, h, :], identity=identb)
                    op('vector', 'tensor_copy', ['B5'], [qp + 'qT%d' % tl], out=qTc[0:64, :, tl * 128:(tl + 1) * 128],
                       in_=bankb[5][0:64, :].rearrange("p (a b) -> p a b", a=8, b=128))
                    if tl == TPC - 1:
                        dma('sync', scr_h[c], hT.rearrange("p a b -> p (a b)"), 'sh%d' % (c % 2), [hp + 'hT%d' % i_ for i_ in range(TPC)], ['scr_h%d' % c])
                    b = t // 2
                    if t % 2 == 1 and b < NB - 1:
                        op('vector', 'tensor_reduce', ['kT%d' % (t - 1), 'kT%d' % t], ['kmsum'], out=kmsum[0:64, :],
                           in_=kT[0:64, :, b * 256:(b + 1) * 256], axis=AX.X, op=ALU.add)
                        op('vector', 'tensor_scalar', ['kmsum'], ['kmT'], out=kmT[0:64, :, b], in0=kmsum[0:64, :], scalar1=1.0 / 256,
                           scalar2=None, op0=ALU.mult)

            def GM(c):
                qTc = qT[c % 2]
                qp = 'q%d_' % (c % 2)
                for tl in range(TPC):
                    t = c * TPC + tl
                    b = t // 2
                    if b >= 3:
                        for h in range(NH):
                            op('tensor', 'matmul', [qp + 'qT%d' % tl, 'kmT'], ['B0'], out=bankf[0][:, h * 16:(h + 1) * 16],
                               lhsT=qTc[0:64, h, tl * 128:(tl + 1) * 128], rhs=kmT[0:64, h, :], start=True, stop=True)
                        op('vector', 'tensor_tensor', ['B0', 'mtab'], ['gm'], out=gm, in0=bankf[0][:, 0:128].rearrange("p (a b) -> p a b", a=8, b=16),
                           in1=mtab[:, 0, b, :].unsqueeze(1).to_broadcast([128, 8, 16]), op=ALU.add)
                        for h in range(NH):
                            op('vector', 'max', ['gm'], ['top8'], out=top8[:, h, :], in_=gm[:, h, :])
                        op('vector', 'tensor_tensor', ['gm', 'top8'], ['m01'], out=m01, in0=gm,
                           in1=top8[:, :, 2:3].to_broadcast([128, 8, 16]), op=ALU.is_lt)
                        op('vector', 'tensor_tensor', ['m01', 'mtab'], ['maskb'], out=maskb, in0=m01,
                           in1=mtab[:, 1, b, :].unsqueeze(1).to_broadcast([128, 8, 16]), op=ALU.mult)
                    else:
                        op('vector', 'tensor_copy', ['mtab'], ['maskb'], out=maskb,
                           in_=mtab[:, 2, b, :].unsqueeze(1).to_broadcast([128, 8, 16]))
                    for h in range(NH):
                        op('tensor', 'transpose', ['maskb', 'identb'], ['B0'], out=bankb[0][0:16, h * 128:(h + 1) * 128],
                           in_=maskb[:, h, :], identity=identb)
                    op('vector', 'tensor_copy', ['B0'], [qp + 'qTm%d' % tl], out=qTc[64:80, :, tl * 128:(tl + 1) * 128],
                       in_=bankb[0][0:16, :].rearrange("p (a b) -> p a b", a=8, b=128))

            def ATT(c, bl, heads=range(NH), fin=True):
                qTc = qT[c % 2]
                qp = 'q%d_' % (c % 2)
                sctr = sctr_box[0]
                if True:
                    b = 2 * c + bl
                    q0 = bl * 256
                    tq = (2 * bl, 2 * bl + 1)
                    qkeys = [qp + 'qT%d' % tq[0], qp + 'qT%d' % tq[1], qp + 'qTm%d' % tq[0], qp + 'qTm%d' % tq[1]]
                    for h in heads:
                        par = h % 2
                        ob = 6 + par
                        obk = 'B%d' % ob
                        npair = b + 1
                        for j in range(npair):
                            sb_ = SB[sctr % len(SB)]
                            pt = PT[sctr % NPT]
                            ptk = 'PT%d' % (sctr % NPT)
                            sk = 'B%d' % sb_
                            sctr += 1
                            sctr_box[0] = sctr
                            k0, k1 = 2 * j, 2 * j + 1
                            last = (j == b)
                            prev = (j == b - 1)
                            op('tensor', 'matmul', ['kT%d' % k0, 'kTE'] + qkeys, [sk], out=bankf[sb_][:, 0:256],
                               lhsT=kT[0:80, h, k0 * 128:(k0 + 1) * 128], rhs=qTc[0:80, h, q0:q0 + 256], start=True, stop=not last)
                            if last:
                                op('tensor', 'matmul', ['BT', 'identb'], [sk], out=bankf[sb_][:, 0:256], lhsT=identb, rhs=BT[:, h, 0:256],
                                   start=False, stop=True)
                                op('tensor', 'matmul', ['kT%d' % k1, 'kTE'] + qkeys, [sk], out=bankf[sb_][:, 256:384],
                                   lhsT=kT[0:80, h, k1 * 128:(k1 + 1) * 128], rhs=qTc[0:80, h, q0 + 128:q0 + 256], start=True, stop=False)
                                op('tensor', 'matmul', ['BT', 'identb'], [sk], out=bankf[sb_][:, 256:384], lhsT=identb, rhs=BT[:, h, 0:128],
                                   start=False, stop=True)
                            else:
                                op('tensor', 'matmul', ['kT%d' % k1, 'kTE'] + qkeys, [sk], out=bankf[sb_][:, 256:512],
                                   lhsT=kT[0:80, h, k1 * 128:(k1 + 1) * 128], rhs=qTc[0:80, h, q0:q0 + 256], start=True, stop=not prev)
                                if prev:
                                    op('tensor', 'matmul', ['BT', 'identb'], [sk], out=bankf[sb_][:, 256:384], lhsT=identb, rhs=BT[:, h, 128:256],
                                       start=False, stop=True)
                            if last:
                                op('scalar', 'activation', [sk], [ptk], out=pt[:, 0:384], in_=bankf[sb_][:, 0:384], func=AF.Exp)
                            else:
                                op('scalar', 'activation', [sk], [ptk], out=pt, in_=bankf[sb_], func=AF.Exp)
                            first = (j == 0)
                            op('tensor', 'matmul', [ptk, 'V%d' % k0], [obk], out=bankf[ob][:, 0:65], lhsT=pt[:, 0:128],
                               rhs=Vg[:, k0, h, :], start=first, stop=last, skip_group_check=True)
                            op('tensor', 'matmul', [ptk, 'V%d' % k0], [obk], out=bankf[ob][:, 128:193], lhsT=pt[:, 128:256],
                               rhs=Vg[:, k0, h, :], start=False, stop=False, skip_group_check=True)
                            if not last:
                                op('tensor', 'matmul', [ptk, 'V%d' % k1], [obk], out=bankf[ob][:, 0:65], lhsT=pt[:, 256:384],
                                   rhs=Vg[:, k1, h, :], start=False, stop=False, skip_group_check=True)
                            op('tensor', 'matmul', [ptk, 'V%d' % k1], [obk], out=bankf[ob][:, 128:193], lhsT=(pt[:, 256:384] if last else pt[:, 384:512]),
                               rhs=Vg[:, k1, h, :], start=False, stop=last, skip_group_check=True)
                        for qi in range(2):
                            oc = qi * 128
                            op('vector', 'reciprocal', [obk], ['rcp%d' % qi], out=rcp[:, qi:qi + 1], in_=bankf[ob][:, oc + 64:oc + 65])
                            op('vector', 'tensor_scalar', [obk, 'rcp%d' % qi], ['otm%d_%d' % (qi, h)], out=otm[qi][:, h * 64:(h + 1) * 64],
                               in0=bankf[ob][:, oc:oc + 64], scalar1=rcp[:, qi:qi + 1], scalar2=None, op0=ALU.mult)
                    for qi in (range(2) if fin else ()):
                        okeys = ['otm%d_%d' % (qi, h) for h in range(NH)]
                        for g in range(4):
                            op('tensor', 'transpose', okeys + ['identb'], ['B3'], out=bankb[3][:, g * 128:(g + 1) * 128],
                               in_=otm[qi][:, g * 128:(g + 1) * 128], identity=identb)
                        col = q0 + qi * 128
                        op('vector', 'tensor_copy', ['B3'], ['oT%d' % (c % 2)], out=oT[c % 2][:, :, col:col + 128],
                           in_=bankb[3][:, 0:512].rearrange("p (a b) -> p a b", a=4, b=128))

            FE(0)
            GM(0)
            FINE = False
            for c in range(NCH):
                if FINE and c + 1 < NCH:
                    ATT(c, 0, range(0, 4), False)
                    FE(c + 1, [0])
                    ATT(c, 0, range(4, 8), True)
                    FE(c + 1, [1])
                    ATT(c, 1, range(0, 4), False)
                    FE(c + 1, [2])
                    ATT(c, 1, range(4, 8), True)
                    FE(c + 1, [3])
                    GM(c + 1)
                else:
                    ORD = 'A3'
                    nx = c + 1 < NCH
                    if ORD == 'A0':
                        ATT(c, 0)
                        if nx:
                            FE(c + 1)
                            GM(c + 1)
                        ATT(c, 1)
                    elif ORD == 'A1':
                        ATT(c, 0)
                        if nx:
                            FE(c + 1)
                        ATT(c, 1)
                        if nx:
                            GM(c + 1)
                    elif ORD == 'A2':
                        if nx:
                            FE(c + 1)
                            GM(c + 1)
                        ATT(c, 0)
                        ATT(c, 1)
                    elif ORD == 'A3':
                        ATT(c, 0, range(0, 4), False)
                        if nx:
                            FE(c + 1)
                            GM(c + 1)
                        ATT(c, 0, range(4, 8), True)
                        ATT(c, 1)
                    elif ORD.startswith('K'):
                        kk = int(ORD[1:])
                        ATT(c, 0, range(0, kk), False)
                        if nx:
                            FE(c + 1)
                            GM(c + 1)
                        ATT(c, 0, range(kk, 8), True)
                        ATT(c, 1)
                    elif ORD.startswith('P'):
                        pa, pb, pg = int(ORD[1:3]), int(ORD[3:5]), int(ORD[5:7])
                        cuts = sorted(set([0, pa, pb, pg, 8, 16]))
                        for i_ in range(len(cuts) - 1):
                            lo_, hi_ = cuts[i_], cuts[i_ + 1]
                            if lo_ == pa and nx:
                                FE(c + 1, [0, 1])
                            if lo_ == pb and nx:
                                FE(c + 1, [2, 3])
                            if lo_ == pg and nx:
                                GM(c + 1)
                            if hi_ > lo_:
                                bl_ = 0 if lo_ < 8 else 1
                                ATT(c, bl_, range(lo_ - 8 * bl_, hi_ - 8 * bl_), (hi_ % 8 == 0))
                    elif ORD == 'A4':
                        ATT(c, 0)
                        if nx:
                            FE(c + 1, [0, 1])
                        ATT(c, 1, range(0, 4), False)
                        if nx:
                            FE(c + 1, [2, 3])
                            GM(c + 1)
                        ATT(c, 1, range(4, 8), True)
                dma('sync', scr_o[c], oT[c % 2].rearrange("p a b -> p (a b)"), 'so%d' % (c % 2), ['oT%d' % (c % 2)], ['scr_o%d' % c])
        AR.reset(mA)

        if 2 in phases:
            P.barrier()
            AR.reset(mP)
            wr = AR.alloc([8, 4096], BF16)
            wpa = AR.alloc([4, D], BF16)
            wpr = AR.alloc([8, D], BF16)
            wo = AR.alloc([8, D], BF16)
            wbda = AR.alloc([8, 128], BF16)
            wbdi = AR.alloc([8, 128], BF16)
            xres = [AR.alloc([D], F32) for _ in range(3)]
            hTs = [AR.alloc([8, CH], BF16) for _ in range(2)]
            halo = AR.alloc([8, 3], F32)
            hstate = AR.alloc([8], F32)
            NR = 2
            xr = [AR.alloc([CH + 3], F32) for _ in range(NR)]
            cv = [AR.alloc([CH], F32) for _ in range(NR)]
            cvb = [AR.alloc([CH], BF16) for _ in range(NR)]
            rr = [AR.alloc([CH], F32) for _ in range(NR)]
            ii = [AR.alloc([CH], F32) for _ in range(NR)]
            aa = [AR.alloc([CH], F32) for _ in range(NR)]
            gq = [AR.alloc([CH], F32) for _ in range(NR)]
            ornTs = [AR.alloc([8, CH], BF16) for _ in range(2)]
            oaTs = [AR.alloc([4, CH], BF16) for _ in range(2)]
            t1 = [AR.alloc([CH], F32) for _ in range(1)]
            t2 = [AR.alloc([CH], F32) for _ in range(1)]
            mT = AR.alloc([8, CH], BF16)

            op('vector', 'memset', [], ['wbda'], ap=wbda.rearrange("p a b -> p (a b)"), constant=0.0)
            op('vector', 'memset', [], ['wbdi'], ap=wbdi.rearrange("p a b -> p (a b)"), constant=0.0)
            op('vector', 'memset', [], ['halo'], ap=halo.rearrange("p a b -> p (a b)"), constant=0.0)
            op('vector', 'memset', [], ['hstate'], ap=hstate, constant=0.0)
            w_in_v = w_in_d.rearrange("(t p) c -> p t c", p=128)
            for g in range(8):
                if g == 4:
                    for wsrc, wdst, nm in ((wrga_d, wbda, 'wbda'), (wrgi_d, wbdi, 'wbdi')):
                        v = wsrc.rearrange("(t two) d e -> two d t e", two=2)
                        dma('gpsimd', wdst[0:64, :, 0:64], v[0], nm, [nm], [nm])
                        dma('gpsimd', wdst[64:128, :, 64:128], v[1], nm, [nm], [nm + 'x'])
                    dma('gpsimd', wpa, wpa_d.rearrange("(t p) c -> p t c", p=128), 'wpa', [], ['wpa'], nowaw=True)
                    dma('gpsimd', wpr, wpr_d.rearrange("(t p) c -> p t c", p=128), 'wpr', [], ['wpr'], nowaw=True)
                dma('gpsimd', wr[:, :, g * 512:(g + 1) * 512], w_in_v[:, :, 1536 + g * 512:1536 + (g + 1) * 512], 'wr%d' % g, [], ['wr%d' % g], nowaw=True)
            dma('gpsimd', wo, wout_d.rearrange("(t p) c -> p t c", p=128), 'wo', [], ['wo'], nowaw=True)

            def FEB(c):
                dma('sync', hTs[c % 2].rearrange("p a b -> p (a b)"), scr_h[c], 'hB%d' % (c % 2), ['scr_h%d' % c], ['BhT%d' % (c % 2)])

            def rnnA(c, ct, hT, hTk):
                r_ = ct % NR
                sfx = '%d' % r_
                bx, by = (1, 2) if ct % 2 == 0 else (3, 4)
                for dt_ in range(8):
                    op('tensor', 'matmul', hTk + ['wr%d' % (ct // 4)], ['B%d' % bx], out=bankf[bx], lhsT=wr[:, dt_, ct * 128:(ct + 1) * 128], rhs=hT[:, dt_, :],
                       start=(dt_ == 0), stop=(dt_ == 7))
                for dt_ in range(8):
                    op('tensor', 'matmul', hTk + ['wr%d' % (2 + ct // 4)], ['B%d' % by], out=bankf[by], lhsT=wr[:, dt_, 1024 + ct * 128:1024 + (ct + 1) * 128],
                       rhs=hT[:, dt_, :], start=(dt_ == 0), stop=(dt_ == 7))
                op('vector', 'tensor_copy', ['halo'], ['xrh' + sfx], out=xr[r_][:, 0:3], in_=halo[:, ct, :])
                op('scalar', 'copy', ['B%d' % bx], ['xr' + sfx], out=xr[r_][:, 3:CH + 3], in_=bankf[bx])
                op('scalar', 'activation', ['xr' + sfx, 'pvec'], ['cv' + sfx], out=cv[r_], in_=xr[r_][:, 3:CH + 3], func=AF.Identity, scale=pvec[:, ct, 3:4],
                   bias=pvec[:, ct, 4:5])
                for j in range(3):
                    op('vector', 'scalar_tensor_tensor', ['xr' + sfx, 'xrh' + sfx, 'pvec', 'cv' + sfx], ['cv' + sfx], out=cv[r_], in0=xr[r_][:, j:j + CH],
                       scalar=pvec[:, ct, j:j + 1], in1=cv[r_], op0=ALU.mult, op1=ALU.add)
                op('vector', 'tensor_copy', ['xr' + sfx], ['halo'], out=halo[:, ct, :], in_=xr[r_][:, CH:CH + 3])
                op('scalar', 'copy', ['cv' + sfx], ['cvb' + sfx], out=cvb[r_], in_=cv[r_])
                op('tensor', 'matmul', ['cvb' + sfx, 'wbda', 'wbdax'], ['B5'], out=bankf[5], lhsT=wbda[:, ct, :], rhs=cvb[r_], start=True, stop=True)
                op('tensor', 'matmul', ['cvb' + sfx, 'wbdi', 'wbdix'], ['B6'], out=bankf[6], lhsT=wbdi[:, ct, :], rhs=cvb[r_], start=True, stop=True)
                op('scalar', 'activation', ['B5', 'hbias'], ['rr' + sfx], out=rr[r_], in_=bankf[5], func=AF.Tanh, scale=0.5, bias=hbias[:, ct, 0:1])
                op('scalar', 'activation', ['B6', 'hbias'], ['ii' + sfx], out=ii[r_], in_=bankf[6], func=AF.Tanh, scale=0.5, bias=hbias[:, ct, 1:2])
                op('scalar', 'activation', ['rr' + sfx, 'nsp'], ['aa' + sfx], out=aa[r_], in_=rr[r_], func=AF.Exp, scale=nsp[:, ct, 0:1], bias=nsp[:, ct, 0:1])
                op('scalar', 'activation', ['rr' + sfx, 'nsp'], ['rr' + sfx], out=rr[r_], in_=rr[r_], func=AF.Exp, scale=nsp[:, ct, 1:2], bias=nsp[:, ct, 1:2])
                op('scalar', 'activation', ['B%d' % by], ['gq' + sfx], out=gq[r_], in_=bankf[by], func=AF.Square, scale=math.sqrt(0.044715))
                op('vector', 'scalar_tensor_tensor', ['gq' + sfx, 'B%d' % by], ['gq' + sfx], out=gq[r_], in0=gq[r_], scalar=1.0, in1=bankf[by],
                   op0=ALU.add, op1=ALU.mult)
                op('scalar', 'activation', ['gq' + sfx], ['gq' + sfx], out=gq[r_], in_=gq[r_], func=AF.Tanh, scale=math.sqrt(2.0 / math.pi))
                op('vector', 'scalar_tensor_tensor', ['gq' + sfx, 'B%d' % by], ['gq' + sfx], out=gq[r_], in0=gq[r_], scalar=1.0, in1=bankf[by],
                   op0=ALU.add, op1=ALU.mult)

            def rnnB(c, ct):
                r_ = ct % NR
                sfx = '%d' % r_
                op('gpsimd', 'tensor_scalar', ['rr' + sfx], ['rr' + sfx], out=rr[r_], in0=rr[r_], scalar1=1.0, scalar2=0.0, op0=ALU.min, op1=ALU.max)
                op('scalar', 'activation', ['rr' + sfx], ['rr' + sfx], out=rr[r_], in_=rr[r_], func=AF.Sqrt, scale=-0.0625, bias=0.0625)

            def rnnC(c, ct):
                r_ = ct % NR
                sfx = '%d' % r_
                hh_ = xr[r_][:, 0:CH]
                op('vector', 'scalar_tensor_tensor', ['ii' + sfx, 'rr' + sfx], ['ii' + sfx], out=ii[r_], in0=ii[r_], scalar=1.0, in1=rr[r_], op0=ALU.add, op1=ALU.mult)
                op('vector', 'tensor_tensor', ['ii' + sfx, 'cv' + sfx], ['cv' + sfx], out=cv[r_], in0=ii[r_], in1=cv[r_], op=ALU.mult)
                op('vector', 'tensor_tensor_scan', ['aa' + sfx, 'cv' + sfx, 'hstate'], ['xr' + sfx, 'xrh' + sfx], out=hh_, data0=aa[r_], data1=cv[r_],
                   initial=hstate[:, ct:ct + 1], op0=ALU.mult, op1=ALU.add)
                op('vector', 'tensor_copy', ['xr' + sfx], ['hstate'], out=hstate[:, ct:ct + 1], in_=xr[r_][:, CH - 1:CH])
                op('gpsimd', 'tensor_tensor', ['xr' + sfx, 'xrh' + sfx, 'gq' + sfx], ['ornT%d_%d' % (c % 2, ct)], out=ornTs[c % 2][:, ct, :], in0=hh_,
                   in1=gq[r_], op=ALU.mult)

            def gating(c, m, hT, hTk):
                b1, b2, b3, b4 = (1, 2, 3, 4) if m % 2 == 0 else (5, 6, 7, 0)
                for dt_ in range(8):
                    op('tensor', 'matmul', hTk + ['wr%d' % (4 + m // 4)], ['B%d' % b1], out=bankf[b1], lhsT=wr[:, dt_, 2048 + m * 128:2048 + (m + 1) * 128],
                       rhs=hT[:, dt_, :], start=(dt_ == 0), stop=(dt_ == 7))
                for dt_ in range(8):
                    op('tensor', 'matmul', hTk + ['wr%d' % (6 + m // 4)], ['B%d' % b2], out=bankf[b2], lhsT=wr[:, dt_, 3072 + m * 128:3072 + (m + 1) * 128],
                       rhs=hT[:, dt_, :], start=(dt_ == 0), stop=(dt_ == 7))
                for kt in range(4):
                    op('tensor', 'matmul', ['oaT%d' % (c % 2), 'wpa'], ['B%d' % b3], out=bankf[b3], lhsT=wpa[:, kt, m * 128:(m + 1) * 128], rhs=oaTs[c % 2][:, kt, :],
                       start=(kt == 0), stop=(kt == 3))
                for kt in range(8):
                    op('tensor', 'matmul', ['ornT%d_%d' % (c % 2, i_) for i_ in range(8)] + ['wpr'], ['B%d' % b4], out=bankf[b4], lhsT=wpr[:, kt, m * 128:(m + 1) * 128],
                       rhs=ornTs[c % 2][:, kt, :], start=(kt == 0), stop=(kt == 7))
                op('scalar', 'activation', ['B%d' % b1, 'hbias'], ['t1'], out=t1[0], in_=bankf[b1], func=AF.Tanh, scale=0.5, bias=hbias[:, m, 2:3])
                op('scalar', 'activation', ['B%d' % b2, 'hbias'], ['t2'], out=t2[0], in_=bankf[b2], func=AF.Tanh, scale=0.5, bias=hbias[:, m, 3:4])
                op('vector', 'scalar_tensor_tensor', ['t1', 'B%d' % b3], ['t1'], out=t1[0], in0=t1[0], scalar=1.0, in1=bankf[b3], op0=ALU.add, op1=ALU.mult)
                op('vector', 'scalar_tensor_tensor', ['t2', 'B%d' % b4], ['t2'], out=t2[0], in0=t2[0], scalar=1.0, in1=bankf[b4], op0=ALU.add, op1=ALU.mult)
                op('vector' if m % 2 else 'gpsimd', 'tensor_tensor', ['t1', 't2'], ['mT%d' % m], out=mT[:, m, :], in0=t1[0], in1=t2[0], op=ALU.add)

            mTk = ['mT%d' % i for i in range(8)]
            def rnn_pair(c, p2):
                hT = hTs[c % 2]
                hTk = ['BhT%d' % (c % 2)]
                for ct in (2 * p2, 2 * p2 + 1):
                    rnnA(c, ct, hT, hTk)
                for ct in (2 * p2, 2 * p2 + 1):
                    rnnB(c, ct)
                for ct in (2 * p2, 2 * p2 + 1):
                    rnnC(c, ct)

            def load_res(t):
                dma('sync', xres[t % 3], tok_tiles[t], 'xr%d' % (t % 3), ['out_s%d' % (t % 3)], ['Bxres%d' % (t % 3)])

            def load_oa(c):
                dma('sync', oaTs[c % 2].rearrange("p a b -> p (a b)"), scr_o[c], 'oa%d' % (c % 2), ['scr_o%d' % c], ['oaT%d' % (c % 2)])

            load_oa(0)
            FEB(0)
            if NCH > 1:
                FEB(1)
            for t_ in range(3):
                load_res(t_)
            for p2 in range(4):
                rnn_pair(0, p2)
            for c in range(NCH):
                hT = hTs[c % 2]
                hTk = ['BhT%d' % (c % 2)]
                if c + 1 < NCH:
                    load_oa(c + 1)
                BO = '0'
                for m in range(8):
                    if BO == '1' and m % 2 == 0 and c + 1 < NCH:
                        rnn_pair(c + 1, m // 2)
                    gating(c, m, hT, hTk)
                    if BO == '0' and m % 2 == 1 and c + 1 < NCH:
                        rnn_pair(c + 1, m // 2)
                    if BO == '2' and m < 4 and c + 1 < NCH:
                        rnn_pair(c + 1, m)
                if c + 2 < NCH:
                    FEB(c + 2)
                for tl in range(TPC):
                    t = c * TPC + tl
                    sl = t % 3
                    for hf in range(2):
                        bk = (7, 0, 5, 6)[(2 * tl + hf) % 4]
                        for kt in range(8):
                            op('tensor', 'matmul', mTk + ['wo'], ['B%d' % bk], out=bankf[bk], lhsT=mT[:, kt, tl * 128:(tl + 1) * 128],
                               rhs=wo[:, kt, hf * 512:(hf + 1) * 512], start=(kt == 0), stop=(kt == 7))
                        op('vector', 'scalar_tensor_tensor', ['B%d' % bk, 'Bxres%d' % sl], ['Bxres%d' % sl], out=xres[sl][:, hf * 512:(hf + 1) * 512],
                           in0=bankf[bk], scalar=0.5, in1=xres[sl][:, hf * 512:(hf + 1) * 512], op0=ALU.mult, op1=ALU.add)
                    dma('sync', out_tiles[t], xres[sl], 'os%d' % sl, ['Bxres%d' % sl], ['out%d' % t, 'out_s%d' % sl])
                    if t + 3 < NT:
                        load_res(t + 3)

        if 3 in phases:
            P.barrier()
            AR.reset(mP)
            W1 = AR.alloc([8, DFF], BF16)
            W2 = AR.alloc([32, D], BF16)
            xs = [AR.alloc([D], F32) for _ in range(2)]
            xres = [AR.alloc([D], F32) for _ in range(2)]
            hb = AR.alloc([D], BF16)
            stat = AR.alloc([4], F32)
            hTs = [AR.alloc([8, CH], BF16) for _ in range(2)]
            uT = AR.alloc([32, CH], BF16)
            rl = [AR.alloc([CH], F32) for _ in range(2)]
            w1_v = w1_d.rearrange("(t p) c -> p t c", p=128)
            w2_v = w2_d.rearrange("(t p) c -> p t c", p=128)
            for g in range(8):
                dma('gpsimd', W1[:, :, g * 512:(g + 1) * 512], w1_v[:, :, g * 512:(g + 1) * 512], 'w1_%d' % g, [], ['w1_%d' % g], nowaw=True)
            for g in range(8):
                dma('gpsimd', W2[:, 4 * g:4 * g + 4, :], w2_v[:, 4 * g:4 * g + 4, :], 'w2_%d' % g, [], ['w2_%d' % g], nowaw=True)

            def load_xC(t):
                dma('sync', xs[t % 2], out_tiles[t], 'xC%d' % (t % 2), ['out%d' % t], ['Cx%d' % (t % 2)])

            def FEC(c):
                for tl in range(TPC):
                    t = c * TPC + tl
                    load_xC(t)
                    frontend('C', 'Cx%d' % (t % 2), xs[t % 2], hb, stat, D, hTs[c % 2], tl, hkey='C%d' % (c % 2))

            rctr = 0
            fctr = 0
            octr = 0
            FPOS = 'f31'
            FEC(0)
            for c in range(NCH):
                hT = hTs[c % 2]
                hTk = ['C%dhT%d' % (c % 2, i) for i in range(TPC)]
                for f in range(32):
                    bk = 1 + fctr % 4
                    rb = fctr % 2
                    fctr += 1
                    for dt_ in range(8):
                        op('tensor', 'matmul', hTk + ['w1_%d' % (f // 4)], ['B%d' % bk], out=bankf[bk], lhsT=W1[:, dt_, f * 128:(f + 1) * 128], rhs=hT[:, dt_, :],
                           start=(dt_ == 0), stop=(dt_ == 7))
                    op('scalar', 'activation', ['B%d' % bk], ['rl%d' % rb], out=rl[rb], in_=bankf[bk], func=AF.Relu)
                    op('gpsimd' if f % 3 == 2 else 'vector', 'tensor_tensor', ['rl%d' % rb], ['uT%d' % f], out=uT[:, f, :], in0=rl[rb], in1=rl[rb], op=ALU.mult)
                    if FPOS == 'f%d' % f and c + 1 < NCH:
                        FEC(c + 1)
                uTk = ['uT%d' % i for i in range(32)]
                for tl in range(TPC):
                    t = c * TPC + tl
                    sl = rctr % 2
                    rctr += 1
                    dma('sync', xres[sl], out_tiles[t], 'xq%d' % sl, ['out%d' % t, 'fin_s%d' % sl], ['Cxres%d' % sl])
                    for hf in range(2):
                        bk = 5 + octr % 3
                        octr += 1
                        for f in range(32):
                            op('tensor', 'matmul', uTk + ['w2_%d' % (f // 4)], ['B%d' % bk], out=bankf[bk], lhsT=uT[:, f, tl * 128:(tl + 1) * 128],
                               rhs=W2[:, f, hf * 512:(hf + 1) * 512], start=(f == 0), stop=(f == 31))
                        op('vector', 'tensor_tensor', ['B%d' % bk, 'Cxres%d' % sl], ['Cxres%d' % sl], out=xres[sl][:, hf * 512:(hf + 1) * 512], in0=bankf[bk],
                           in1=xres[sl][:, hf * 512:(hf + 1) * 512], op=ALU.add)
                    dma('sync', out_tiles[t], xres[sl], 'fs%d' % sl, ['Cxres%d' % sl], ['out%d' % t, 'fin_s%d' % sl])
                    if FPOS == 't%d' % tl and c + 1 < NCH:
                        FEC(c + 1)

        fk = ['scr_o%d' % c for c in range(NCH)] if 1 in phases else []
        P.emit(nc, final_wait_keys=fk + ['out%d' % t for t in range(NT)])
        if P.dbg is not None:
            for kk, v in sorted(P.dbg.items(), key=lambda t: -t[1])[:40]:
                print('GAP', kk, round(v / 1e3, 1))
        build_nc.stats = (len(P.ops), P.n_sems, P.n_waits, P.sim_ns, getattr(P, 'seg_end', None), getattr(P, 'seg_busy', None))
    return nc


def _prep_inputs(inputs):
    f = lambda a: np.ascontiguousarray(np.asarray(a, dtype=np.float32))
    shared = {}
    shared["w_in"] = f(inputs["w_in"][0])
    shared["w_proj_attn"] = f(inputs["w_proj_attn"][0])
    shared["w_proj_rnn"] = f(inputs["w_proj_rnn"][0])
    shared["w_out"] = f(inputs["w_out"][0])
    shared["w_ff1"] = f(inputs["w_ff1"][0])
    shared["w_ff2"] = f(inputs["w_ff2"][0])
    nwcat = np.concatenate([f(inputs["norm1_w"][0]), f(inputs["norm2_w"][0])])
    shared["nw_bc"] = np.ascontiguousarray(np.broadcast_to(nwcat[None, :], (128, 2 * D)))
    shared["nw_col"] = np.ascontiguousarray(nwcat.reshape(2, 8, 128).transpose(2, 0, 1).reshape(128, 16))
    shared["qk_col"] = np.ascontiguousarray(np.stack([f(inputs["q_norm_w"][0]), f(inputs["k_norm_w"][0])], axis=1))
    shared["rel_bias"] = f(inputs["rel_bias"])
    vecs = [f(inputs["conv_w"][0][j]) for j in range(4)] + [f(inputs["conv_b"][0]), f(inputs["b_rg_a"][0]),
            f(inputs["b_rg_i"][0]), f(inputs["lru_lambda"][0]), f(inputs["b_gate"][0][0]), f(inputs["b_gate"][0][1])]
    vecs.append(np.zeros(D, np.float32))
    pv = np.stack(vecs, axis=1)
    pv = pv.reshape(8, 128, 11).transpose(1, 0, 2)
    shared["pvec"] = np.ascontiguousarray(pv.reshape(128, 88))
    shared["w_rg_a"] = f(inputs["w_rg_a"][0])
    shared["w_rg_i"] = f(inputs["w_rg_i"][0])
    shared.update(_host_consts())
    return shared


def kernel(**inputs):
    x = np.asarray(inputs["x"], dtype=np.float32)
    shared = _prep_inputs(inputs)
    nc = build_nc()
    in_maps = []
    for i in range(NCORES):
        m = dict(shared)
        m["x"] = np.ascontiguousarray(x[i])
        in_maps.append(m)
    res = run_bass_kernel_spmd(nc, in_maps, core_ids=list(range(NCORES)))
    return np.stack([np.asarray(r["out"], dtype=np.float32) for r in res.results], axis=0)
```
